# Optimizing a Trainium2 kernel written in Bass

```python
import math
import jax
import jax.numpy as jnp
from jax import lax
import numpy as np

D_MODEL = 2048
BATCH = 4
SEQ = 2048
DEPTH = 4

MEM_LEN = 256
EPS = 1e-5
NEG_INF = -1e30
H_A = 8
D_A = 64
W_A = H_A * 2 * D_A
DILATED_GROUPS = ((128, 1), (512, 4), (2048, 16))
N_GROUPS = 3
HB_PER_GROUP = 4
D_B = 128
W_BQKV = N_GROUPS * HB_PER_GROUP * D_B
W_B = HB_PER_GROUP * D_B
H_C = 4
D_C = 128
W_C = H_C * D_C
IN_WIDTHS = (W_A, W_A, W_A, W_BQKV, W_BQKV, W_BQKV, W_C)
N_IN = 3 * W_A + 3 * W_BQKV + W_C
N_BRANCH = 3
NUM_BUCKETS = 32
REL_MAX_DISTANCE = 1024
N_BIAS_COLS = 2 * H_A + N_GROUPS * HB_PER_GROUP
Q_BLOCK = 128
D_FF = -(-8 * D_MODEL // (3 * 256)) * 256

kernel_name = "hybrid_diff_dilated_mem_encoder"


def rmsnorm(t, g):
    tf = t.astype(jnp.float32)
    tf = tf * lax.rsqrt(jnp.mean(tf * tf, axis=-1, keepdims=True) + EPS)
    return (tf * g.astype(jnp.float32)).astype(t.dtype)


def rel_bucket(rel):
    half_b = NUM_BUCKETS // 2
    max_exact = half_b // 2
    n = jnp.abs(rel)
    nf = jnp.maximum(n, 1).astype(jnp.float32)
    large = max_exact + (jnp.log(nf / max_exact) / math.log(REL_MAX_DISTANCE / max_exact)
                         * (half_b - max_exact)).astype(jnp.int32)
    large = jnp.minimum(large, half_b - 1)
    return jnp.where(rel > 0, half_b, 0) + jnp.where(n < max_exact, n, large)


def diff_attention(q, k, v, lam, lam_init, sub_gain, bias_tab):
    b, s = q.shape[0], q.shape[1]
    nq = s // Q_BLOCK
    scale = D_A ** -0.5
    kpos = jnp.arange(s)

    def one_block(args):
        qb, i0 = args
        rel = kpos[None, :] - (i0 + jnp.arange(Q_BLOCK))[:, None]
        bias = bias_tab[rel_bucket(rel)].reshape(Q_BLOCK, s, 2, H_A).transpose(3, 2, 0, 1)
        logits = jnp.einsum("bqhmd,bkhmd->bhmqk", qb, k).astype(jnp.float32) * scale + bias.astype(jnp.float32)
        p = jax.nn.softmax(logits, axis=-1)
        a = (p[:, :, 0] - lam * p[:, :, 1]).astype(v.dtype)
        return jnp.einsum("bhqk,bkhe->bqhe", a, v)

    qs = q.reshape(b, nq, Q_BLOCK, H_A, 2, D_A).transpose(1, 0, 2, 3, 4, 5)
    o = lax.map(one_block, (qs, jnp.arange(nq) * Q_BLOCK))
    o = o.transpose(1, 0, 2, 3, 4).reshape(b, s, H_A, 2 * D_A)
    o = rmsnorm(o, sub_gain) * (1.0 - lam_init)
    return o.reshape(b, s, W_A)


def dilated_group(q, k, v, bias_tab, window, dilation):
    b, s, hg, hd = q.shape
    half = window // (2 * dilation)
    L = s // dilation
    blk = half
    nb = -(-L // blk)
    lp = nb * blk

    def to_residue(t):
        t = t.reshape(b, L, dilation, hg, hd).transpose(0, 3, 2, 1, 4)
        return jnp.pad(t, ((0, 0), (0, 0), (0, 0), (0, lp - L), (0, 0)))

    def band(t):
        t = jnp.pad(t, ((0, 0), (0, 0), (0, 0), (blk, blk), (0, 0))).reshape(b, hg, dilation, nb + 2, blk, hd)
        return jnp.concatenate([t[:, :, :, :-2], t[:, :, :, 1:-1], t[:, :, :, 2:]], axis=4)

    qb = to_residue(q).reshape(b, hg, dilation, nb, blk, hd)
    kb = band(to_residue(k))
    vb = band(to_residue(v))
    kj = jnp.arange(3 * blk)[None, :] - blk
    step = kj - jnp.arange(blk)[:, None]
    bias = bias_tab[rel_bucket(step * dilation)].transpose(2, 0, 1)
    kidx = jnp.arange(nb)[:, None] * blk + kj
    valid = (jnp.abs(step) <= half)[None] & ((kidx >= 0) & (kidx < L))[:, None, :]
    logits = jnp.einsum("bhrnqd,bhrnkd->bhrnqk", qb, kb).astype(jnp.float32) * (hd ** -0.5) \
        + bias[None, :, None, None].astype(jnp.float32)
    logits = jnp.where(valid, logits, NEG_INF)
    lse = jax.nn.logsumexp(logits, axis=-1)
    p = jnp.exp(logits - lse[..., None]).astype(v.dtype)
    o = jnp.einsum("bhrnqk,bhrnkd->bhrnqd", p, vb)
    o = o.reshape(b, hg, dilation, lp, hd)[:, :, :, :L].transpose(0, 3, 2, 1, 4).reshape(b, s, hg, hd)
    lse = lse.reshape(b, hg, dilation, lp)[..., :L].transpose(0, 3, 2, 1).reshape(b, s, hg)
    return o, lse


def dilated_attention(q, k, v, bias_tab):
    b, s = q.shape[0], q.shape[1]
    q, k, v = (t.reshape(b, s, N_GROUPS, HB_PER_GROUP, D_B) for t in (q, k, v))
    outs, lses = [], []
    for g, (window, dilation) in enumerate(DILATED_GROUPS):
        cols = bias_tab[:, g * HB_PER_GROUP:(g + 1) * HB_PER_GROUP]
        o, lse = dilated_group(q[:, :, g], k[:, :, g], v[:, :, g], cols, window, dilation)
        outs.append(o)
        lses.append(lse)
    alpha = jax.nn.softmax(jnp.stack(lses), axis=0)
    o = jnp.sum(alpha[..., None] * jnp.stack(outs).astype(jnp.float32), axis=0)
    return o.astype(q.dtype).reshape(b, s, W_B)


def mem_attention(q, mem_kv):
    b, s = q.shape[0], q.shape[1]
    q = q.reshape(b, s, H_C, D_C)
    kv = mem_kv.reshape(b, mem_kv.shape[1], 2, H_C, D_C)
    logits = jnp.einsum("bshd,bmhd->bhsm", q, kv[:, :, 0]).astype(jnp.float32) * (D_C ** -0.5)
    p = jax.nn.softmax(logits, axis=-1).astype(q.dtype)
    return jnp.einsum("bhsm,bmhd->bshd", p, kv[:, :, 1]).reshape(b, s, W_C)


def setup_inputs(seed: int = 0) -> dict:
    key = jax.random.key(seed)
    ks = jax.random.split(key, 20)
    f32 = jnp.float32

    def nrm(k, shape, scale):
        return jax.random.normal(k, shape, f32) * scale

    def gain(k, shape):
        return 1.0 + 0.05 * jax.random.normal(k, shape, f32)

    return {
        "x": nrm(ks[0], (BATCH, SEQ, D_MODEL), 1.0),
        "mem": nrm(ks[1], (BATCH, MEM_LEN, D_MODEL), 1.0),
        "rel_bias": nrm(ks[2], (NUM_BUCKETS, N_BIAS_COLS), 0.5),
        "mem_norm": gain(ks[3], (D_MODEL,)),
        "attn_norm": gain(ks[4], (DEPTH, D_MODEL)),
        "w_in": nrm(ks[5], (DEPTH, D_MODEL, N_IN), D_MODEL ** -0.5),
        "diff_lambda": nrm(ks[6], (DEPTH, 4, D_A), 0.1),
        "diff_subln": gain(ks[7], (DEPTH, 2 * D_A)),
        "w_mem_kv": nrm(ks[8], (DEPTH, D_MODEL, 2 * W_C), D_MODEL ** -0.5),
        "w_gate": nrm(ks[9], (DEPTH, D_MODEL, N_BRANCH * D_MODEL), D_MODEL ** -0.5),
        "b_gate": nrm(ks[10], (DEPTH, N_BRANCH * D_MODEL), 0.1),
        "w_proj_a": nrm(ks[11], (DEPTH, W_A, D_MODEL), W_A ** -0.5),
        "w_proj_b": nrm(ks[12], (DEPTH, W_B, D_MODEL), W_B ** -0.5),
        "w_proj_c": nrm(ks[13], (DEPTH, W_C, D_MODEL), W_C ** -0.5),
        "w_out": nrm(ks[14], (DEPTH, D_MODEL, D_MODEL), D_MODEL ** -0.5),
        "ffn_norm": gain(ks[15], (DEPTH, D_MODEL)),
        "w_ffn_gate": nrm(ks[16], (DEPTH, D_MODEL, D_FF), D_MODEL ** -0.5),
        "w_ffn_up": nrm(ks[17], (DEPTH, D_MODEL, D_FF), D_MODEL ** -0.5),
        "w_ffn_down": nrm(ks[18], (DEPTH, D_FF, D_MODEL), D_FF ** -0.5),
        "final_norm": gain(ks[19], (D_MODEL,)),
    }


def reference(x, mem, rel_bias, mem_norm, attn_norm, w_in, diff_lambda, diff_subln, w_mem_kv,
              w_gate, b_gate, w_proj_a, w_proj_b, w_proj_c, w_out, ffn_norm, w_ffn_gate,
              w_ffn_up, w_ffn_down, final_norm):
    b, s, _ = x.shape
    split_at = np.cumsum(IN_WIDTHS)[:-1].tolist()
    mem_n = rmsnorm(mem, mem_norm)
    bias_a = rel_bias[:, :2 * H_A]
    bias_b = rel_bias[:, 2 * H_A:]
    for l in range(DEPTH):
        h = rmsnorm(x, attn_norm[l])
        qa, ka, va, qb, kb, vb, qc = jnp.split(h @ w_in[l], split_at, axis=-1)
        dl = diff_lambda[l].astype(jnp.float32)
        lam_init = 0.8 - 0.6 * math.exp(-0.3 * l)
        lam = jnp.exp(jnp.sum(dl[0] * dl[1])) - jnp.exp(jnp.sum(dl[2] * dl[3])) + lam_init
        o_a = diff_attention(qa.reshape(b, s, H_A, 2, D_A), ka.reshape(b, s, H_A, 2, D_A),
                             va.reshape(b, s, H_A, 2 * D_A), lam, lam_init, diff_subln[l], bias_a)
        o_b = dilated_attention(qb, kb, vb, bias_b)
        o_c = mem_attention(qc, mem_n @ w_mem_kv[l])
        gates = jax.nn.sigmoid((h @ w_gate[l] + b_gate[l]).astype(jnp.float32)).astype(h.dtype)
        gates = gates.reshape(b, s, N_BRANCH, D_MODEL)
        merged = (gates[:, :, 0] * (o_a @ w_proj_a[l])
                  + gates[:, :, 1] * (o_b @ w_proj_b[l])
                  + gates[:, :, 2] * (o_c @ w_proj_c[l]))
        x = x + merged @ w_out[l]
        h2 = rmsnorm(x, ffn_norm[l])
        x = x + (jax.nn.silu(h2 @ w_ffn_gate[l]) * (h2 @ w_ffn_up[l])) @ w_ffn_down[l]
    return rmsnorm(x, final_norm)
```

```python
import math
from contextlib import ExitStack
import numpy as np
import ml_dtypes
import concourse.bass as bass
import concourse.mybir as mybir
from concourse.bass_utils import run_bass_kernel_spmd

F32 = mybir.dt.float32
BF16 = mybir.dt.bfloat16
AF = mybir.ActivationFunctionType
ALU = mybir.AluOpType
NPBF = ml_dtypes.bfloat16

D = 2048
T = 1024
SEQ = 2048
DEPTH = 4
DFF = 5632
EPS = 1e-5
KC = 16
GROUPS = ((128, 1), (512, 4), (2048, 16))
NEG = -1e30

ENGS = ("pe", "act", "dve", "pool", "sp")


class Buf:
    __slots__ = ("name", "w", "r")

    def __init__(self, name=""):
        self.name = name
        self.w = None
        self.r = []


class Plan:
    def __init__(self):
        self.ops = {e: [] for e in ENGS}
        self.seen = {e: {} for e in ENGS}
        self.dma_count = {}

    def buf(self, name=""):
        return Buf(name)

    def bufs(self, n, name=""):
        return [Buf(f"{name}{i}") for i in range(n)]

    def _need(self, eng, ev, waits):
        kind, k, v = ev
        if kind == "op":
            if k == eng and eng == "pe":
                return
            key = ("op", k)
        else:
            key = ("dma", k)
        s = self.seen[eng]
        if s.get(key, -1) >= v:
            return
        s[key] = v
        if kind == "op":
            self.ops[k][v]["inc"] = True
        waits.append(ev)

    def add(self, eng, fn, reads=(), writes=(), dma=None, inc=16):
        waits = []
        for b in reads:
            if b.w is not None:
                self._need(eng, b.w, waits)
        for b in writes:
            if b.w is not None:
                self._need(eng, b.w, waits)
            for ev in b.r:
                self._need(eng, ev, waits)
        idx = len(self.ops[eng])
        op = {"fn": fn, "waits": waits, "inc": False, "dma": None}
        if dma is not None:
            c = self.dma_count.get(dma, 0) + inc
            self.dma_count[dma] = c
            op["dma"] = dma
            op["dinc"] = inc
            ev = ("dma", dma, c)
        else:
            ev = ("op", eng, idx)
        self.ops[eng].append(op)
        for b in reads:
            b.r = [e for e in b.r if not (e[0] == ev[0] and e[1] == ev[1])]
            b.r.append(ev)
        for b in writes:
            b.w = ev
            b.r = []
        return ev

    def wait_events(self, eng, evs):
        waits = []
        for ev in evs:
            self._need(eng, ev, waits)
        self.ops[eng].append({"fn": None, "waits": waits, "inc": False, "dma": None})

    def barrier(self):
        evs = []
        for e in ENGS:
            if e == "sp":
                continue
            if self.ops[e]:
                for i in range(len(self.ops[e]) - 1, -1, -1):
                    if self.ops[e][i]["fn"] is not None and self.ops[e][i]["dma"] is None:
                        evs.append(("op", e, i))
                        break
        for k, c in self.dma_count.items():
            evs.append(("dma", k, c))
        for e in ENGS:
            if e != "pool":
                self.wait_events(e, evs)

    def emit(self, nc, es):
        esem = {e: es.enter_context(nc.semaphore(f"s_{e}")) for e in ENGS}
        dsem = {k: es.enter_context(nc.semaphore(f"d_{i}")) for i, k in enumerate(self.dma_count)}
        val = {}
        for e in ENGS:
            c = 0
            v = []
            for op in self.ops[e]:
                if op["inc"]:
                    c += 1
                v.append(c)
            val[e] = v
        self.n_sems = len(esem) + len(dsem)

        def body(e):
            def run(h):
                for op in self.ops[e]:
                    for (kind, k, v) in op["waits"]:
                        if kind == "op":
                            h.wait_ge(esem[k], val[k][v])
                        else:
                            h.wait_ge(dsem[k], v)
                    if op["fn"] is None:
                        continue
                    ins = op["fn"](h)
                    if op["dma"] is not None:
                        if op["dinc"] == 1:
                            ins.then_inc(dsem[op["dma"]])
                        else:
                            ins.then_inc(dsem[op["dma"]], op["dinc"])
                    elif op["inc"]:
                        ins.then_inc(esem[e], 1)
            return run

        with nc.Block() as block:
            block.tensor(body("pe"))
            block.scalar(body("act"))
            block.vector(body("dve"))
            block.gpsimd(body("pool"))
            block.sync(body("sp"))


class Builder:
    def __init__(self):
        self.nc = bass.Bass("TRN2", target_bir_lowering=False)
        self.P = Plan()
        self.es = ExitStack()
        self.n_dma = 0
        nc = self.nc
        self.ps = [self.es.enter_context(nc.psum_tensor(f"ps{i}", [128, 512], F32)) for i in range(8)]
        self.bps = self.P.bufs(8, "ps")
        self.ps_rr = {}
        self.NW = 256
        self.wslots = [self.sb(f"wslot{i}", [128, 16 * 256], BF16) for i in range(3)]
        self.bw = self.P.bufs(3, "w")
        self.w_i = 0
        self.ones_f = self.sb("ones_f", [128, 128], F32)
        self.ones_b = self.sb("ones_b", [128, 128], BF16)
        self.b_const = self.P.buf("const")
        self.P.add("dve", lambda h: h.memset(self.ones_f[:], 1.0), writes=[self.b_const])
        self.P.add("dve", lambda h: h.memset(self.ones_b[:], 1.0), writes=[self.b_const])
        self.ident_b = self.sb("ident_b", [128, 128], BF16)
        self.stg_i = 0
        self.stg = [self.sb(f"stg{i}", [128, T], BF16) for i in range(3)]
        self.bstg = self.P.bufs(3, "stg")
        self.tmpf = [self.sb(f"tmpf{i}", [128, 512], F32) for i in range(4)]
        self.btmpf = self.P.bufs(4, "tmpf")
        self.tmpf_i = 0

    def sb(self, name, shape, dt):
        return self.es.enter_context(self.nc.sbuf_tensor(name, shape, dt))

    def uname(self, name):
        self._uid = getattr(self, "_uid", 0) + 1
        return f"{name}_u{self._uid}"

    def dram_in(self, name, shape, dt=F32):
        return self.nc.dram_tensor(name, list(shape), dt, kind="ExternalInput").ap()

    def dram_out(self, name, shape, dt=F32):
        return self.nc.dram_tensor(name, list(shape), dt, kind="ExternalOutput").ap()

    def dram_tmp(self, name, shape, dt=F32):
        return self.nc.dram_tensor(name, list(shape), dt, kind="Internal").ap()

    def dkey(self, name):
        self.n_dma += 1
        return f"{name}_{self.n_dma}"

    def psum(self, pool, banks):
        i = self.ps_rr.get(pool, 0)
        self.ps_rr[pool] = i + 1
        b = banks[i % len(banks)]
        return self.ps[b], self.bps[b]

    def next_tmpf(self):
        i = self.tmpf_i % 4
        self.tmpf_i += 1
        return self.tmpf[i], self.btmpf[i]

    def next_stg(self):
        i = self.stg_i % 3
        self.stg_i += 1
        return self.stg[i], self.bstg[i], f"stg{i}"

    def wload(self, W, k0, kc, c0, ncol):
        i = self.w_i % 3
        self.w_i += 1
        slot = self.wslots[i]
        view = slot[:, 0:kc * ncol].rearrange("p (c n) -> p c n", n=ncol)
        src = W[k0:k0 + kc * 128, c0:c0 + ncol].rearrange("(c p) n -> p c n", p=128)
        self.P.add("pool", lambda h: h.dma_start(out=view, in_=src), writes=[self.bw[i]], dma=f"w{i}")
        return view, self.bw[i]

    def rmsnorm(self, src, bsrc, dst, bdst, gain, bgain, ntok, nfeat_chunks=KC, dfeat=D):
        P = self.P
        for t0 in range(0, ntok, 512):
            n = min(512, ntok - t0)
            pst, bpst = self.psum("misc", [7])
            for c in range(nfeat_chunks):
                sq, bsq = self.next_tmpf()
                P.add("act", lambda h, sq=sq, c=c, t0=t0, n=n: h.activation(out=sq[:, 0:n], in_=src[:, c, t0:t0 + n], func=AF.Square),
                      reads=[bsrc], writes=[bsq])
                P.add("pe", lambda h, sq=sq, c=c, pst=pst, n=n: h.matmul(pst[:, 0:n], self.ones_f[:], sq[:, 0:n],
                                                                     start=(c == 0), stop=(c == nfeat_chunks - 1)),
                      reads=[bsq, self.b_const], writes=[bpst])
            rs, brs = self.next_tmpf()
            P.add("act", lambda h, rs=rs, pst=pst, n=n: h.activation(out=rs[:, 0:n], in_=pst[:, 0:n], func=AF.Sqrt,
                                                                 bias=self.eps_col[:, 0:1], scale=1.0 / dfeat),
                  reads=[bpst, self.b_const], writes=[brs])
            P.add("dve", lambda h, rs=rs, n=n: h.reciprocal(out=rs[:, 0:n], in_=rs[:, 0:n]), reads=[brs], writes=[brs])
            for c in range(nfeat_chunks):
                P.add("dve", lambda h, rs=rs, c=c, t0=t0, n=n: h.scalar_tensor_tensor(out=dst[:, c, t0:t0 + n], in0=src[:, c, t0:t0 + n],
                                                                          scalar=gain[:, c:c + 1], in1=rs[:, 0:n],
                                                                          op0=ALU.mult, op1=ALU.mult),
                      reads=[bsrc, brs, bgain], writes=[bdst])

    def dense_fm(self, W, k0, kc, c0, ncols, rhs_fn, brhs, tgs, evac, nw=None):
        nw = nw or self.NW
        P = self.P
        for w0 in range(0, ncols, nw):
            wn = min(nw, ncols - w0)
            wt, bwt = self.wload(W, k0, kc, c0 + w0, wn)
            for oc in range(wn // 128):
                for tg in tgs:
                    ps, bps = self.psum("dense", [0, 1, 2, 3])
                    r0 = rhs_fn(0, tg)
                    n = 1
                    for s_ in r0.shape[1:]:
                        n *= s_
                    for k in range(kc):
                        P.add("pe", lambda h, wt=wt, k=k, oc=oc, tg=tg, ps=ps, n=n: h.matmul(
                            ps[:, 0:n], wt[:, k, oc * 128:(oc + 1) * 128], rhs_fn(k, tg), start=(k == 0), stop=(k == kc - 1)),
                            reads=[bwt, brhs], writes=[bps])
                    evac((w0 // 128) + oc, tg, ps, bps)

    def dense_tm(self, W, k0, kc, c0, ncols, lhs_fn, blhs, ntt, evac):
        P = self.P
        nw = self.NW
        for w0 in range(0, ncols, nw):
            wt, bwt = self.wload(W, k0, kc, c0 + w0, nw)
            for tt in range(ntt):
                ps, bps = self.psum("dense", [0, 1, 2, 3])
                nparts = len(lhs_fn(0, tt))
                for part in range(nparts):
                    for k in range(kc):
                        def mm(h, wt=wt, k=k, tt=tt, ps=ps, part=part):
                            ap, r0, nr = lhs_fn(k, tt)[part]
                            return h.matmul(ps[r0:r0 + nr, 0:nw], ap, wt[:, k, :], start=(k == 0), stop=(k == kc - 1))
                        P.add("pe", mm, reads=[bwt, blhs], writes=[bps])
                evac(w0, tt, ps, bps)

    def finish(self, out_events):
        self.P.wait_events("sp", out_events)
        self.P.emit(self.nc, self.es)
        if getattr(self, "_es5", None) is not None:
            self._es5.close()
        self.es.close()
        return self.nc


def class_view(ap2d, dil):
    if dil == 1:
        return ap2d.rearrange("p (r pos) -> p r pos", r=1)
    return ap2d.rearrange("p (pos r) -> p r pos", r=dil)


def phase1(B, xT, bxT, hT, bhT, g_attn, bg, w_in, w_gate, bgate_sb, bbg, outs, part="all"):
    P = B.P
    if part != "gates":
        B.rmsnorm(xT, bxT, hT, bhT, g_attn, bg, T)
    out_evs = []

    def nat_rhs(k, tg):
        return hT[:, k, tg * 512:(tg + 1) * 512]

    def cls_rhs(dil):
        def f(k, tg):
            v = class_view(hT[:, k, :], dil)
            nr = (512 * dil) // T
            if dil == 1:
                return hT[:, k, tg * 512:(tg + 1) * 512]
            return v[:, tg * nr:(tg + 1) * nr, :]
        return f

    cur = {}

    def make_evac(dst, scale=None, sigmoid=False, chunk_off=0, eng_alt=("dve", "act"), dil=1, rev=False):
        cnt = [0]

        def evac(ci, tg, ps, bps):
            c = ci + chunk_off
            if tg == 0:
                cur["s"] = B.next_stg()
            stg, bstg, skey = cur["s"]
            o = stg[:, tg * 512:(tg + 1) * 512]
            psv = ps[:]
            if rev:
                psv = bass.AP(tensor=ps[:].tensor, offset=ps[:, 511:512].offset, ap=[list(ps[:].ap[0]), [-1, 512]])
            if dil > 1:
                npos = 512 // dil
                o = stg[:].rearrange("p (r pos) -> p r pos", r=dil)[:, :, tg * npos:(tg + 1) * npos]
                psv = class_view(ps[:], dil)
            if sigmoid:
                P.add("act", lambda h, o=o, ps=ps, c=c: h.activation(out=o, in_=ps[:], func=AF.Sigmoid,
                                                                    bias=bgate_sb[:, c:c + 1], scale=1.0),
                      reads=[bps, bbg], writes=[bstg])
            else:
                e = eng_alt[cnt[0] % 2]
                cnt[0] += 1
                if e == "act":
                    P.add("act", lambda h, o=o, psv=psv: h.activation(out=o, in_=psv, func=AF.Copy,
                                                                      scale=(scale if scale is not None else 1.0)),
                          reads=[bps], writes=[bstg])
                elif scale is not None:
                    P.add("dve", lambda h, o=o, psv=psv: h.tensor_scalar_mul(out=o, in0=psv, scalar1=scale),
                          reads=[bps], writes=[bstg])
                else:
                    P.add("dve", lambda h, o=o, psv=psv: h.tensor_copy(out=o, in_=psv), reads=[bps], writes=[bstg])
            if tg == 1:
                ev = P.add("sp", lambda h, stg=stg, c=c: h.dma_start(out=dst[c], in_=stg[:]), reads=[bstg],
                           writes=[B.P.buf()], dma=skey)
                out_evs.append(ev)
        return evac

    sB = 128 ** -0.5
    if part == "gates":
        B.dense_fm(w_gate, 0, KC, 0, 3 * D, nat_rhs, bhT, [0, 1], make_evac(outs["gate"], sigmoid=True))
        return out_evs
    B.dense_fm(w_in, 0, KC, 0, 1024, nat_rhs, bhT, [0, 1], make_evac(outs["qa"], scale=0.125, rev=True))
    B.dense_fm(w_in, 0, KC, 1024, 1024, nat_rhs, bhT, [0, 1], make_evac(outs["ka"]))
    for g, (_, dil) in enumerate(GROUPS):
        B.dense_fm(w_in, 0, KC, 3072 + g * 512, 512, nat_rhs, bhT, [0, 1],
                   make_evac(outs["qb"], scale=sB, chunk_off=g * 4, dil=dil))
        B.dense_fm(w_in, 0, KC, 4608 + g * 512, 512, nat_rhs, bhT, [0, 1],
                   make_evac(outs["kb"], chunk_off=g * 4, dil=dil))
    B.dense_fm(w_in, 0, KC, 7680, 512, nat_rhs, bhT, [0, 1], make_evac(outs["qc"], scale=sB))

    def make_evac_tm(dst2d):
        cnt = [0]

        def evac(w0, tt, ps, bps):
            stg, bstg, skey = B.next_stg()
            e = ("dve", "act")[cnt[0] % 2]
            cnt[0] += 1
            if e == "dve":
                P.add("dve", lambda h, stg=stg, ps=ps: h.tensor_copy(out=stg[:, 0:256], in_=ps[:, 0:256]), reads=[bps], writes=[bstg])
            else:
                P.add("act", lambda h, stg=stg, ps=ps: h.activation(out=stg[:, 0:256], in_=ps[:, 0:256], func=AF.Copy),
                      reads=[bps], writes=[bstg])
            ev = P.add("sp", lambda h, stg=stg, tt=tt, w0=w0: h.dma_start(out=dst2d[tt * 128:(tt + 1) * 128, w0:w0 + 256],
                                                                      in_=stg[:, 0:256]),
                       reads=[bstg], writes=[B.P.buf()], dma=skey)
            out_evs.append(ev)
        return evac

    def nat_lhs(k, tt):
        return [(hT[:, k, tt * 128:(tt + 1) * 128], 0, 128)]

    def cls_lhs(dil):
        def f(k, tt):
            if dil == 1:
                return [(hT[:, k, tt * 128:(tt + 1) * 128], 0, 128)]
            v = class_view(hT[:, k, :], dil)
            Lh = T // dil
            if Lh >= 128:
                r = (tt * 128) // Lh
                p0 = (tt * 128) % Lh
                return [(v[:, r, p0:p0 + 128], 0, 128)]
            nr = 128 // Lh
            return [(v[:, tt * nr + i, :], i * Lh, Lh) for i in range(nr)]
        return f

    B.dense_tm(w_in, 0, KC, 2048, 1024, nat_lhs, bhT, 8, make_evac_tm(outs["va"]))
    for g, (_, dil) in enumerate(GROUPS):
        B.dense_tm(w_in, 0, KC, 6144 + g * 512, 512, cls_lhs(dil), bhT, 8, make_evac_tm(outs["vb"][g]))
    if part == "all":
        B.dense_fm(w_gate, 0, KC, 0, 3 * D, nat_rhs, bhT, [0, 1], make_evac(outs["gate"], sigmoid=True))
    return out_evs


def load_small(B, name, dram_ap, shape, dt=F32):
    t = B.sb(name, shape, dt)
    b = B.P.buf(name)
    B.P.add("sp", lambda h: h.dma_start(out=t[:], in_=dram_ap), writes=[b], dma="small")
    B.small_bufs = getattr(B, "small_bufs", []) + [b]
    return t, b


def finalize_small(B):
    tot = B.P.dma_count.get("small", 0)
    for b in getattr(B, "small_bufs", []):
        b.w = ("dma", "small", tot)


W_G = 2944


def phase2(B, I, lam_init):
    P = B.P
    nc = B.nc
    es2 = ExitStack()

    def sb(name, shape, dt):
        return es2.enter_context(nc.sbuf_tensor(B.uname("s2_" + name), shape, dt))

    kc_sb = sb("kc_sb", [128, 4, 256], BF16)
    bkc = P.buf("kc")
    vc_sb = sb("vc_sb", [128, 2, 512], BF16)
    bvc = P.buf("vc")
    esm = ExitStack()
    memn = esm.enter_context(nc.sbuf_tensor(B.uname("s2_memn"), [128, KC, 256], BF16))
    bmemn = P.buf("memn")
    memf = esm.enter_context(nc.sbuf_tensor(B.uname("s2_memf"), [128, KC, 256], F32))
    bmemf = P.buf("memf")
    msrc = I["memT"].rearrange("(c p) t -> p c t", p=128)
    P.add("sp", lambda h: h.dma_start(out=memf[:], in_=msrc), writes=[bmemf], dma="memf")
    B.rmsnorm(memf, bmemf, memn, bmemn, I["gmem_sb"][0], I["gmem_sb"][1], 256)

    def evac_kc(ci, tg, ps, bps):
        P.add("dve", lambda h, ci=ci, ps=ps: h.tensor_copy(out=kc_sb[:, ci, :], in_=ps[:, 0:256]), reads=[bps], writes=[bkc])

    B.dense_fm(I["w_mem_kv"], 0, KC, 0, 512, lambda k, tg: memn[:, k, :], bmemn, [0], evac_kc)

    def evac_vc(w0, tt, ps, bps):
        P.add("dve", lambda h, w0=w0, tt=tt, ps=ps: h.tensor_copy(out=vc_sb[:, tt, w0:w0 + 256], in_=ps[:, 0:256]), reads=[bps], writes=[bvc])

    B.dense_tm(I["w_mem_kv"], 0, KC, 512, 512, lambda k, tt: [(memn[:, k, tt * 128:(tt + 1) * 128], 0, 128)], bmemn, 2, evac_vc)
    P.barrier()
    esm.close()

    oT, boT = B.A32, B.bA32
    kbuf = [sb(f"kbuf{i}", [128, 3072], BF16) for i in range(2)]
    bkbuf = P.bufs(2, "kbuf")
    vbuf = [sb("vbuf0", [128, 32, 128], BF16)]
    bvbuf = P.bufs(2, "vbuf")
    gbuf = [sb(f"gbuf{i}", [128, W_G], BF16) for i in range(2)]
    bgbuf = P.bufs(2, "gbuf")
    tbuf = [sb(f"tbuf{i}", [128, 512], F32) for i in range(2)]
    btbuf = P.bufs(2, "tbuf")
    cnt_t = [0]
    NPT = 4
    pt = [sb(f"pt{i}", [128, 512], BF16) for i in range(NPT)]
    bpt = P.bufs(NPT, "pt")
    t01 = [sb(f"t01_{i}", [128, T], F32) for i in range(2)]
    bt01 = P.bufs(2, "t01")
    obuf = sb("obuf", [128, T], F32)
    bobuf = P.buf("obuf")
    cnt = {"k": 0, "q": 0, "g": 0, "p": 0}
    qpad = [[sb(f"qpad{i}_{m}", [128, T], BF16) for m in range(2)] for i in range(2)]
    bqpad = [P.buf(f"qpad{i}") for i in range(2)]
    for i in range(2):
        P.add("dve", lambda h, i=i: h.memset(qpad[i][0][64:128, :], 0.0), writes=[bqpad[i]])
        P.add("dve", lambda h, i=i: h.memset(qpad[i][1][0:64, :], 0.0), writes=[bqpad[i]])
    qbuf = [qpad[0][0], qpad[1][0]]
    bqbuf = bqpad
    dacc = [sb(f"dacc{i}", [128, 512], F32) for i in range(2)]
    bdacc = P.bufs(2, "dacc")
    deferred = []

    def run_deferred(flush=False):
        for d in list(deferred):
            d[0] -= 1
            if flush or d[0] <= 0:
                deferred.remove(d)
                d[1]()
    cnt_d = [0]

    def nxt(name, n):
        i = cnt[name] % n
        cnt[name] += 1
        return i

    dl = sb("dl", [128, 256], F32)
    bdl = P.buf("dl")
    dl_src = bass.AP(tensor=I["dl"].tensor, offset=I["dl"].offset, ap=[[0, 128], [1, 256]])
    P.add("sp", lambda h: h.dma_start(out=dl[:], in_=dl_src), writes=[bdl], dma="dl")
    lam = sb("lam", [128, 8], F32)
    blam = P.buf("lam")
    prod = sb("prod", [128, 128], F32)
    bprod = P.buf("prod")
    dv = dl[:].rearrange("p (a d) -> p a d", d=64)
    P.add("dve", lambda h: h.tensor_tensor(out=prod[:].rearrange("p (a d) -> p a d", d=64), in0=dv[:, 0:4:2, :], in1=dv[:, 1:4:2, :],
                                           op=ALU.mult), reads=[bdl], writes=[bprod])
    P.add("dve", lambda h: h.reduce_sum(out=lam[:, 0:2], in_=prod[:].rearrange("p (a d) -> p a d", d=64), axis=mybir.AxisListType.X),
          reads=[bprod], writes=[blam])
    P.add("act", lambda h: h.activation(out=lam[:, 0:2], in_=lam[:, 0:2], func=AF.Exp), reads=[blam], writes=[blam])
    P.add("dve", lambda h: h.tensor_tensor(out=lam[:, 2:3], in0=lam[:, 1:2], in1=lam[:, 0:1], op=ALU.subtract), reads=[blam], writes=[blam])
    if lam_init is None:
        lc, blc = I["lconst_sb"]
        P.add("dve", lambda h: h.tensor_tensor(out=lam[:, 2:3], in0=lam[:, 2:3], in1=lc[:, 0:1], op=ALU.subtract),
              reads=[blam, blc], writes=[blam])
        P.add("dve", lambda h: h.tensor_tensor(out=lam[:, 3:4], in0=I["subln_sb"][0][:, 0:1], in1=lc[:, 1:2], op=ALU.mult),
              reads=[blam, blc, I["subln_sb"][1]], writes=[blam])
    else:
        P.add("dve", lambda h: h.tensor_scalar_add(out=lam[:, 2:3], in0=lam[:, 2:3], scalar1=-lam_init), reads=[blam], writes=[blam])
        P.add("dve", lambda h: h.tensor_scalar_mul(out=lam[:, 3:4], in0=I["subln_sb"][0][:, 0:1], scalar1=1.0 - lam_init),
              reads=[blam, I["subln_sb"][1]], writes=[blam])

    def pipeline(items, LA=3):
        n = len(items)
        for st_ in range(n + LA):
            if st_ < n:
                items[st_][0]()
            if st_ - LA >= 0:
                items[st_ - LA][1]()

    ST_BANKS = [4, 5, 6, 7]

    for hh in range(8):
        ki = nxt("k", 2)
        qi = nxt("q", 2)
        kt, bkt = kbuf[ki], bkbuf[ki]
        qp, bqt = qpad[qi], bqpad[qi]
        hf = hh % 2
        vt, bvt = vbuf[0][:, 16 * hf:16 * hf + 16, :], bvbuf[hf]
        for r in range(2):
            P.add("sp", lambda h, kt=kt, r=r, hh=hh: h.dma_start(out=kt[:, r * T:(r + 1) * T], in_=I["ka_all"][r, hh]),
                  writes=[bkt], dma=f"kb{ki}")
            vsrc = I["va_all"][r, :, hh * 128:(hh + 1) * 128].rearrange("(t p) e -> p t e", p=128)
            P.add("sp", lambda h, vt=vt, r=r, vsrc=vsrc: h.dma_start(out=vt[:, r * 8:(r + 1) * 8, :], in_=vsrc),
                  writes=[bvt], dma=f"vb{hf}")
        for m in range(2):
            P.add("sp", lambda h, qp=qp, hh=hh, m=m: h.dma_start(out=qp[m][m * 64:(m + 1) * 64, :], in_=I["qa"][hh][m * 64:(m + 1) * 64, :]),
                  writes=[bqt], dma=f"qp{qi}")
        items = []
        for m in range(2):
            gi = nxt("g", 2)
            gt, bgt = gbuf[gi], bgbuf[gi]
            hm = m * 8 + hh
            gsrc = bass.AP(tensor=I["mrow"].tensor, offset=I["mrow"][hm, 0:1].offset, ap=[[1, 128], [1, W_G]])
            P.add("pool", lambda h, gt=gt, gsrc=gsrc: h.dma_start(out=gt[:], in_=gsrc), writes=[bgt], dma=f"gb{gi}")
            for qg in range(2):
                grp = {}
                for k2 in range(0, 16, 2):
                    it = {}

                    def sA(m=m, qg=qg, k2=k2, it=it, gt=gt, bgt=bgt, kt=kt, qp=qp, bkt=bkt, bqt=bqt):
                        sts = [B.psum("st", ST_BANKS) for _ in range(2)]
                        for u in range(2):
                            kk = k2 + u
                            st, bst = sts[u]
                            P.add("pe", lambda h, st=st, kk=kk: h.matmul(st[:], kt[:, kk * 128:(kk + 1) * 128], qp[m][:, qg * 512:(qg + 1) * 512],
                                                                         start=True, stop=False), reads=[bkt, bqt], writes=[bst])
                        for u in range(2):
                            kk = k2 + u
                            st, bst = sts[u]
                            c0 = kk * 128 - 512 * qg + 512
                            P.add("pe", lambda h, st=st, c0=c0: h.matmul(st[:], B.ident_b[:], gt[:, c0:c0 + 512], start=False, stop=True),
                                  reads=[bgt, B.b_const], writes=[bst])
                        it["pi"] = []
                        for u in range(2):
                            st, bst = sts[u]
                            pi = nxt("p", NPT)
                            it["pi"].append(pi)
                            P.add("act", lambda h, st=st, pi=pi: h.activation(out=pt[pi][:], in_=st[:], func=AF.Exp), reads=[bst], writes=[bpt[pi]])

                    def sB(m=m, qg=qg, k2=k2, it=it, grp=grp, vt=vt, bvt=bvt):
                        if k2 == 0:
                            grp["num"] = B.psum("num", [0, 1])
                            grp["den"] = B.psum("den", [2, 3])
                            di = cnt_d[0] % 2
                            cnt_d[0] += 1
                            grp["da"] = (dacc[di], bdacc[di])
                        num, bnum = grp["num"]
                        den, bden = grp["den"]
                        da, bda = grp["da"]
                        for u in range(2):
                            kk = k2 + u
                            pi = it["pi"][u]
                            P.add("pe", lambda h, num=num, pi=pi, kk=kk: h.matmul(num[:], vt[:, kk, :], pt[pi][:], start=(kk == 0), stop=(kk == 15)),
                                  reads=[bvt, bpt[pi]], writes=[bnum])
                        pi1 = it["pi"][1]
                        P.add("pe", lambda h, den=den, pi1=pi1: h.matmul(den[:], B.ones_b[:], pt[pi1][:], start=(k2 == 0), stop=False),
                              reads=[B.b_const, bpt[pi1]], writes=[bden])
                        pi0 = it["pi"][0]
                        if k2 == 0:
                            P.add("dve", lambda h, da=da, pi0=pi0: h.tensor_copy(out=da[:], in_=pt[pi0][:]), reads=[bpt[pi0]], writes=[bda])
                        else:
                            P.add("dve", lambda h, da=da, pi0=pi0: h.tensor_tensor(out=da[:], in0=pt[pi0][:], in1=da[:], op=ALU.add),
                                  reads=[bpt[pi0], bda], writes=[bda])
                        run_deferred()
                        if k2 == 14:
                            def epi(num=num, bnum=bnum, den=den, bden=bden, da=da, bda=bda, m=m, qg=qg):
                                P.add("pe", lambda h, den=den, da=da: h.matmul(den[:], B.ones_f[:], da[:], start=False, stop=True),
                                      reads=[B.b_const, bda], writes=[bden])
                                rc, brc = B.next_tmpf()
                                P.add("dve", lambda h, rc=rc, den=den: h.reciprocal(out=rc[:], in_=den[:]), reads=[bden], writes=[brc])
                                numr = bass.AP(tensor=num[:].tensor, offset=num[:, 511:512].offset, ap=[list(num[:].ap[0]), [-1, 512]])
                                rcr = bass.AP(tensor=rc[:].tensor, offset=rc[:, 511:512].offset, ap=[list(rc[:].ap[0]), [-1, 512]])
                                P.add("dve", lambda h, rcr=rcr, numr=numr: h.tensor_tensor(out=t01[m][:, qg * 512:(qg + 1) * 512], in0=numr, in1=rcr,
                                                                                         op=ALU.mult), reads=[bnum, brc], writes=[bt01[m]])
                            deferred.append([3, epi])
                    items.append((sA, sB))
        pipeline(items, LA=1)

        def head_epi(hh=hh):
            P.add("dve", lambda h: h.scalar_tensor_tensor(out=obuf[:], in0=t01[1][:], scalar=lam[:, 2:3], in1=t01[0][:], op0=ALU.mult, op1=ALU.add),
                  reads=[bt01[0], bt01[1], blam], writes=[bobuf])
            for qg in range(2):
                sq, bsq = B.next_tmpf()
                P.add("act", lambda h, sq=sq, qg=qg: h.activation(out=sq[:], in_=obuf[:, qg * 512:(qg + 1) * 512], func=AF.Square),
                      reads=[bobuf], writes=[bsq])
                pss, bpss = B.psum("st", ST_BANKS)
                P.add("pe", lambda h, pss=pss, sq=sq: h.matmul(pss[:], B.ones_f[:], sq[:], start=True, stop=True), reads=[bsq, B.b_const], writes=[bpss])
                rs, brs = B.next_tmpf()
                P.add("act", lambda h, rs=rs, pss=pss: h.activation(out=rs[:], in_=pss[:], func=AF.Ln, bias=B.eps_col[:, 0:1], scale=1.0 / 128),
                      reads=[bpss, B.b_const], writes=[brs])
                P.add("act", lambda h, rs=rs: h.activation(out=rs[:], in_=rs[:], func=AF.Exp, scale=-0.5), reads=[brs], writes=[brs])
                P.add("dve", lambda h, rs=rs, qg=qg, hh=hh: h.scalar_tensor_tensor(out=oT[:, hh, qg * 512:(qg + 1) * 512], in0=obuf[:, qg * 512:(qg + 1) * 512],
                                                                             scalar=lam[:, 3:4], in1=rs[:], op0=ALU.mult, op1=ALU.mult),
                      reads=[bobuf, brs, blam], writes=[boT])
        deferred.append([5, head_epi])
    run_deferred(flush=True)

    accN, baccN = t01[0], bt01[0]
    accD, baccD = t01[1], bt01[1]
    for j in range(4):
        for g, (_, dil) in enumerate(GROUPS):
            Lh = T // dil
            QT = min(128, Lh)
            ntq = Lh // QT
            LP = Lh + 128
            ntile = (LP + 127) // 128
            ki = nxt("k", 2)
            qi = nxt("q", 2)
            gi = cnt_t[0] % 2
            cnt_t[0] += 1
            kt, bkt = kbuf[ki], bkbuf[ki]
            qt, bqt = qbuf[qi], bqbuf[qi]
            vt, bvt = vbuf[0], bvbuf[0]
            bvt2 = bvbuf[1]
            tb, btb = tbuf[gi], btbuf[gi]
            P.add("sp", lambda h, kt=kt, g=g, j=j, dil=dil, LP=LP: h.dma_start(out=kt[:, 0:dil * LP], in_=I[f"kbp{g}"][j]), writes=[bkt], dma=f"kb{ki}")
            vsrc = I[f"vbp{g}"][:, :, j * 128:(j + 1) * 128].rearrange("r (t p) e -> p (r t) e", p=128)
            P.add("sp", lambda h, vt=vt, vsrc=vsrc, dil=dil, ntile=ntile: h.dma_start(out=vt[:, 0:dil * ntile, :], in_=vsrc), writes=[bvt, bvt2], dma="vb0")
            P.add("sp", lambda h, qt=qt, g=g, j=j: h.dma_start(out=qt[:], in_=I["qb"][g * 4 + j]), writes=[bqt], dma=f"qb{qi}")
            P.add("sp", lambda h, tb=tb, g=g, j=j: h.dma_start(out=tb[:, 0:512], in_=I["btab"][g * 4 + j]), writes=[btb], dma=f"tb{gi}")
            tbv = tb[:, 0:512].rearrange("p (a q) -> p a q", a=4)
            nat = class_view(accN[:], dil)
            natD = class_view(accD[:], dil)
            items = []
            for r in range(dil):
                for i in range(ntq):
                    grp = {}
                    q_ap = qt[:, r * Lh + i * QT: r * Lh + i * QT + QT]
                    for ch in range(2):
                        nk = 128 if ch == 0 else QT
                        k0 = r * LP + i * QT + ch * 128
                        tsel = (2 if i == 0 else 0) if ch == 0 else (3 if i == ntq - 1 else 1)
                        vti = r * ntile + i * (QT // 128 if QT >= 128 else 0) + ch
                        it = {}

                        def sA(it=it, nk=nk, k0=k0, tsel=tsel, q_ap=q_ap, kt=kt, bkt=bkt, bqt=bqt, tbv=tbv, btb=btb, QT=QT):
                            st, bst = B.psum("st", ST_BANKS)
                            P.add("pe", lambda h, st=st: h.matmul(st[0:nk, 0:QT], kt[:, k0:k0 + nk], q_ap, start=True, stop=True),
                                  reads=[bkt, bqt], writes=[bst])
                            lg, blg = B.next_tmpf()
                            P.add("dve", lambda h, lg=lg, st=st: h.tensor_tensor(out=lg[0:nk, 0:QT], in0=st[0:nk, 0:QT],
                                                                              in1=tbv[0:nk, tsel, 0:QT], op=ALU.add),
                                  reads=[bst, btb], writes=[blg])
                            pi = nxt("p", NPT)
                            it["pi"] = pi
                            P.add("act", lambda h, lg=lg, pi=pi: h.activation(out=pt[pi][0:nk, 0:QT], in_=lg[0:nk, 0:QT], func=AF.Exp),
                                  reads=[blg], writes=[bpt[pi]])

                        def sB(it=it, nk=nk, ch=ch, vti=vti, grp=grp, vt=vt, bvt=bvt, bvt2=bvt2, QT=QT, r=r, i=i, g=g, nat=nat, natD=natD):
                            if ch == 0:
                                grp["num"] = B.psum("num", [0, 1])
                                grp["den"] = B.psum("den", [2, 3])
                            num, bnum = grp["num"]
                            den, bden = grp["den"]
                            pi = it["pi"]
                            P.add("pe", lambda h, num=num, pi=pi: h.matmul(num[:, 0:QT], vt[0:nk, vti, :], pt[pi][0:nk, 0:QT],
                                                                          start=(ch == 0), stop=(ch == 1)),
                                  reads=[bvt, bvt2, bpt[pi]], writes=[bnum])
                            P.add("pe", lambda h, den=den, pi=pi: h.matmul(den[:, 0:QT], B.ones_b[0:nk, :], pt[pi][0:nk, 0:QT],
                                                                          start=(ch == 0), stop=(ch == 1)),
                                  reads=[B.b_const, bpt[pi]], writes=[bden])
                            if ch == 1:
                                dN = nat[:, r, i * QT:(i + 1) * QT]
                                dD = natD[:, r, i * QT:(i + 1) * QT]
                                if g == 0:
                                    P.add("dve", lambda h, num=num: h.tensor_copy(out=dN, in_=num[:, 0:QT]), reads=[bnum], writes=[baccN])
                                    P.add("act", lambda h, den=den: h.activation(out=dD, in_=den[:, 0:QT], func=AF.Copy), reads=[bden], writes=[baccD])
                                else:
                                    P.add("dve", lambda h, num=num: h.tensor_tensor(out=dN, in0=num[:, 0:QT], in1=dN, op=ALU.add), reads=[bnum, baccN], writes=[baccN])
                                    P.add("dve", lambda h, den=den: h.tensor_tensor(out=dD, in0=den[:, 0:QT], in1=dD, op=ALU.add), reads=[bden, baccD], writes=[baccD])
                        items.append((sA, sB))
            pipeline(items)
        P.add("dve", lambda h: h.reciprocal(out=accD[:], in_=accD[:]), reads=[baccD], writes=[baccD])
        P.add("dve", lambda h, j=j: h.tensor_tensor(out=oT[:, 8 + j, :], in0=accN[:], in1=accD[:], op=ALU.mult), reads=[baccN, baccD], writes=[boT])

    for j in range(4):
        qi = nxt("q", 2)
        qt, bqt = qbuf[qi], bqbuf[qi]
        P.add("sp", lambda h, qt=qt, j=j: h.dma_start(out=qt[:], in_=I["qc"][j]), writes=[bqt], dma=f"qb{qi}")
        items = []
        for qg in range(2):
            grp = {}
            for mt in range(2):
                it = {}

                def sA(it=it, qg=qg, mt=mt, qt=qt, bqt=bqt, j=j):
                    st, bst = B.psum("st", ST_BANKS)
                    P.add("pe", lambda h, st=st: h.matmul(st[:], kc_sb[:, j, mt * 128:(mt + 1) * 128], qt[:, qg * 512:(qg + 1) * 512],
                                                          start=True, stop=True), reads=[bkc, bqt], writes=[bst])
                    pi = nxt("p", NPT)
                    it["pi"] = pi
                    P.add("act", lambda h, st=st, pi=pi: h.activation(out=pt[pi][:], in_=st[:], func=AF.Exp), reads=[bst], writes=[bpt[pi]])

                def sB(it=it, qg=qg, mt=mt, grp=grp, j=j):
                    if mt == 0:
                        grp["num"] = B.psum("num", [0, 1])
                        grp["den"] = B.psum("den", [2, 3])
                    num, bnum = grp["num"]
                    den, bden = grp["den"]
                    pi = it["pi"]
                    P.add("pe", lambda h, num=num, pi=pi: h.matmul(num[:], vc_sb[:, mt, j * 128:(j + 1) * 128], pt[pi][:], start=(mt == 0), stop=(mt == 1)),
                          reads=[bvc, bpt[pi]], writes=[bnum])
                    P.add("pe", lambda h, den=den, pi=pi: h.matmul(den[:], B.ones_b[:], pt[pi][:], start=(mt == 0), stop=(mt == 1)),
                          reads=[B.b_const, bpt[pi]], writes=[bden])
                    if mt == 1:
                        rc, brc = B.next_tmpf()
                        P.add("dve", lambda h, rc=rc, den=den: h.reciprocal(out=rc[:], in_=den[:]), reads=[bden], writes=[brc])
                        P.add("dve", lambda h, rc=rc, num=num: h.tensor_tensor(out=oT[:, 12 + j, qg * 512:(qg + 1) * 512], in0=num[:], in1=rc[:], op=ALU.mult),
                              reads=[bnum, brc], writes=[boT])
                items.append((sA, sB))
        pipeline(items)
    P.barrier()
    es2.close()


def phase3(B, I):
    P = B.P
    nc = B.nc
    es3 = ExitStack()
    oT, boT = B.A32, B.bA32
    merged = es3.enter_context(nc.sbuf_tensor(B.uname("merged"), [128, KC, T], BF16))
    bmerged = P.buf("merged")
    gts = [es3.enter_context(nc.sbuf_tensor(B.uname(f"gts{i}"), [128, 3, T], BF16)) for i in range(2)]
    bgts = P.bufs(2, "gts")
    acc = [es3.enter_context(nc.sbuf_tensor(B.uname(f"macc{i}"), [128, T], F32)) for i in range(2)]
    bacc = P.bufs(2, "macc")
    gview = I["gate"].rearrange("(b c) p t -> c p b t", b=3)
    projs = ((I["w_proj_a"], 8, 0), (I["w_proj_b"], 4, 8), (I["w_proj_c"], 4, 12))
    for op2 in range(0, KC, 2):
        for bi, (W, kcb, c0) in enumerate(projs):
            def evac(ci, tg, ps, bps, bi=bi, op2=op2):
                oc = op2 + ci
                gi = oc % 2
                if bi == 0 and tg == 0:
                    P.add("sp", lambda h, gi=gi, oc=oc: h.dma_start(out=gts[gi][:], in_=gview[oc]), writes=[bgts[gi]], dma=f"gts{gi}")
                sl = slice(tg * 512, (tg + 1) * 512)
                if bi == 0:
                    P.add("dve", lambda h, ps=ps, gi=gi, sl=sl: h.tensor_tensor(out=acc[gi][:, sl], in0=ps[:], in1=gts[gi][:, 0, sl], op=ALU.mult),
                          reads=[bps, bgts[gi]], writes=[bacc[gi]])
                else:
                    tmp, btmp = B.next_tmpf()
                    P.add("dve", lambda h, ps=ps, gi=gi, sl=sl, tmp=tmp, bi=bi: h.tensor_tensor(out=tmp[:], in0=ps[:], in1=gts[gi][:, bi, sl], op=ALU.mult),
                          reads=[bps, bgts[gi]], writes=[btmp])
                    if bi == 1:
                        P.add("dve", lambda h, gi=gi, sl=sl, tmp=tmp: h.tensor_tensor(out=acc[gi][:, sl], in0=acc[gi][:, sl], in1=tmp[:], op=ALU.add),
                              reads=[btmp, bacc[gi]], writes=[bacc[gi]])
                    else:
                        P.add("dve", lambda h, gi=gi, sl=sl, tmp=tmp, oc=oc: h.tensor_tensor(out=merged[:, oc, sl], in0=acc[gi][:, sl], in1=tmp[:], op=ALU.add),
                              reads=[btmp, bacc[gi]], writes=[bmerged])
            B.dense_fm(W, 0, kcb, op2 * 128, 256, lambda k, tg, c0=c0: oT[:, c0 + k, tg * 512:(tg + 1) * 512], boT, [0, 1], evac)

    def evac_out(ci, tg, ps, bps):
        sl = slice(tg * 512, (tg + 1) * 512)
        P.add("dve", lambda h, ci=ci, sl=sl, ps=ps: h.tensor_tensor(out=B.xT[:, ci, sl], in0=ps[:], in1=B.xT[:, ci, sl], op=ALU.add),
              reads=[bps, B.bxT], writes=[B.bxT])

    B.dense_fm(I["w_out"], 0, KC, 0, D, lambda k, tg: merged[:, k, tg * 512:(tg + 1) * 512], bmerged, [0, 1], evac_out)
    P.barrier()
    es3.close()


def phase4(B, I):
    P = B.P
    nc = B.nc
    es4 = ExitStack()
    h2, bh2 = B.A32, B.bA32
    B.rmsnorm(B.xT, B.bxT, h2, bh2, I["gffn_sb"][0], I["gffn_sb"][1], T)
    NH = 22
    act = es4.enter_context(nc.sbuf_tensor(B.uname("actT"), [128, NH, T], BF16))
    bact = P.buf("act")
    sg = [es4.enter_context(nc.sbuf_tensor(B.uname(f"sg{i}"), [128, 2, T], BF16)) for i in range(2)]
    bsg = P.bufs(2, "sg")

    def rhs(k, tg):
        return h2[:, k, tg * 512:(tg + 1) * 512]

    for half in range(2):
        for w0 in range(0, NH * 128, 256):
            f0 = half * NH * 128 + w0
            si = (w0 // 256) % 2

            def evac_g(ci, tg, ps, bps, si=si):
                P.add("act", lambda h, ps=ps, ci=ci, tg=tg, si=si: h.activation(out=sg[si][:, ci, tg * 512:(tg + 1) * 512], in_=ps[:], func=AF.Silu),
                      reads=[bps], writes=[bsg[si]])

            def evac_u(ci, tg, ps, bps, si=si, w0=w0):
                fc = w0 // 128 + ci
                P.add("dve", lambda h, ps=ps, ci=ci, tg=tg, si=si, fc=fc: h.tensor_tensor(out=act[:, fc, tg * 512:(tg + 1) * 512], in0=ps[:],
                                                                                 in1=sg[si][:, ci, tg * 512:(tg + 1) * 512], op=ALU.mult),
                      reads=[bps, bsg[si]], writes=[bact])
            ncol = min(256, NH * 128 - w0)
            B.dense_fm(I["w_ffn_gate"], 0, KC, f0, ncol, rhs, bh2, [0, 1], evac_g)
            B.dense_fm(I["w_ffn_up"], 0, KC, f0, ncol, rhs, bh2, [0, 1], evac_u)

        def evac_d(ci, tg, ps, bps):
            sl = slice(tg * 512, (tg + 1) * 512)
            P.add("dve", lambda h, ci=ci, sl=sl, ps=ps: h.tensor_tensor(out=B.xT[:, ci, sl], in0=ps[:], in1=B.xT[:, ci, sl], op=ALU.add),
                  reads=[bps, B.bxT], writes=[B.bxT])
        B.dense_fm(I["w_ffn_down"], half * NH * 128, NH, 0, D, lambda k, tg: act[:, k, tg * 512:(tg + 1) * 512], bact, [0, 1], evac_d, nw=128)
    P.barrier()
    es4.close()


def store_x(B, out_d, final_gain=None):
    P = B.P
    evs = []
    dst = out_d.rearrange("(c p) t -> p c t", p=128)
    if final_gain is None:
        for c in range(0, KC, 4):
            evs.append(P.add("sp", lambda h, c=c: h.dma_start(out=dst[:, c:c + 4, :], in_=B.xT[:, c:c + 4, :]), reads=[B.bxT], writes=[P.buf()], dma="xo"))
        return evs
    es5 = ExitStack()
    fo = es5.enter_context(B.nc.sbuf_tensor(B.uname("final_o"), [128, KC, T], F32))
    bfo = P.buf("fo")
    B.rmsnorm(B.xT, B.bxT, fo, bfo, final_gain[0], final_gain[1], T)
    for c in range(0, KC, 4):
        evs.append(P.add("sp", lambda h, c=c: h.dma_start(out=dst[:, c:c + 4, :], in_=fo[:, c:c + 4, :]), reads=[bfo], writes=[P.buf()], dma="xo"))
    B._es5 = es5
    return evs


def load_x(B, src2d):
    src = src2d.rearrange("(c p) t -> p c t", p=128)
    for c in range(0, KC, 4):
        B.P.add("sp", lambda h, c=c: h.dma_start(out=B.xT[:, c:c + 4, :], in_=src[:, c:c + 4, :]), writes=[B.bxT], dma="x")


KV_PARTS = (("ka", 1024), ("va", 1024), ("kb1", 1024), ("kb2", 512), ("vb1", 1024), ("vb2", 512))


class _Idx:
    def __init__(self, fn):
        self.fn = fn

    def __getitem__(self, idx):
        if isinstance(idx, tuple):
            v = self.fn(idx[0])
            rest = idx[1:]
            return v[rest] if len(rest) > 1 else v[rest[0]]
        return self.fn(idx)


def kv_views(parts):
    vb1 = parts["vb1"].rearrange("(g t2) (two e) -> g (t2 two) e", g=2, two=2)
    vb2 = parts["vb2"].rearrange("t2 (two e) -> (t2 two) e", two=2)
    kb1 = parts["kb1"].rearrange("(h p) t -> h p t", p=128)
    kb2 = parts["kb2"].rearrange("(h p) t -> h p t", p=128)
    return {
        "ka": parts["ka"].rearrange("(h p) t -> h p t", p=128),
        "va": parts["va"],
        "kb": _Idx(lambda c: kb1[c] if c < 8 else kb2[c - 8]),
        "vb": _Idx(lambda g: vb1[g] if g < 2 else vb2),
    }


def build_fused8():
    B = Builder()
    P = B.P
    I0 = {}
    for n, shp in (("xT_in", [D, T]), ("memT", [D, 256]), ("w_in", [DEPTH, D, 8192]), ("w_gate", [DEPTH, D, 3 * D]),
                   ("w_mem_kv", [DEPTH, D, 1024]), ("w_proj_a", [DEPTH, 1024, D]), ("w_proj_b", [DEPTH, 512, D]),
                   ("w_proj_c", [DEPTH, 512, D]), ("w_out", [DEPTH, D, D]), ("w_ffn_gate", [DEPTH, D, DFF]),
                   ("w_ffn_up", [DEPTH, D, DFF]), ("w_ffn_down", [DEPTH, DFF, D]), ("mrow", [16, 3072]),
                   ("btab", [12, 128, 512]), ("dl", [DEPTH, 1, 256])):
        I0[n] = B.dram_in(n, shp)
    small = {}
    for n, shp in (("g_attn", [128, DEPTH * KC]), ("b_gate", [128, DEPTH * 48]), ("gffn", [128, DEPTH * KC]),
                   ("subln", [128, DEPTH]), ("gmem", [128, KC]), ("gfinal", [128, KC])):
        small[n] = B.dram_in(n, shp)
    ident_d = B.dram_in("ident", [128, 128])
    out_d = B.dram_out("out", [D, T])
    kv_own = [{n: B.nc.dram_tensor(f"kvo_{n}{i}", [r, 1024], BF16) for n, r in KV_PARTS} for i in range(2)]
    kv_all = [{n: B.nc.dram_tensor(f"kva_{n}{i}", [2 * r, 1024], BF16) for n, r in KV_PARTS} for i in range(2)]
    Sq = {"qa": B.dram_tmp("s_qa", [8, 128, T], BF16), "qb": B.dram_tmp("s_qb", [12, 128, T], BF16),
          "qc": B.dram_tmp("s_qc", [4, 128, T], BF16), "gate": B.dram_tmp("s_gate", [48, 128, T], BF16)}
    kbp, vbp = [], []
    for g, (_, dil) in enumerate(GROUPS):
        Lh = T // dil
        nrow = ((Lh + 128 + 127) // 128) * 128
        kbp.append(B.dram_tmp(f"s_kbp{g}", [4, 128, dil * (Lh + 128)], BF16))
        vbp.append(B.dram_tmp(f"s_vbp{g}", [dil, nrow, 512], BF16))

    B.xT = B.sb("xT_sb", [128, KC, T], F32)
    B.bxT = P.buf("xT")
    B.A32 = B.sb("A32", [128, KC, T], BF16)
    B.bA32 = P.buf("A32")
    B.eps_col = B.sb("eps_col", [128, 1], F32)
    P.add("dve", lambda h: h.memset(B.eps_col[:], EPS), writes=[B.b_const])
    sm = {n: load_small(B, n + "_sb", small[n], list(small[n].shape)) for n in small}
    identf = load_small(B, "ident_f", ident_d, [128, 128])
    finalize_small(B)
    P.add("dve", lambda h: h.tensor_copy(out=B.ident_b[:], in_=identf[0][:]), reads=[identf[1]], writes=[B.b_const])
    load_x(B, I0["xT_in"])

    for l in range(DEPTH):
        lam_init = 0.8 - 0.6 * math.exp(-0.3 * l)
        own = kv_views({n: kv_own[l % 2][n].ap() for n, _ in KV_PARTS})
        allv = [kv_views({n: kv_all[l % 2][n].ap()[r * rows:(r + 1) * rows, :] for n, rows in KV_PARTS}) for r in range(2)]
        outs = dict(Sq)
        outs.update(own)
        p1args = (B, B.xT, B.bxT, B.A32, B.bA32, sm["g_attn"][0][:, l * KC:(l + 1) * KC], sm["g_attn"][1],
                  I0["w_in"][l], I0["w_gate"][l], sm["b_gate"][0][:, l * 48:(l + 1) * 48], sm["b_gate"][1], outs)
        phase1(*p1args, part="main")
        P.wait_events("pool", [("dma", k, c) for k, c in P.dma_count.items()])
        for n, _ in KV_PARTS:
            P.add("pool", lambda h, l=l, n=n: h.collective_compute("AllGather", ALU.bypass, replica_groups=[[0, 1], [2, 3], [4, 5], [6, 7]],
                                                                   ins=[kv_own[l % 2][n].ap().opt()], outs=[kv_all[l % 2][n].ap().opt()]),
                  dma="cc", inc=1)
        phase1(*p1args, part="gates")
        P.barrier()
        for g, (_, dil) in enumerate(GROUPS):
            Lh = T // dil
            for j in range(4):
                dst = kbp[g][j].rearrange("p (r q) -> p r q", r=dil)

                def src(vw, g=g, j=j, dil=dil):
                    return vw["kb"][g * 4 + j].rearrange("p (r q) -> p r q", r=dil)
                P.add("sp", lambda h, dst=dst, a=src(own), Lh=Lh: h.dma_start(out=dst[:, :, 64:64 + Lh], in_=a), dma="pad")
                P.add("sp", lambda h, dst=dst, a=src(allv[0]), Lh=Lh: h.dma_start(out=dst[:, :, 0:64], in_=a[:, :, Lh - 64:Lh]), dma="pad")
                P.add("sp", lambda h, dst=dst, a=src(allv[1]), Lh=Lh: h.dma_start(out=dst[:, :, 64 + Lh:128 + Lh], in_=a[:, :, 0:64]), dma="pad")
            dstv = vbp[g]

            def srcv(vw, g=g, dil=dil):
                return vw["vb"][g].rearrange("(r q) e -> r q e", r=dil)
            P.add("sp", lambda h, dstv=dstv, a=srcv(own), Lh=Lh: h.dma_start(out=dstv[:, 64:64 + Lh, :], in_=a), dma="pad")
            P.add("sp", lambda h, dstv=dstv, a=srcv(allv[0]), Lh=Lh: h.dma_start(out=dstv[:, 0:64, :], in_=a[:, Lh - 64:Lh, :]), dma="pad")
            P.add("sp", lambda h, dstv=dstv, a=srcv(allv[1]), Lh=Lh: h.dma_start(out=dstv[:, 64 + Lh:128 + Lh, :], in_=a[:, 0:64, :]), dma="pad")
        P.barrier()

        I = {"qa": Sq["qa"], "qb": Sq["qb"], "qc": Sq["qc"], "gate": Sq["gate"], "ka_all": _Idx(lambda r, allv=allv: allv[r]["ka"]), "va_all": _Idx(lambda r, allv=allv: allv[r]["va"]),
             "memT": I0["memT"], "w_mem_kv": I0["w_mem_kv"][l], "mrow": I0["mrow"], "btab": I0["btab"], "dl": I0["dl"][l],
             "w_proj_a": I0["w_proj_a"][l], "w_proj_b": I0["w_proj_b"][l], "w_proj_c": I0["w_proj_c"][l], "w_out": I0["w_out"][l],
             "w_ffn_gate": I0["w_ffn_gate"][l], "w_ffn_up": I0["w_ffn_up"][l], "w_ffn_down": I0["w_ffn_down"][l],
             "gmem_sb": sm["gmem"], "subln_sb": (sm["subln"][0][:, l:l + 1], sm["subln"][1]),
             "gffn_sb": (sm["gffn"][0][:, l * KC:(l + 1) * KC], sm["gffn"][1])}
        for g in range(3):
            I[f"kbp{g}"] = kbp[g]
            I[f"vbp{g}"] = vbp[g]
        phase2(B, I, lam_init)
        phase3(B, I)
        phase4(B, I)
    evs = store_x(B, out_d, sm["gfinal"])
    return B.finish(evs)


def _rel_bucket_np(rel):
    rel = np.asarray(rel, dtype=np.int64)
    half_b, max_exact = 16, 8
    n = np.abs(rel)
    nf = np.maximum(n, 1).astype(np.float32)
    large = max_exact + (np.log(nf / np.float32(max_exact)) / np.float32(math.log(1024 / max_exact))
                         * np.float32(half_b - max_exact)).astype(np.int32)
    large = np.minimum(large, half_b - 1)
    return np.where(rel > 0, half_b, 0) + np.where(n < max_exact, n, large)


def _bias_layouts(rel_bias):
    rel_bias = np.asarray(rel_bias, dtype=np.float32)
    bk = _rel_bucket_np(np.arange(0, 4097) - 2048)
    Mf = rel_bias[bk][:, :16].T
    out = []
    p = np.arange(128)[:, None]
    jj = np.arange(128)[None, :]
    for s in range(2):
        B0 = 1025 - 1024 * s
        mrow = np.ascontiguousarray(Mf[:, B0:B0 + 3072])
        btab = np.zeros((12, 128, 4, 128), np.float32)
        for g, (_, dil) in enumerate(GROUPS):
            QT = min(128, (T // dil))
            for j in range(4):
                col = 16 + g * 4 + j
                sa = p - 64 - jj
                sb_ = p + 64 - jj
                va = np.abs(sa) <= 64
                vb = np.abs(sb_) <= 64
                ta = rel_bias[_rel_bucket_np(sa * dil), col]
                tb = rel_bias[_rel_bucket_np(sb_ * dil), col]
                vaf = va & ((p >= 64) if s == 0 else True)
                vbl = vb & ((p < QT - 64) if s == 1 else True)
                for a, (tv, vv) in enumerate(((ta, va), (tb, vb), (ta, vaf), (tb, vbl))):
                    btab[g * 4 + j, :, a, :] = np.where(vv, tv, np.float32(-30000.0))
        out.append((mrow, np.ascontiguousarray(btab.reshape(12, 128, 512))))
    return out


def _fm_vec(v, nchunk):
    return np.ascontiguousarray(np.asarray(v, np.float32).reshape(nchunk, 128).T)


_NC_CACHE = {}


def _get_nc(name):
    if name not in _NC_CACHE:
        _NC_CACHE[name] = {"fused8": build_fused8}[name]()
    return _NC_CACHE[name]


def kernel(x, mem, rel_bias, mem_norm, attn_norm, w_in, diff_lambda, diff_subln, w_mem_kv, w_gate, b_gate,
           w_proj_a, w_proj_b, w_proj_c, w_out, ffn_norm, w_ffn_gate, w_ffn_up, w_ffn_down, final_norm):
    f32 = np.float32
    x = np.asarray(x, f32)
    mem = np.asarray(mem, f32)
    cores = list(range(8))
    lay = _bias_layouts(rel_bias)
    A = lambda a: np.ascontiguousarray(np.asarray(a, f32))

    def fm_all(v, nchunk):
        return np.ascontiguousarray(np.concatenate([_fm_vec(v[l], nchunk) for l in range(DEPTH)], axis=1))

    common = {
        "w_in": A(w_in), "w_gate": A(w_gate), "w_mem_kv": A(w_mem_kv), "w_proj_a": A(w_proj_a), "w_proj_b": A(w_proj_b),
        "w_proj_c": A(w_proj_c), "w_out": A(w_out), "w_ffn_gate": A(w_ffn_gate), "w_ffn_up": A(w_ffn_up), "w_ffn_down": A(w_ffn_down),
        "dl": A(diff_lambda).reshape(DEPTH, 1, 256), "g_attn": fm_all(attn_norm, KC), "b_gate": fm_all(b_gate, 48),
        "gffn": fm_all(ffn_norm, KC), "subln": np.ascontiguousarray(A(diff_subln).T), "gmem": _fm_vec(mem_norm, KC),
        "gfinal": _fm_vec(final_norm, KC), "ident": np.eye(128, dtype=f32),
    }
    in_maps = []
    for c in cores:
        b, s_ = c // 2, c % 2
        d = dict(common)
        d["xT_in"] = np.ascontiguousarray(x[b, s_ * T:(s_ + 1) * T].T)
        d["memT"] = np.ascontiguousarray(mem[b].T)
        d["mrow"] = lay[s_][0]
        d["btab"] = lay[s_][1]
        in_maps.append(d)
    res = run_bass_kernel_spmd(_get_nc("fused8"), in_maps, core_ids=cores).results
    out = np.empty((4, SEQ, D), f32)
    for c in cores:
        b, s_ = c // 2, c % 2
        out[b, s_ * T:(s_ + 1) * T] = np.asarray(res[c]["out"], f32).T
    return out
```

```python
import math
from contextlib import ExitStack
import numpy as np
import ml_dtypes
import concourse.bass as bass
import concourse.mybir as mybir
from concourse.bass_utils import run_bass_kernel_spmd

F32 = mybir.dt.float32
BF16 = mybir.dt.bfloat16
AF = mybir.ActivationFunctionType
ALU = mybir.AluOpType
NPBF = ml_dtypes.bfloat16

D = 2048
T = 1024
SEQ = 2048
DEPTH = 4
DFF = 5632
EPS = 1e-5
KC = 16
GROUPS = ((128, 1), (512, 4), (2048, 16))
NEG = -1e30

ENGS = ("pe", "act", "dve", "pool", "sp")


class Buf:
    __slots__ = ("name", "w", "r")

    def __init__(self, name=""):
        self.name = name
        self.w = None
        self.r = []


class Plan:
    def __init__(self):
        self.ops = {e: [] for e in ENGS}
        self.seen = {e: {} for e in ENGS}
        self.dma_count = {}

    def buf(self, name=""):
        return Buf(name)

    def bufs(self, n, name=""):
        return [Buf(f"{name}{i}") for i in range(n)]

    def _need(self, eng, ev, waits):
        kind, k, v = ev
        if kind == "op":
            if k == eng and eng == "pe":
                return
            key = ("op", k)
        else:
            key = ("dma", k)
        s = self.seen[eng]
        if s.get(key, -1) >= v:
            return
        s[key] = v
        if kind == "op":
            self.ops[k][v]["inc"] = True
        waits.append(ev)

    def add(self, eng, fn, reads=(), writes=(), dma=None, inc=16):
        waits = []
        for b in reads:
            if b.w is not None:
                self._need(eng, b.w, waits)
        for b in writes:
            if b.w is not None:
                self._need(eng, b.w, waits)
            for ev in b.r:
                self._need(eng, ev, waits)
        idx = len(self.ops[eng])
        op = {"fn": fn, "waits": waits, "inc": False, "dma": None}
        if dma is not None:
            c = self.dma_count.get(dma, 0) + inc
            self.dma_count[dma] = c
            op["dma"] = dma
            op["dinc"] = inc
            ev = ("dma", dma, c)
        else:
            ev = ("op", eng, idx)
        self.ops[eng].append(op)
        for b in reads:
            b.r = [e for e in b.r if not (e[0] == ev[0] and e[1] == ev[1])]
            b.r.append(ev)
        for b in writes:
            b.w = ev
            b.r = []
        return ev

    def wait_events(self, eng, evs):
        waits = []
        for ev in evs:
            self._need(eng, ev, waits)
        self.ops[eng].append({"fn": None, "waits": waits, "inc": False, "dma": None})

    def barrier(self):
        evs = []
        for e in ENGS:
            if e == "sp":
                continue
            if self.ops[e]:
                for i in range(len(self.ops[e]) - 1, -1, -1):
                    if self.ops[e][i]["fn"] is not None and self.ops[e][i]["dma"] is None:
                        evs.append(("op", e, i))
                        break
        for k, c in self.dma_count.items():
            evs.append(("dma", k, c))
        for e in ENGS:
            if e != "pool":
                self.wait_events(e, evs)

    def emit(self, nc, es):
        esem = {e: es.enter_context(nc.semaphore(f"s_{e}")) for e in ENGS}
        dsem = {k: es.enter_context(nc.semaphore(f"d_{i}")) for i, k in enumerate(self.dma_count)}
        val = {}
        for e in ENGS:
            c = 0
            v = []
            for op in self.ops[e]:
                if op["inc"]:
                    c += 1
                v.append(c)
            val[e] = v
        self.n_sems = len(esem) + len(dsem)

        def body(e):
            def run(h):
                for op in self.ops[e]:
                    for (kind, k, v) in op["waits"]:
                        if kind == "op":
                            h.wait_ge(esem[k], val[k][v])
                        else:
                            h.wait_ge(dsem[k], v)
                    if op["fn"] is None:
                        continue
                    ins = op["fn"](h)
                    if op["dma"] is not None:
                        if op["dinc"] == 1:
                            ins.then_inc(dsem[op["dma"]])
                        else:
                            ins.then_inc(dsem[op["dma"]], op["dinc"])
                    elif op["inc"]:
                        ins.then_inc(esem[e], 1)
            return run

        with nc.Block() as block:
            block.tensor(body("pe"))
            block.scalar(body("act"))
            block.vector(body("dve"))
            block.gpsimd(body("pool"))
            block.sync(body("sp"))


class Builder:
    def __init__(self):
        self.nc = bass.Bass("TRN2", target_bir_lowering=False)
        self.P = Plan()
        self.es = ExitStack()
        self.n_dma = 0
        nc = self.nc
        self.ps = [self.es.enter_context(nc.psum_tensor(f"ps{i}", [128, 512], F32)) for i in range(8)]
        self.bps = self.P.bufs(8, "ps")
        self.ps_rr = {}
        self.NW = 256
        self.wslots = [self.sb(f"wslot{i}", [128, 16 * 256], BF16) for i in range(3)]
        self.bw = self.P.bufs(3, "w")
        self.w_i = 0
        self.ones_f = self.sb("ones_f", [128, 128], F32)
        self.ones_b = self.sb("ones_b", [128, 128], BF16)
        self.b_const = self.P.buf("const")
        self.P.add("dve", lambda h: h.memset(self.ones_f[:], 1.0), writes=[self.b_const])
        self.P.add("dve", lambda h: h.memset(self.ones_b[:], 1.0), writes=[self.b_const])
        self.ident_b = self.sb("ident_b", [128, 128], BF16)
        self.stg_i = 0
        self.stg = [self.sb(f"stg{i}", [128, T], BF16) for i in range(3)]
        self.bstg = self.P.bufs(3, "stg")
        self.tmpf = [self.sb(f"tmpf{i}", [128, 512], F32) for i in range(4)]
        self.btmpf = self.P.bufs(4, "tmpf")
        self.tmpf_i = 0

    def sb(self, name, shape, dt):
        return self.es.enter_context(self.nc.sbuf_tensor(name, shape, dt))

    def uname(self, name):
        self._uid = getattr(self, "_uid", 0) + 1
        return f"{name}_u{self._uid}"

    def dram_in(self, name, shape, dt=F32):
        return self.nc.dram_tensor(name, list(shape), dt, kind="ExternalInput").ap()

    def dram_out(self, name, shape, dt=F32):
        return self.nc.dram_tensor(name, list(shape), dt, kind="ExternalOutput").ap()

    def dram_tmp(self, name, shape, dt=F32):
        return self.nc.dram_tensor(name, list(shape), dt, kind="Internal").ap()

    def dkey(self, name):
        self.n_dma += 1
        return f"{name}_{self.n_dma}"

    def psum(self, pool, banks):
        i = self.ps_rr.get(pool, 0)
        self.ps_rr[pool] = i + 1
        b = banks[i % len(banks)]
        return self.ps[b], self.bps[b]

    def next_tmpf(self):
        i = self.tmpf_i % 4
        self.tmpf_i += 1
        return self.tmpf[i], self.btmpf[i]

    def next_stg(self):
        i = self.stg_i % 3
        self.stg_i += 1
        return self.stg[i], self.bstg[i], f"stg{i}"

    def wload(self, W, k0, kc, c0, ncol):
        i = self.w_i % 3
        self.w_i += 1
        slot = self.wslots[i]
        view = slot[:, 0:kc * ncol].rearrange("p (c n) -> p c n", n=ncol)
        src = W[k0:k0 + kc * 128, c0:c0 + ncol].rearrange("(c p) n -> p c n", p=128)
        self.P.add("pool", lambda h: h.dma_start(out=view, in_=src), writes=[self.bw[i]], dma=f"w{i}")
        return view, self.bw[i]

    def rmsnorm(self, src, bsrc, dst, bdst, gain, bgain, ntok, nfeat_chunks=KC, dfeat=D):
        P = self.P
        for t0 in range(0, ntok, 512):
            n = min(512, ntok - t0)
            pst, bpst = self.psum("misc", [7])
            for c in range(nfeat_chunks):
                sq, bsq = self.next_tmpf()
                P.add("act", lambda h, sq=sq, c=c, t0=t0, n=n: h.activation(out=sq[:, 0:n], in_=src[:, c, t0:t0 + n], func=AF.Square),
                      reads=[bsrc], writes=[bsq])
                P.add("pe", lambda h, sq=sq, c=c, pst=pst, n=n: h.matmul(pst[:, 0:n], self.ones_f[:], sq[:, 0:n],
                                                                     start=(c == 0), stop=(c == nfeat_chunks - 1)),
                      reads=[bsq, self.b_const], writes=[bpst])
            rs, brs = self.next_tmpf()
            P.add("act", lambda h, rs=rs, pst=pst, n=n: h.activation(out=rs[:, 0:n], in_=pst[:, 0:n], func=AF.Sqrt,
                                                                 bias=self.eps_col[:, 0:1], scale=1.0 / dfeat),
                  reads=[bpst, self.b_const], writes=[brs])
            P.add("dve", lambda h, rs=rs, n=n: h.reciprocal(out=rs[:, 0:n], in_=rs[:, 0:n]), reads=[brs], writes=[brs])
            for c in range(nfeat_chunks):
                P.add("dve", lambda h, rs=rs, c=c, t0=t0, n=n: h.scalar_tensor_tensor(out=dst[:, c, t0:t0 + n], in0=src[:, c, t0:t0 + n],
                                                                          scalar=gain[:, c:c + 1], in1=rs[:, 0:n],
                                                                          op0=ALU.mult, op1=ALU.mult),
                      reads=[bsrc, brs, bgain], writes=[bdst])

    def dense_fm(self, W, k0, kc, c0, ncols, rhs_fn, brhs, tgs, evac, nw=None):
        nw = nw or self.NW
        P = self.P
        for w0 in range(0, ncols, nw):
            wn = min(nw, ncols - w0)
            wt, bwt = self.wload(W, k0, kc, c0 + w0, wn)
            for oc in range(wn // 128):
                for tg in tgs:
                    ps, bps = self.psum("dense", [0, 1, 2, 3])
                    r0 = rhs_fn(0, tg)
                    n = 1
                    for s_ in r0.shape[1:]:
                        n *= s_
                    for k in range(kc):
                        P.add("pe", lambda h, wt=wt, k=k, oc=oc, tg=tg, ps=ps, n=n: h.matmul(
                            ps[:, 0:n], wt[:, k, oc * 128:(oc + 1) * 128], rhs_fn(k, tg), start=(k == 0), stop=(k == kc - 1)),
                            reads=[bwt, brhs], writes=[bps])
                    evac((w0 // 128) + oc, tg, ps, bps)

    def dense_tm(self, W, k0, kc, c0, ncols, lhs_fn, blhs, ntt, evac):
        P = self.P
        nw = self.NW
        for w0 in range(0, ncols, nw):
            wt, bwt = self.wload(W, k0, kc, c0 + w0, nw)
            for tt in range(ntt):
                ps, bps = self.psum("dense", [0, 1, 2, 3])
                nparts = len(lhs_fn(0, tt))
                for part in range(nparts):
                    for k in range(kc):
                        def mm(h, wt=wt, k=k, tt=tt, ps=ps, part=part):
                            ap, r0, nr = lhs_fn(k, tt)[part]
                            return h.matmul(ps[r0:r0 + nr, 0:nw], ap, wt[:, k, :], start=(k == 0), stop=(k == kc - 1))
                        P.add("pe", mm, reads=[bwt, blhs], writes=[bps])
                evac(w0, tt, ps, bps)

    def finish(self, out_events):
        self.P.wait_events("sp", out_events)
        self.P.emit(self.nc, self.es)
        if getattr(self, "_es5", None) is not None:
            self._es5.close()
        self.es.close()
        return self.nc


def class_view(ap2d, dil):
    if dil == 1:
        return ap2d.rearrange("p (r pos) -> p r pos", r=1)
    return ap2d.rearrange("p (pos r) -> p r pos", r=dil)


def phase1(B, xT, bxT, hT, bhT, g_attn, bg, w_in, w_gate, bgate_sb, bbg, outs, part="all"):
    P = B.P
    if part != "gates":
        B.rmsnorm(xT, bxT, hT, bhT, g_attn, bg, T)
    out_evs = []

    def nat_rhs(k, tg):
        return hT[:, k, tg * 512:(tg + 1) * 512]

    def cls_rhs(dil):
        def f(k, tg):
            v = class_view(hT[:, k, :], dil)
            nr = (512 * dil) // T
            if dil == 1:
                return hT[:, k, tg * 512:(tg + 1) * 512]
            return v[:, tg * nr:(tg + 1) * nr, :]
        return f

    cur = {}

    def make_evac(dst, scale=None, sigmoid=False, chunk_off=0, eng_alt=("dve", "act"), dil=1, rev=False):
        cnt = [0]

        def evac(ci, tg, ps, bps):
            c = ci + chunk_off
            if tg == 0:
                cur["s"] = B.next_stg()
            stg, bstg, skey = cur["s"]
            o = stg[:, tg * 512:(tg + 1) * 512]
            psv = ps[:]
            if rev:
                psv = bass.AP(tensor=ps[:].tensor, offset=ps[:, 511:512].offset, ap=[list(ps[:].ap[0]), [-1, 512]])
            if dil > 1:
                npos = 512 // dil
                o = stg[:].rearrange("p (r pos) -> p r pos", r=dil)[:, :, tg * npos:(tg + 1) * npos]
                psv = class_view(ps[:], dil)
            if sigmoid:
                P.add("act", lambda h, o=o, ps=ps, c=c: h.activation(out=o, in_=ps[:], func=AF.Sigmoid,
                                                                    bias=bgate_sb[:, c:c + 1], scale=1.0),
                      reads=[bps, bbg], writes=[bstg])
            else:
                e = eng_alt[cnt[0] % 2]
                cnt[0] += 1
                if e == "act":
                    P.add("act", lambda h, o=o, psv=psv: h.activation(out=o, in_=psv, func=AF.Copy,
                                                                      scale=(scale if scale is not None else 1.0)),
                          reads=[bps], writes=[bstg])
                elif scale is not None:
                    P.add("dve", lambda h, o=o, psv=psv: h.tensor_scalar_mul(out=o, in0=psv, scalar1=scale),
                          reads=[bps], writes=[bstg])
                else:
                    P.add("dve", lambda h, o=o, psv=psv: h.tensor_copy(out=o, in_=psv), reads=[bps], writes=[bstg])
            if tg == 1:
                ev = P.add("sp", lambda h, stg=stg, c=c: h.dma_start(out=dst[c], in_=stg[:]), reads=[bstg],
                           writes=[B.P.buf()], dma=skey)
                out_evs.append(ev)
        return evac

    sB = 128 ** -0.5
    if part == "gates":
        B.dense_fm(w_gate, 0, KC, 0, 3 * D, nat_rhs, bhT, [0, 1], make_evac(outs["gate"], sigmoid=True))
        return out_evs
    B.dense_fm(w_in, 0, KC, 0, 1024, nat_rhs, bhT, [0, 1], make_evac(outs["qa"], scale=0.125, rev=True))
    B.dense_fm(w_in, 0, KC, 1024, 1024, nat_rhs, bhT, [0, 1], make_evac(outs["ka"]))
    for g, (_, dil) in enumerate(GROUPS):
        B.dense_fm(w_in, 0, KC, 3072 + g * 512, 512, nat_rhs, bhT, [0, 1],
                   make_evac(outs["qb"], scale=sB, chunk_off=g * 4, dil=dil))
        B.dense_fm(w_in, 0, KC, 4608 + g * 512, 512, nat_rhs, bhT, [0, 1],
                   make_evac(outs["kb"], chunk_off=g * 4, dil=dil))
    B.dense_fm(w_in, 0, KC, 7680, 512, nat_rhs, bhT, [0, 1], make_evac(outs["qc"], scale=sB))

    def make_evac_tm(dst2d):
        cnt = [0]

        def evac(w0, tt, ps, bps):
            stg, bstg, skey = B.next_stg()
            e = ("dve", "act")[cnt[0] % 2]
            cnt[0] += 1
            if e == "dve":
                P.add("dve", lambda h, stg=stg, ps=ps: h.tensor_copy(out=stg[:, 0:256], in_=ps[:, 0:256]), reads=[bps], writes=[bstg])
            else:
                P.add("act", lambda h, stg=stg, ps=ps: h.activation(out=stg[:, 0:256], in_=ps[:, 0:256], func=AF.Copy),
                      reads=[bps], writes=[bstg])
            ev = P.add("sp", lambda h, stg=stg, tt=tt, w0=w0: h.dma_start(out=dst2d[tt * 128:(tt + 1) * 128, w0:w0 + 256],
                                                                      in_=stg[:, 0:256]),
                       reads=[bstg], writes=[B.P.buf()], dma=skey)
            out_evs.append(ev)
        return evac

    def nat_lhs(k, tt):
        return [(hT[:, k, tt * 128:(tt + 1) * 128], 0, 128)]

    def cls_lhs(dil):
        def f(k, tt):
            if dil == 1:
                return [(hT[:, k, tt * 128:(tt + 1) * 128], 0, 128)]
            v = class_view(hT[:, k, :], dil)
            Lh = T // dil
            if Lh >= 128:
                r = (tt * 128) // Lh
                p0 = (tt * 128) % Lh
                return [(v[:, r, p0:p0 + 128], 0, 128)]
            nr = 128 // Lh
            return [(v[:, tt * nr + i, :], i * Lh, Lh) for i in range(nr)]
        return f

    B.dense_tm(w_in, 0, KC, 2048, 1024, nat_lhs, bhT, 8, make_evac_tm(outs["va"]))
    for g, (_, dil) in enumerate(GROUPS):
        B.dense_tm(w_in, 0, KC, 6144 + g * 512, 512, cls_lhs(dil), bhT, 8, make_evac_tm(outs["vb"][g]))
    if part == "all":
        B.dense_fm(w_gate, 0, KC, 0, 3 * D, nat_rhs, bhT, [0, 1], make_evac(outs["gate"], sigmoid=True))
    return out_evs


def load_small(B, name, dram_ap, shape, dt=F32):
    t = B.sb(name, shape, dt)
    b = B.P.buf(name)
    B.P.add("sp", lambda h: h.dma_start(out=t[:], in_=dram_ap), writes=[b], dma="small")
    B.small_bufs = getattr(B, "small_bufs", []) + [b]
    return t, b


def finalize_small(B):
    tot = B.P.dma_count.get("small", 0)
    for b in getattr(B, "small_bufs", []):
        b.w = ("dma", "small", tot)


W_G = 2944


def phase2(B, I, lam_init):
    P = B.P
    nc = B.nc
    es2 = ExitStack()

    def sb(name, shape, dt):
        return es2.enter_context(nc.sbuf_tensor(B.uname("s2_" + name), shape, dt))

    kc_sb = sb("kc_sb", [128, 4, 256], BF16)
    bkc = P.buf("kc")
    vc_sb = sb("vc_sb", [128, 2, 512], BF16)
    bvc = P.buf("vc")
    esm = ExitStack()
    memn = esm.enter_context(nc.sbuf_tensor(B.uname("s2_memn"), [128, KC, 256], BF16))
    bmemn = P.buf("memn")
    memf = esm.enter_context(nc.sbuf_tensor(B.uname("s2_memf"), [128, KC, 256], F32))
    bmemf = P.buf("memf")
    msrc = I["memT"].rearrange("(c p) t -> p c t", p=128)
    P.add("sp", lambda h: h.dma_start(out=memf[:], in_=msrc), writes=[bmemf], dma="memf")
    B.rmsnorm(memf, bmemf, memn, bmemn, I["gmem_sb"][0], I["gmem_sb"][1], 256)

    def evac_kc(ci, tg, ps, bps):
        P.add("dve", lambda h, ci=ci, ps=ps: h.tensor_copy(out=kc_sb[:, ci, :], in_=ps[:, 0:256]), reads=[bps], writes=[bkc])

    B.dense_fm(I["w_mem_kv"], 0, KC, 0, 512, lambda k, tg: memn[:, k, :], bmemn, [0], evac_kc)

    def evac_vc(w0, tt, ps, bps):
        P.add("dve", lambda h, w0=w0, tt=tt, ps=ps: h.tensor_copy(out=vc_sb[:, tt, w0:w0 + 256], in_=ps[:, 0:256]), reads=[bps], writes=[bvc])

    B.dense_tm(I["w_mem_kv"], 0, KC, 512, 512, lambda k, tt: [(memn[:, k, tt * 128:(tt + 1) * 128], 0, 128)], bmemn, 2, evac_vc)
    P.barrier()
    esm.close()

    oT, boT = B.A32, B.bA32
    kbuf = [sb(f"kbuf{i}", [128, 3072], BF16) for i in range(2)]
    bkbuf = P.bufs(2, "kbuf")
    vbuf = [sb("vbuf0", [128, 32, 128], BF16)]
    bvbuf = P.bufs(2, "vbuf")
    gbuf = [sb(f"gbuf{i}", [128, W_G], BF16) for i in range(2)]
    bgbuf = P.bufs(2, "gbuf")
    tbuf = [sb(f"tbuf{i}", [128, 512], F32) for i in range(2)]
    btbuf = P.bufs(2, "tbuf")
    cnt_t = [0]
    NPT = 4
    pt = [sb(f"pt{i}", [128, 512], BF16) for i in range(NPT)]
    bpt = P.bufs(NPT, "pt")
    t01 = [sb(f"t01_{i}", [128, T], F32) for i in range(2)]
    bt01 = P.bufs(2, "t01")
    obuf = sb("obuf", [128, T], F32)
    bobuf = P.buf("obuf")
    cnt = {"k": 0, "q": 0, "g": 0, "p": 0}
    qpad = [[sb(f"qpad{i}_{m}", [128, T], BF16) for m in range(2)] for i in range(2)]
    bqpad = [P.buf(f"qpad{i}") for i in range(2)]
    for i in range(2):
        P.add("dve", lambda h, i=i: h.memset(qpad[i][0][64:128, :], 0.0), writes=[bqpad[i]])
        P.add("dve", lambda h, i=i: h.memset(qpad[i][1][0:64, :], 0.0), writes=[bqpad[i]])
    qbuf = [qpad[0][0], qpad[1][0]]
    bqbuf = bqpad
    dacc = [sb(f"dacc{i}", [128, 512], F32) for i in range(2)]
    bdacc = P.bufs(2, "dacc")
    deferred = []

    def run_deferred(flush=False):
        for d in list(deferred):
            d[0] -= 1
            if flush or d[0] <= 0:
                deferred.remove(d)
                d[1]()
    cnt_d = [0]

    def nxt(name, n):
        i = cnt[name] % n
        cnt[name] += 1
        return i

    dl = sb("dl", [128, 256], F32)
    bdl = P.buf("dl")
    dl_src = bass.AP(tensor=I["dl"].tensor, offset=I["dl"].offset, ap=[[0, 128], [1, 256]])
    P.add("sp", lambda h: h.dma_start(out=dl[:], in_=dl_src), writes=[bdl], dma="dl")
    lam = sb("lam", [128, 8], F32)
    blam = P.buf("lam")
    prod = sb("prod", [128, 128], F32)
    bprod = P.buf("prod")
    dv = dl[:].rearrange("p (a d) -> p a d", d=64)
    P.add("dve", lambda h: h.tensor_tensor(out=prod[:].rearrange("p (a d) -> p a d", d=64), in0=dv[:, 0:4:2, :], in1=dv[:, 1:4:2, :],
                                           op=ALU.mult), reads=[bdl], writes=[bprod])
    P.add("dve", lambda h: h.reduce_sum(out=lam[:, 0:2], in_=prod[:].rearrange("p (a d) -> p a d", d=64), axis=mybir.AxisListType.X),
          reads=[bprod], writes=[blam])
    P.add("act", lambda h: h.activation(out=lam[:, 0:2], in_=lam[:, 0:2], func=AF.Exp), reads=[blam], writes=[blam])
    P.add("dve", lambda h: h.tensor_tensor(out=lam[:, 2:3], in0=lam[:, 1:2], in1=lam[:, 0:1], op=ALU.subtract), reads=[blam], writes=[blam])
    if lam_init is None:
        lc, blc = I["lconst_sb"]
        P.add("dve", lambda h: h.tensor_tensor(out=lam[:, 2:3], in0=lam[:, 2:3], in1=lc[:, 0:1], op=ALU.subtract),
              reads=[blam, blc], writes=[blam])
        P.add("dve", lambda h: h.tensor_tensor(out=lam[:, 3:4], in0=I["subln_sb"][0][:, 0:1], in1=lc[:, 1:2], op=ALU.mult),
              reads=[blam, blc, I["subln_sb"][1]], writes=[blam])
    else:
        P.add("dve", lambda h: h.tensor_scalar_add(out=lam[:, 2:3], in0=lam[:, 2:3], scalar1=-lam_init), reads=[blam], writes=[blam])
        P.add("dve", lambda h: h.tensor_scalar_mul(out=lam[:, 3:4], in0=I["subln_sb"][0][:, 0:1], scalar1=1.0 - lam_init),
              reads=[blam, I["subln_sb"][1]], writes=[blam])

    def pipeline(items, LA=3):
        n = len(items)
        for st_ in range(n + LA):
            if st_ < n:
                items[st_][0]()
            if st_ - LA >= 0:
                items[st_ - LA][1]()

    ST_BANKS = [4, 5, 6, 7]

    for hh in range(8):
        ki = nxt("k", 2)
        qi = nxt("q", 2)
        kt, bkt = kbuf[ki], bkbuf[ki]
        qp, bqt = qpad[qi], bqpad[qi]
        hf = hh % 2
        vt, bvt = vbuf[0][:, 16 * hf:16 * hf + 16, :], bvbuf[hf]
        for r in range(2):
            P.add("sp", lambda h, kt=kt, r=r, hh=hh: h.dma_start(out=kt[:, r * T:(r + 1) * T], in_=I["ka_all"][r, hh]),
                  writes=[bkt], dma=f"kb{ki}")
            vsrc = I["va_all"][r, :, hh * 128:(hh + 1) * 128].rearrange("(t p) e -> p t e", p=128)
            P.add("sp", lambda h, vt=vt, r=r, vsrc=vsrc: h.dma_start(out=vt[:, r * 8:(r + 1) * 8, :], in_=vsrc),
                  writes=[bvt], dma=f"vb{hf}")
        for m in range(2):
            P.add("sp", lambda h, qp=qp, hh=hh, m=m: h.dma_start(out=qp[m][m * 64:(m + 1) * 64, :], in_=I["qa"][hh][m * 64:(m + 1) * 64, :]),
                  writes=[bqt], dma=f"qp{qi}")
        items = []
        for m in range(2):
            gi = nxt("g", 2)
            gt, bgt = gbuf[gi], bgbuf[gi]
            hm = m * 8 + hh
            gsrc = bass.AP(tensor=I["mrow"].tensor, offset=I["mrow"][hm, 0:1].offset, ap=[[1, 128], [1, W_G]])
            P.add("pool", lambda h, gt=gt, gsrc=gsrc: h.dma_start(out=gt[:], in_=gsrc), writes=[bgt], dma=f"gb{gi}")
            for qg in range(2):
                grp = {}
                for k2 in range(0, 16, 2):
                    it = {}

                    def sA(m=m, qg=qg, k2=k2, it=it, gt=gt, bgt=bgt, kt=kt, qp=qp, bkt=bkt, bqt=bqt):
                        sts = [B.psum("st", ST_BANKS) for _ in range(2)]
                        for u in range(2):
                            kk = k2 + u
                            st, bst = sts[u]
                            P.add("pe", lambda h, st=st, kk=kk: h.matmul(st[:], kt[:, kk * 128:(kk + 1) * 128], qp[m][:, qg * 512:(qg + 1) * 512],
                                                                         start=True, stop=False), reads=[bkt, bqt], writes=[bst])
                        for u in range(2):
                            kk = k2 + u
                            st, bst = sts[u]
                            c0 = kk * 128 - 512 * qg + 512
                            P.add("pe", lambda h, st=st, c0=c0: h.matmul(st[:], B.ident_b[:], gt[:, c0:c0 + 512], start=False, stop=True),
                                  reads=[bgt, B.b_const], writes=[bst])
                        it["pi"] = []
                        for u in range(2):
                            st, bst = sts[u]
                            pi = nxt("p", NPT)
                            it["pi"].append(pi)
                            P.add("act", lambda h, st=st, pi=pi: h.activation(out=pt[pi][:], in_=st[:], func=AF.Exp), reads=[bst], writes=[bpt[pi]])

                    def sB(m=m, qg=qg, k2=k2, it=it, grp=grp, vt=vt, bvt=bvt):
                        if k2 == 0:
                            grp["num"] = B.psum("num", [0, 1])
                            grp["den"] = B.psum("den", [2, 3])
                            di = cnt_d[0] % 2
                            cnt_d[0] += 1
                            grp["da"] = (dacc[di], bdacc[di])
                        num, bnum = grp["num"]
                        den, bden = grp["den"]
                        da, bda = grp["da"]
                        for u in range(2):
                            kk = k2 + u
                            pi = it["pi"][u]
                            P.add("pe", lambda h, num=num, pi=pi, kk=kk: h.matmul(num[:], vt[:, kk, :], pt[pi][:], start=(kk == 0), stop=(kk == 15)),
                                  reads=[bvt, bpt[pi]], writes=[bnum])
                        pi1 = it["pi"][1]
                        P.add("pe", lambda h, den=den, pi1=pi1: h.matmul(den[:], B.ones_b[:], pt[pi1][:], start=(k2 == 0), stop=False),
                              reads=[B.b_const, bpt[pi1]], writes=[bden])
                        pi0 = it["pi"][0]
                        if k2 == 0:
                            P.add("dve", lambda h, da=da, pi0=pi0: h.tensor_copy(out=da[:], in_=pt[pi0][:]), reads=[bpt[pi0]], writes=[bda])
                        else:
                            P.add("dve", lambda h, da=da, pi0=pi0: h.tensor_tensor(out=da[:], in0=pt[pi0][:], in1=da[:], op=ALU.add),
                                  reads=[bpt[pi0], bda], writes=[bda])
                        run_deferred()
                        if k2 == 14:
                            def epi(num=num, bnum=bnum, den=den, bden=bden, da=da, bda=bda, m=m, qg=qg):
                                P.add("pe", lambda h, den=den, da=da: h.matmul(den[:], B.ones_f[:], da[:], start=False, stop=True),
                                      reads=[B.b_const, bda], writes=[bden])
                                rc, brc = B.next_tmpf()
                                P.add("dve", lambda h, rc=rc, den=den: h.reciprocal(out=rc[:], in_=den[:]), reads=[bden], writes=[brc])
                                numr = bass.AP(tensor=num[:].tensor, offset=num[:, 511:512].offset, ap=[list(num[:].ap[0]), [-1, 512]])
                                rcr = bass.AP(tensor=rc[:].tensor, offset=rc[:, 511:512].offset, ap=[list(rc[:].ap[0]), [-1, 512]])
                                P.add("dve", lambda h, rcr=rcr, numr=numr: h.tensor_tensor(out=t01[m][:, qg * 512:(qg + 1) * 512], in0=numr, in1=rcr,
                                                                                         op=ALU.mult), reads=[bnum, brc], writes=[bt01[m]])
                            deferred.append([3, epi])
                    items.append((sA, sB))
        pipeline(items, LA=1)

        def head_epi(hh=hh):
            P.add("dve", lambda h: h.scalar_tensor_tensor(out=obuf[:], in0=t01[1][:], scalar=lam[:, 2:3], in1=t01[0][:], op0=ALU.mult, op1=ALU.add),
                  reads=[bt01[0], bt01[1], blam], writes=[bobuf])
            for qg in range(2):
                sq, bsq = B.next_tmpf()
                P.add("act", lambda h, sq=sq, qg=qg: h.activation(out=sq[:], in_=obuf[:, qg * 512:(qg + 1) * 512], func=AF.Square),
                      reads=[bobuf], writes=[bsq])
                pss, bpss = B.psum("st", ST_BANKS)
                P.add("pe", lambda h, pss=pss, sq=sq: h.matmul(pss[:], B.ones_f[:], sq[:], start=True, stop=True), reads=[bsq, B.b_const], writes=[bpss])
                rs, brs = B.next_tmpf()
                P.add("act", lambda h, rs=rs, pss=pss: h.activation(out=rs[:], in_=pss[:], func=AF.Ln, bias=B.eps_col[:, 0:1], scale=1.0 / 128),
                      reads=[bpss, B.b_const], writes=[brs])
                P.add("act", lambda h, rs=rs: h.activation(out=rs[:], in_=rs[:], func=AF.Exp, scale=-0.5), reads=[brs], writes=[brs])
                P.add("dve", lambda h, rs=rs, qg=qg, hh=hh: h.scalar_tensor_tensor(out=oT[:, hh, qg * 512:(qg + 1) * 512], in0=obuf[:, qg * 512:(qg + 1) * 512],
                                                                             scalar=lam[:, 3:4], in1=rs[:], op0=ALU.mult, op1=ALU.mult),
                      reads=[bobuf, brs, blam], writes=[boT])
        deferred.append([5, head_epi])
    run_deferred(flush=True)

    accN, baccN = t01[0], bt01[0]
    accD, baccD = t01[1], bt01[1]
    for j in range(4):
        for g, (_, dil) in enumerate(GROUPS):
            Lh = T // dil
            QT = min(128, Lh)
            ntq = Lh // QT
            LP = Lh + 128
            ntile = (LP + 127) // 128
            ki = nxt("k", 2)
            qi = nxt("q", 2)
            gi = cnt_t[0] % 2
            cnt_t[0] += 1
            kt, bkt = kbuf[ki], bkbuf[ki]
            qt, bqt = qbuf[qi], bqbuf[qi]
            vt = vbuf[0]
            tb, btb = tbuf[gi], btbuf[gi]
            P.add("sp", lambda h, kt=kt, g=g, j=j, dil=dil, LP=LP: h.dma_start(out=kt[:, 0:dil * LP], in_=I[f"kbp{g}"][j]), writes=[bkt], dma=f"kb{ki}")
            vsrc = I[f"vbp{g}"][:, :, j * 128:(j + 1) * 128].rearrange("r (t p) e -> p (r t) e", p=128)
            nvt = dil * ntile
            if g == 2:
                vbase = 0
                for hf in range(2):
                    P.add("sp", lambda h, vt=vt, vsrc=vsrc, hf=hf: h.dma_start(out=vt[:, 16 * hf:16 * hf + 16, :], in_=vsrc[:, 16 * hf:16 * hf + 16, :]),
                          writes=[bvbuf[hf]], dma=f"vb{hf}")
            else:
                vbase = 16 * g
                P.add("sp", lambda h, vt=vt, vsrc=vsrc, vbase=vbase, nvt=nvt: h.dma_start(out=vt[:, vbase:vbase + nvt, :], in_=vsrc),
                      writes=[bvbuf[g]], dma=f"vb{g}")
            P.add("sp", lambda h, qt=qt, g=g, j=j: h.dma_start(out=qt[:], in_=I["qb"][g * 4 + j]), writes=[bqt], dma=f"qb{qi}")
            P.add("sp", lambda h, tb=tb, g=g, j=j: h.dma_start(out=tb[:, 0:512], in_=I["btab"][g * 4 + j]), writes=[btb], dma=f"tb{gi}")
            tbv = tb[:, 0:512].rearrange("p (a q) -> p a q", a=4)
            nat = class_view(accN[:], dil)
            natD = class_view(accD[:], dil)
            items = []
            for r in range(dil):
                for i in range(ntq):
                    q_ap = qt[:, r * Lh + i * QT: r * Lh + i * QT + QT]
                    k0 = r * LP + i * QT
                    ta0 = 1 if i == 0 else 0
                    ta1 = 3 if i == ntq - 1 else 2
                    vti = vbase + r * ntile + i * (QT // 128 if QT >= 128 else 0)
                    bvh = bvbuf[vti // 16]
                    it = {}

                    def sA(it=it, k0=k0, ta0=ta0, ta1=ta1, q_ap=q_ap, kt=kt, bkt=bkt, bqt=bqt, tbv=tbv, btb=btb, QT=QT):
                        st, bst = B.psum("st", ST_BANKS)
                        P.add("pe", lambda h, st=st: h.matmul(st[0:128, 0:QT], kt[:, k0:k0 + 128], q_ap, start=True, stop=True),
                              reads=[bkt, bqt], writes=[bst])
                        P.add("pe", lambda h, st=st: h.matmul(st[0:QT, QT:2 * QT], kt[:, k0 + 128:k0 + 128 + QT], q_ap, start=True, stop=True),
                              reads=[bkt, bqt], writes=[bst])
                        lg, blg = B.next_tmpf()
                        P.add("dve", lambda h, lg=lg, st=st: h.tensor_tensor(out=lg[:, 0:2 * QT].rearrange("p (a q) -> p a q", a=2),
                                                                          in0=st[:, 0:2 * QT].rearrange("p (a q) -> p a q", a=2),
                                                                          in1=tbv[:, ta0:ta1 + 1:ta1 - ta0, 0:QT], op=ALU.add),
                              reads=[bst, btb], writes=[blg])
                        pi = nxt("p", NPT)
                        it["pi"] = pi
                        P.add("act", lambda h, lg=lg, pi=pi: h.activation(out=pt[pi][:, 0:2 * QT], in_=lg[:, 0:2 * QT], func=AF.Exp),
                              reads=[blg], writes=[bpt[pi]])

                    def sB(it=it, vti=vti, vt=vt, bvh=bvh, QT=QT, r=r, i=i, g=g, nat=nat, natD=natD):
                        num, bnum = B.psum("num", [0, 1])
                        den, bden = B.psum("den", [2, 3])
                        pi = it["pi"]
                        P.add("pe", lambda h, num=num, pi=pi: h.matmul(num[:, 0:QT], vt[0:128, vti, :], pt[pi][0:128, 0:QT], start=True, stop=False),
                              reads=[bvh, bpt[pi]], writes=[bnum])
                        P.add("pe", lambda h, num=num, pi=pi: h.matmul(num[:, 0:QT], vt[0:QT, vti + 1, :], pt[pi][0:QT, QT:2 * QT], start=False, stop=True),
                              reads=[bvh, bpt[pi]], writes=[bnum])
                        P.add("pe", lambda h, den=den, pi=pi: h.matmul(den[:, 0:QT], B.ones_b[0:128, :], pt[pi][0:128, 0:QT], start=True, stop=False),
                              reads=[B.b_const, bpt[pi]], writes=[bden])
                        P.add("pe", lambda h, den=den, pi=pi: h.matmul(den[:, 0:QT], B.ones_b[0:QT, :], pt[pi][0:QT, QT:2 * QT], start=False, stop=True),
                              reads=[B.b_const, bpt[pi]], writes=[bden])
                        dN = nat[:, r, i * QT:(i + 1) * QT]
                        dD = natD[:, r, i * QT:(i + 1) * QT]
                        if g == 0:
                            P.add("dve", lambda h, num=num: h.tensor_copy(out=dN, in_=num[:, 0:QT]), reads=[bnum], writes=[baccN])
                            P.add("act", lambda h, den=den: h.activation(out=dD, in_=den[:, 0:QT], func=AF.Copy), reads=[bden], writes=[baccD])
                        else:
                            P.add("dve", lambda h, num=num: h.tensor_tensor(out=dN, in0=num[:, 0:QT], in1=dN, op=ALU.add), reads=[bnum, baccN], writes=[baccN])
                            P.add("dve", lambda h, den=den: h.tensor_tensor(out=dD, in0=den[:, 0:QT], in1=dD, op=ALU.add), reads=[bden, baccD], writes=[baccD])
                    items.append((sA, sB))
            pipeline(items)
        P.add("dve", lambda h: h.reciprocal(out=accD[:], in_=accD[:]), reads=[baccD], writes=[baccD])
        P.add("dve", lambda h, j=j: h.tensor_tensor(out=oT[:, 8 + j, :], in0=accN[:], in1=accD[:], op=ALU.mult), reads=[baccN, baccD], writes=[boT])

    for j in range(4):
        qi = nxt("q", 2)
        qt, bqt = qbuf[qi], bqbuf[qi]
        P.add("sp", lambda h, qt=qt, j=j: h.dma_start(out=qt[:], in_=I["qc"][j]), writes=[bqt], dma=f"qb{qi}")
        items = []
        for qg in range(2):
            grp = {}
            for mt in range(2):
                it = {}

                def sA(it=it, qg=qg, mt=mt, qt=qt, bqt=bqt, j=j):
                    st, bst = B.psum("st", ST_BANKS)
                    P.add("pe", lambda h, st=st: h.matmul(st[:], kc_sb[:, j, mt * 128:(mt + 1) * 128], qt[:, qg * 512:(qg + 1) * 512],
                                                          start=True, stop=True), reads=[bkc, bqt], writes=[bst])
                    pi = nxt("p", NPT)
                    it["pi"] = pi
                    P.add("act", lambda h, st=st, pi=pi: h.activation(out=pt[pi][:], in_=st[:], func=AF.Exp), reads=[bst], writes=[bpt[pi]])

                def sB(it=it, qg=qg, mt=mt, grp=grp, j=j):
                    if mt == 0:
                        grp["num"] = B.psum("num", [0, 1])
                        grp["den"] = B.psum("den", [2, 3])
                    num, bnum = grp["num"]
                    den, bden = grp["den"]
                    pi = it["pi"]
                    P.add("pe", lambda h, num=num, pi=pi: h.matmul(num[:], vc_sb[:, mt, j * 128:(j + 1) * 128], pt[pi][:], start=(mt == 0), stop=(mt == 1)),
                          reads=[bvc, bpt[pi]], writes=[bnum])
                    P.add("pe", lambda h, den=den, pi=pi: h.matmul(den[:], B.ones_b[:], pt[pi][:], start=(mt == 0), stop=(mt == 1)),
                          reads=[B.b_const, bpt[pi]], writes=[bden])
                    if mt == 1:
                        rc, brc = B.next_tmpf()
                        P.add("dve", lambda h, rc=rc, den=den: h.reciprocal(out=rc[:], in_=den[:]), reads=[bden], writes=[brc])
                        P.add("dve", lambda h, rc=rc, num=num: h.tensor_tensor(out=oT[:, 12 + j, qg * 512:(qg + 1) * 512], in0=num[:], in1=rc[:], op=ALU.mult),
                              reads=[bnum, brc], writes=[boT])
                items.append((sA, sB))
        pipeline(items)
    P.barrier()
    es2.close()


def phase3(B, I):
    P = B.P
    nc = B.nc
    es3 = ExitStack()
    oT, boT = B.A32, B.bA32
    merged = es3.enter_context(nc.sbuf_tensor(B.uname("merged"), [128, KC, T], BF16))
    bmerged = P.buf("merged")
    gts = [es3.enter_context(nc.sbuf_tensor(B.uname(f"gts{i}"), [128, 3, T], BF16)) for i in range(2)]
    bgts = P.bufs(2, "gts")
    acc = [es3.enter_context(nc.sbuf_tensor(B.uname(f"macc{i}"), [128, T], F32)) for i in range(2)]
    bacc = P.bufs(2, "macc")
    gview = I["gate"].rearrange("(b c) p t -> c p b t", b=3)
    projs = ((I["w_proj_a"], 8, 0), (I["w_proj_b"], 4, 8), (I["w_proj_c"], 4, 12))
    for op2 in range(0, KC, 2):
        for bi, (W, kcb, c0) in enumerate(projs):
            def evac(ci, tg, ps, bps, bi=bi, op2=op2):
                oc = op2 + ci
                gi = oc % 2
                if bi == 0 and tg == 0:
                    P.add("sp", lambda h, gi=gi, oc=oc: h.dma_start(out=gts[gi][:], in_=gview[oc]), writes=[bgts[gi]], dma=f"gts{gi}")
                sl = slice(tg * 512, (tg + 1) * 512)
                if bi == 0:
                    P.add("dve", lambda h, ps=ps, gi=gi, sl=sl: h.tensor_tensor(out=acc[gi][:, sl], in0=ps[:], in1=gts[gi][:, 0, sl], op=ALU.mult),
                          reads=[bps, bgts[gi]], writes=[bacc[gi]])
                else:
                    tmp, btmp = B.next_tmpf()
                    P.add("dve", lambda h, ps=ps, gi=gi, sl=sl, tmp=tmp, bi=bi: h.tensor_tensor(out=tmp[:], in0=ps[:], in1=gts[gi][:, bi, sl], op=ALU.mult),
                          reads=[bps, bgts[gi]], writes=[btmp])
                    if bi == 1:
                        P.add("dve", lambda h, gi=gi, sl=sl, tmp=tmp: h.tensor_tensor(out=acc[gi][:, sl], in0=acc[gi][:, sl], in1=tmp[:], op=ALU.add),
                              reads=[btmp, bacc[gi]], writes=[bacc[gi]])
                    else:
                        P.add("dve", lambda h, gi=gi, sl=sl, tmp=tmp, oc=oc: h.tensor_tensor(out=merged[:, oc, sl], in0=acc[gi][:, sl], in1=tmp[:], op=ALU.add),
                              reads=[btmp, bacc[gi]], writes=[bmerged])
            B.dense_fm(W, 0, kcb, op2 * 128, 256, lambda k, tg, c0=c0: oT[:, c0 + k, tg * 512:(tg + 1) * 512], boT, [0, 1], evac)

    def evac_out(ci, tg, ps, bps):
        sl = slice(tg * 512, (tg + 1) * 512)
        P.add("dve", lambda h, ci=ci, sl=sl, ps=ps: h.tensor_tensor(out=B.xT[:, ci, sl], in0=ps[:], in1=B.xT[:, ci, sl], op=ALU.add),
              reads=[bps, B.bxT], writes=[B.bxT])

    B.dense_fm(I["w_out"], 0, KC, 0, D, lambda k, tg: merged[:, k, tg * 512:(tg + 1) * 512], bmerged, [0, 1], evac_out)
    P.barrier()
    es3.close()


def phase4(B, I):
    P = B.P
    nc = B.nc
    es4 = ExitStack()
    h2, bh2 = B.A32, B.bA32
    B.rmsnorm(B.xT, B.bxT, h2, bh2, I["gffn_sb"][0], I["gffn_sb"][1], T)
    NH = 22
    act = es4.enter_context(nc.sbuf_tensor(B.uname("actT"), [128, NH, T], BF16))
    bact = P.buf("act")
    sg = [es4.enter_context(nc.sbuf_tensor(B.uname(f"sg{i}"), [128, 2, T], BF16)) for i in range(2)]
    bsg = P.bufs(2, "sg")

    def rhs(k, tg):
        return h2[:, k, tg * 512:(tg + 1) * 512]

    for half in range(2):
        for w0 in range(0, NH * 128, 256):
            f0 = half * NH * 128 + w0
            si = (w0 // 256) % 2

            def evac_g(ci, tg, ps, bps, si=si):
                P.add("act", lambda h, ps=ps, ci=ci, tg=tg, si=si: h.activation(out=sg[si][:, ci, tg * 512:(tg + 1) * 512], in_=ps[:], func=AF.Silu),
                      reads=[bps], writes=[bsg[si]])

            def evac_u(ci, tg, ps, bps, si=si, w0=w0):
                fc = w0 // 128 + ci
                P.add("dve", lambda h, ps=ps, ci=ci, tg=tg, si=si, fc=fc: h.tensor_tensor(out=act[:, fc, tg * 512:(tg + 1) * 512], in0=ps[:],
                                                                                 in1=sg[si][:, ci, tg * 512:(tg + 1) * 512], op=ALU.mult),
                      reads=[bps, bsg[si]], writes=[bact])
            ncol = min(256, NH * 128 - w0)
            B.dense_fm(I["w_ffn_gate"], 0, KC, f0, ncol, rhs, bh2, [0, 1], evac_g)
            B.dense_fm(I["w_ffn_up"], 0, KC, f0, ncol, rhs, bh2, [0, 1], evac_u)

        def evac_d(ci, tg, ps, bps):
            sl = slice(tg * 512, (tg + 1) * 512)
            P.add("dve", lambda h, ci=ci, sl=sl, ps=ps: h.tensor_tensor(out=B.xT[:, ci, sl], in0=ps[:], in1=B.xT[:, ci, sl], op=ALU.add),
                  reads=[bps, B.bxT], writes=[B.bxT])
        B.dense_fm(I["w_ffn_down"], half * NH * 128, NH, 0, D, lambda k, tg: act[:, k, tg * 512:(tg + 1) * 512], bact, [0, 1], evac_d, nw=128)
    P.barrier()
    es4.close()


def store_x(B, out_d, final_gain=None):
    P = B.P
    evs = []
    dst = out_d.rearrange("(c p) t -> p c t", p=128)
    if final_gain is None:
        for c in range(0, KC, 4):
            evs.append(P.add("sp", lambda h, c=c: h.dma_start(out=dst[:, c:c + 4, :], in_=B.xT[:, c:c + 4, :]), reads=[B.bxT], writes=[P.buf()], dma="xo"))
        return evs
    es5 = ExitStack()
    fo = es5.enter_context(B.nc.sbuf_tensor(B.uname("final_o"), [128, KC, T], F32))
    bfo = P.buf("fo")
    B.rmsnorm(B.xT, B.bxT, fo, bfo, final_gain[0], final_gain[1], T)
    for c in range(0, KC, 4):
        evs.append(P.add("sp", lambda h, c=c: h.dma_start(out=dst[:, c:c + 4, :], in_=fo[:, c:c + 4, :]), reads=[bfo], writes=[P.buf()], dma="xo"))
    B._es5 = es5
    return evs


def load_x(B, src2d):
    src = src2d.rearrange("(c p) t -> p c t", p=128)
    for c in range(0, KC, 4):
        B.P.add("sp", lambda h, c=c: h.dma_start(out=B.xT[:, c:c + 4, :], in_=src[:, c:c + 4, :]), writes=[B.bxT], dma="x")


KV_PARTS = (("ka", 1024), ("va", 1024), ("kb1", 1024), ("kb2", 512), ("vb1", 1024), ("vb2", 512))


class _Idx:
    def __init__(self, fn):
        self.fn = fn

    def __getitem__(self, idx):
        if isinstance(idx, tuple):
            v = self.fn(idx[0])
            rest = idx[1:]
            return v[rest] if len(rest) > 1 else v[rest[0]]
        return self.fn(idx)


def kv_views(parts):
    vb1 = parts["vb1"].rearrange("(g t2) (two e) -> g (t2 two) e", g=2, two=2)
    vb2 = parts["vb2"].rearrange("t2 (two e) -> (t2 two) e", two=2)
    kb1 = parts["kb1"].rearrange("(h p) t -> h p t", p=128)
    kb2 = parts["kb2"].rearrange("(h p) t -> h p t", p=128)
    return {
        "ka": parts["ka"].rearrange("(h p) t -> h p t", p=128),
        "va": parts["va"],
        "kb": _Idx(lambda c: kb1[c] if c < 8 else kb2[c - 8]),
        "vb": _Idx(lambda g: vb1[g] if g < 2 else vb2),
    }


def build_fused8():
    B = Builder()
    P = B.P
    I0 = {}
    for n, shp in (("xT_in", [D, T]), ("memT", [D, 256]), ("w_in", [DEPTH, D, 8192]), ("w_gate", [DEPTH, D, 3 * D]),
                   ("w_mem_kv", [DEPTH, D, 1024]), ("w_proj_a", [DEPTH, 1024, D]), ("w_proj_b", [DEPTH, 512, D]),
                   ("w_proj_c", [DEPTH, 512, D]), ("w_out", [DEPTH, D, D]), ("w_ffn_gate", [DEPTH, D, DFF]),
                   ("w_ffn_up", [DEPTH, D, DFF]), ("w_ffn_down", [DEPTH, DFF, D]), ("mrow", [16, 3072]),
                   ("btab", [12, 128, 512]), ("dl", [DEPTH, 1, 256])):
        I0[n] = B.dram_in(n, shp)
    small = {}
    for n, shp in (("g_attn", [128, DEPTH * KC]), ("b_gate", [128, DEPTH * 48]), ("gffn", [128, DEPTH * KC]),
                   ("subln", [128, DEPTH]), ("gmem", [128, KC]), ("gfinal", [128, KC])):
        small[n] = B.dram_in(n, shp)
    ident_d = B.dram_in("ident", [128, 128])
    out_d = B.dram_out("out", [D, T])
    kv_own = [{n: B.nc.dram_tensor(f"kvo_{n}{i}", [r, 1024], BF16) for n, r in KV_PARTS} for i in range(2)]
    kv_all = [{n: B.nc.dram_tensor(f"kva_{n}{i}", [2 * r, 1024], BF16) for n, r in KV_PARTS} for i in range(2)]
    Sq = {"qa": B.dram_tmp("s_qa", [8, 128, T], BF16), "qb": B.dram_tmp("s_qb", [12, 128, T], BF16),
          "qc": B.dram_tmp("s_qc", [4, 128, T], BF16), "gate": B.dram_tmp("s_gate", [48, 128, T], BF16)}
    kbp, vbp = [], []
    for g, (_, dil) in enumerate(GROUPS):
        Lh = T // dil
        nrow = ((Lh + 128 + 127) // 128) * 128
        kbp.append(B.dram_tmp(f"s_kbp{g}", [4, 128, dil * (Lh + 128)], BF16))
        vbp.append(B.dram_tmp(f"s_vbp{g}", [dil, nrow, 512], BF16))

    B.xT = B.sb("xT_sb", [128, KC, T], F32)
    B.bxT = P.buf("xT")
    B.A32 = B.sb("A32", [128, KC, T], BF16)
    B.bA32 = P.buf("A32")
    B.eps_col = B.sb("eps_col", [128, 1], F32)
    P.add("dve", lambda h: h.memset(B.eps_col[:], EPS), writes=[B.b_const])
    sm = {n: load_small(B, n + "_sb", small[n], list(small[n].shape)) for n in small}
    identf = load_small(B, "ident_f", ident_d, [128, 128])
    finalize_small(B)
    P.add("dve", lambda h: h.tensor_copy(out=B.ident_b[:], in_=identf[0][:]), reads=[identf[1]], writes=[B.b_const])
    load_x(B, I0["xT_in"])

    for l in range(DEPTH):
        lam_init = 0.8 - 0.6 * math.exp(-0.3 * l)
        own = kv_views({n: kv_own[l % 2][n].ap() for n, _ in KV_PARTS})
        allv = [kv_views({n: kv_all[l % 2][n].ap()[r * rows:(r + 1) * rows, :] for n, rows in KV_PARTS}) for r in range(2)]
        outs = dict(Sq)
        outs.update(own)
        p1args = (B, B.xT, B.bxT, B.A32, B.bA32, sm["g_attn"][0][:, l * KC:(l + 1) * KC], sm["g_attn"][1],
                  I0["w_in"][l], I0["w_gate"][l], sm["b_gate"][0][:, l * 48:(l + 1) * 48], sm["b_gate"][1], outs)
        phase1(*p1args, part="main")
        P.wait_events("pool", [("dma", k, c) for k, c in P.dma_count.items()])
        for n, _ in KV_PARTS:
            P.add("pool", lambda h, l=l, n=n: h.collective_compute("AllGather", ALU.bypass, replica_groups=[[0, 1], [2, 3], [4, 5], [6, 7]],
                                                                   ins=[kv_own[l % 2][n].ap().opt()], outs=[kv_all[l % 2][n].ap().opt()]),
                  dma="cc", inc=1)
        phase1(*p1args, part="gates")
        P.barrier()
        for g, (_, dil) in enumerate(GROUPS):
            Lh = T // dil
            for j in range(4):
                dst = kbp[g][j].rearrange("p (r q) -> p r q", r=dil)

                def src(vw, g=g, j=j, dil=dil):
                    return vw["kb"][g * 4 + j].rearrange("p (r q) -> p r q", r=dil)
                P.add("sp", lambda h, dst=dst, a=src(own), Lh=Lh: h.dma_start(out=dst[:, :, 64:64 + Lh], in_=a), dma="pad")
                P.add("sp", lambda h, dst=dst, a=src(allv[0]), Lh=Lh: h.dma_start(out=dst[:, :, 0:64], in_=a[:, :, Lh - 64:Lh]), dma="pad")
                P.add("sp", lambda h, dst=dst, a=src(allv[1]), Lh=Lh: h.dma_start(out=dst[:, :, 64 + Lh:128 + Lh], in_=a[:, :, 0:64]), dma="pad")
            dstv = vbp[g]

            def srcv(vw, g=g, dil=dil):
                return vw["vb"][g].rearrange("(r q) e -> r q e", r=dil)
            P.add("sp", lambda h, dstv=dstv, a=srcv(own), Lh=Lh: h.dma_start(out=dstv[:, 64:64 + Lh, :], in_=a), dma="pad")
            P.add("sp", lambda h, dstv=dstv, a=srcv(allv[0]), Lh=Lh: h.dma_start(out=dstv[:, 0:64, :], in_=a[:, Lh - 64:Lh, :]), dma="pad")
            P.add("sp", lambda h, dstv=dstv, a=srcv(allv[1]), Lh=Lh: h.dma_start(out=dstv[:, 64 + Lh:128 + Lh, :], in_=a[:, 0:64, :]), dma="pad")
        P.barrier()

        I = {"qa": Sq["qa"], "qb": Sq["qb"], "qc": Sq["qc"], "gate": Sq["gate"], "ka_all": _Idx(lambda r, allv=allv: allv[r]["ka"]), "va_all": _Idx(lambda r, allv=allv: allv[r]["va"]),
             "memT": I0["memT"], "w_mem_kv": I0["w_mem_kv"][l], "mrow": I0["mrow"], "btab": I0["btab"], "dl": I0["dl"][l],
             "w_proj_a": I0["w_proj_a"][l], "w_proj_b": I0["w_proj_b"][l], "w_proj_c": I0["w_proj_c"][l], "w_out": I0["w_out"][l],
             "w_ffn_gate": I0["w_ffn_gate"][l], "w_ffn_up": I0["w_ffn_up"][l], "w_ffn_down": I0["w_ffn_down"][l],
             "gmem_sb": sm["gmem"], "subln_sb": (sm["subln"][0][:, l:l + 1], sm["subln"][1]),
             "gffn_sb": (sm["gffn"][0][:, l * KC:(l + 1) * KC], sm["gffn"][1])}
        for g in range(3):
            I[f"kbp{g}"] = kbp[g]
            I[f"vbp{g}"] = vbp[g]
        phase2(B, I, lam_init)
        phase3(B, I)
        phase4(B, I)
    evs = store_x(B, out_d, sm["gfinal"])
    return B.finish(evs)


def _rel_bucket_np(rel):
    rel = np.asarray(rel, dtype=np.int64)
    half_b, max_exact = 16, 8
    n = np.abs(rel)
    nf = np.maximum(n, 1).astype(np.float32)
    large = max_exact + (np.log(nf / np.float32(max_exact)) / np.float32(math.log(1024 / max_exact))
                         * np.float32(half_b - max_exact)).astype(np.int32)
    large = np.minimum(large, half_b - 1)
    return np.where(rel > 0, half_b, 0) + np.where(n < max_exact, n, large)


def _bias_layouts(rel_bias):
    rel_bias = np.asarray(rel_bias, dtype=np.float32)
    bk = _rel_bucket_np(np.arange(0, 4097) - 2048)
    Mf = rel_bias[bk][:, :16].T
    out = []
    p = np.arange(128)[:, None]
    jj = np.arange(128)[None, :]
    for s in range(2):
        B0 = 1025 - 1024 * s
        mrow = np.ascontiguousarray(Mf[:, B0:B0 + 3072])
        btab = np.zeros((12, 128, 4, 128), np.float32)
        for g, (_, dil) in enumerate(GROUPS):
            QT = min(128, (T // dil))
            for j in range(4):
                col = 16 + g * 4 + j
                sa = p - 64 - jj
                sb_ = p + 64 - jj
                va = np.abs(sa) <= 64
                vb = np.abs(sb_) <= 64
                ta = rel_bias[_rel_bucket_np(sa * dil), col]
                tb = rel_bias[_rel_bucket_np(sb_ * dil), col]
                vaf = va & ((p >= 64) if s == 0 else True)
                vbl = vb & ((p < QT - 64) if s == 1 else True)
                for a, (tv, vv) in enumerate(((ta, va), (ta, vaf), (tb, vb), (tb, vbl))):
                    btab[g * 4 + j, :, a, :] = np.where(vv, tv, np.float32(-30000.0))
        out.append((mrow, np.ascontiguousarray(btab.reshape(12, 128, 512))))
    return out


def _fm_vec(v, nchunk):
    return np.ascontiguousarray(np.asarray(v, np.float32).reshape(nchunk, 128).T)


_NC_CACHE = {}


def _get_nc(name):
    if name not in _NC_CACHE:
        _NC_CACHE[name] = {"fused8": build_fused8}[name]()
    return _NC_CACHE[name]


def kernel(x, mem, rel_bias, mem_norm, attn_norm, w_in, diff_lambda, diff_subln, w_mem_kv, w_gate, b_gate,
           w_proj_a, w_proj_b, w_proj_c, w_out, ffn_norm, w_ffn_gate, w_ffn_up, w_ffn_down, final_norm):
    f32 = np.float32
    x = np.asarray(x, f32)
    mem = np.asarray(mem, f32)
    cores = list(range(8))
    lay = _bias_layouts(rel_bias)
    A = lambda a: np.ascontiguousarray(np.asarray(a, f32))

    def fm_all(v, nchunk):
        return np.ascontiguousarray(np.concatenate([_fm_vec(v[l], nchunk) for l in range(DEPTH)], axis=1))

    common = {
        "w_in": A(w_in), "w_gate": A(w_gate), "w_mem_kv": A(w_mem_kv), "w_proj_a": A(w_proj_a), "w_proj_b": A(w_proj_b),
        "w_proj_c": A(w_proj_c), "w_out": A(w_out), "w_ffn_gate": A(w_ffn_gate), "w_ffn_up": A(w_ffn_up), "w_ffn_down": A(w_ffn_down),
        "dl": A(diff_lambda).reshape(DEPTH, 1, 256), "g_attn": fm_all(attn_norm, KC), "b_gate": fm_all(b_gate, 48),
        "gffn": fm_all(ffn_norm, KC), "subln": np.ascontiguousarray(A(diff_subln).T), "gmem": _fm_vec(mem_norm, KC),
        "gfinal": _fm_vec(final_norm, KC), "ident": np.eye(128, dtype=f32),
    }
    in_maps = []
    for c in cores:
        b, s_ = c // 2, c % 2
        d = dict(common)
        d["xT_in"] = np.ascontiguousarray(x[b, s_ * T:(s_ + 1) * T].T)
        d["memT"] = np.ascontiguousarray(mem[b].T)
        d["mrow"] = lay[s_][0]
        d["btab"] = lay[s_][1]
        in_maps.append(d)
    res = run_bass_kernel_spmd(_get_nc("fused8"), in_maps, core_ids=cores).results
    out = np.empty((4, SEQ, D), f32)
    for c in cores:
        b, s_ = c // 2, c % 2
        out[b, s_ * T:(s_ + 1) * T] = np.asarray(res[c]["out"], f32).T
    return out
```

```python
import math
from contextlib import ExitStack
import numpy as np
import ml_dtypes
import concourse.bass as bass
import concourse.mybir as mybir
from concourse.bass_utils import run_bass_kernel_spmd

F32 = mybir.dt.float32
BF16 = mybir.dt.bfloat16
AF = mybir.ActivationFunctionType
ALU = mybir.AluOpType
NPBF = ml_dtypes.bfloat16

D = 2048
T = 1024
SEQ = 2048
DEPTH = 4
DFF = 5632
EPS = 1e-5
KC = 16
GROUPS = ((128, 1), (512, 4), (2048, 16))
NEG = -1e30

ENGS = ("pe", "act", "dve", "pool", "sp")


class Buf:
    __slots__ = ("name", "w", "r")

    def __init__(self, name=""):
        self.name = name
        self.w = None
        self.r = []


class Plan:
    def __init__(self):
        self.ops = {e: [] for e in ENGS}
        self.seen = {e: {} for e in ENGS}
        self.dma_count = {}

    def buf(self, name=""):
        return Buf(name)

    def bufs(self, n, name=""):
        return [Buf(f"{name}{i}") for i in range(n)]

    def _need(self, eng, ev, waits):
        kind, k, v = ev
        if kind == "op":
            if k == eng and eng == "pe":
                return
            key = ("op", k)
        else:
            key = ("dma", k)
        s = self.seen[eng]
        if s.get(key, -1) >= v:
            return
        s[key] = v
        if kind == "op":
            self.ops[k][v]["inc"] = True
        waits.append(ev)

    def add(self, eng, fn, reads=(), writes=(), dma=None, inc=16):
        waits = []
        for b in reads:
            if b.w is not None:
                self._need(eng, b.w, waits)
        for b in writes:
            if b.w is not None:
                self._need(eng, b.w, waits)
            for ev in b.r:
                self._need(eng, ev, waits)
        idx = len(self.ops[eng])
        op = {"fn": fn, "waits": waits, "inc": False, "dma": None}
        if dma is not None:
            c = self.dma_count.get(dma, 0) + inc
            self.dma_count[dma] = c
            op["dma"] = dma
            op["dinc"] = inc
            ev = ("dma", dma, c)
        else:
            ev = ("op", eng, idx)
        self.ops[eng].append(op)
        for b in reads:
            b.r = [e for e in b.r if not (e[0] == ev[0] and e[1] == ev[1])]
            b.r.append(ev)
        for b in writes:
            b.w = ev
            b.r = []
        return ev

    def wait_events(self, eng, evs):
        waits = []
        for ev in evs:
            self._need(eng, ev, waits)
        self.ops[eng].append({"fn": None, "waits": waits, "inc": False, "dma": None})

    def barrier(self):
        evs = []
        for e in ENGS:
            if e == "sp":
                continue
            if self.ops[e]:
                for i in range(len(self.ops[e]) - 1, -1, -1):
                    if self.ops[e][i]["fn"] is not None and self.ops[e][i]["dma"] is None:
                        evs.append(("op", e, i))
                        break
        for k, c in self.dma_count.items():
            evs.append(("dma", k, c))
        for e in ENGS:
            if e != "pool":
                self.wait_events(e, evs)

    def emit(self, nc, es):
        esem = {e: es.enter_context(nc.semaphore(f"s_{e}")) for e in ENGS}
        dsem = {k: es.enter_context(nc.semaphore(f"d_{i}")) for i, k in enumerate(self.dma_count)}
        val = {}
        for e in ENGS:
            c = 0
            v = []
            for op in self.ops[e]:
                if op["inc"]:
                    c += 1
                v.append(c)
            val[e] = v
        self.n_sems = len(esem) + len(dsem)

        def body(e):
            def run(h):
                for op in self.ops[e]:
                    for (kind, k, v) in op["waits"]:
                        if kind == "op":
                            h.wait_ge(esem[k], val[k][v])
                        else:
                            h.wait_ge(dsem[k], v)
                    if op["fn"] is None:
                        continue
                    ins = op["fn"](h)
                    if op["dma"] is not None:
                        if op["dinc"] == 1:
                            ins.then_inc(dsem[op["dma"]])
                        else:
                            ins.then_inc(dsem[op["dma"]], op["dinc"])
                    elif op["inc"]:
                        ins.then_inc(esem[e], 1)
            return run

        with nc.Block() as block:
            block.tensor(body("pe"))
            block.scalar(body("act"))
            block.vector(body("dve"))
            block.gpsimd(body("pool"))
            block.sync(body("sp"))


class Builder:
    def __init__(self):
        self.nc = bass.Bass("TRN2", target_bir_lowering=False)
        self.P = Plan()
        self.es = ExitStack()
        self.n_dma = 0
        nc = self.nc
        self.ps = [self.es.enter_context(nc.psum_tensor(f"ps{i}", [128, 512], F32)) for i in range(8)]
        self.bps = self.P.bufs(8, "ps")
        self.ps_rr = {}
        self.NW = 256
        self.wslots = [self.sb(f"wslot{i}", [128, 16 * 256], BF16) for i in range(3)]
        self.bw = self.P.bufs(3, "w")
        self.w_i = 0
        self.ones_f = self.sb("ones_f", [128, 128], F32)
        self.ones_b = self.sb("ones_b", [128, 128], BF16)
        self.b_const = self.P.buf("const")
        self.P.add("dve", lambda h: h.memset(self.ones_f[:], 1.0), writes=[self.b_const])
        self.P.add("dve", lambda h: h.memset(self.ones_b[:], 1.0), writes=[self.b_const])
        self.ident_b = self.sb("ident_b", [128, 128], BF16)
        self.stg_i = 0
        self.stg = [self.sb(f"stg{i}", [128, T], BF16) for i in range(3)]
        self.bstg = self.P.bufs(3, "stg")
        self.tmpf = [self.sb(f"tmpf{i}", [128, 512], F32) for i in range(4)]
        self.btmpf = self.P.bufs(4, "tmpf")
        self.tmpf_i = 0

    def sb(self, name, shape, dt):
        return self.es.enter_context(self.nc.sbuf_tensor(name, shape, dt))

    def uname(self, name):
        self._uid = getattr(self, "_uid", 0) + 1
        return f"{name}_u{self._uid}"

    def dram_in(self, name, shape, dt=F32):
        return self.nc.dram_tensor(name, list(shape), dt, kind="ExternalInput").ap()

    def dram_out(self, name, shape, dt=F32):
        return self.nc.dram_tensor(name, list(shape), dt, kind="ExternalOutput").ap()

    def dram_tmp(self, name, shape, dt=F32):
        return self.nc.dram_tensor(name, list(shape), dt, kind="Internal").ap()

    def dkey(self, name):
        self.n_dma += 1
        return f"{name}_{self.n_dma}"

    def psum(self, pool, banks):
        i = self.ps_rr.get(pool, 0)
        self.ps_rr[pool] = i + 1
        b = banks[i % len(banks)]
        return self.ps[b], self.bps[b]

    def next_tmpf(self):
        i = self.tmpf_i % 4
        self.tmpf_i += 1
        return self.tmpf[i], self.btmpf[i]

    def next_stg(self):
        i = self.stg_i % 3
        self.stg_i += 1
        return self.stg[i], self.bstg[i], f"stg{i}"

    def wload(self, W, k0, kc, c0, ncol):
        i = self.w_i % 3
        self.w_i += 1
        slot = self.wslots[i]
        view = slot[:, 0:kc * ncol].rearrange("p (c n) -> p c n", n=ncol)
        src = W[k0:k0 + kc * 128, c0:c0 + ncol].rearrange("(c p) n -> p c n", p=128)
        self.P.add("pool", lambda h: h.dma_start(out=view, in_=src), writes=[self.bw[i]], dma=f"w{i}")
        return view, self.bw[i]

    def rmsnorm(self, src, bsrc, dst, bdst, gain, bgain, ntok, nfeat_chunks=KC, dfeat=D):
        P = self.P
        for t0 in range(0, ntok, 512):
            n = min(512, ntok - t0)
            pst, bpst = self.psum("misc", [7])
            for c in range(nfeat_chunks):
                sq, bsq = self.next_tmpf()
                P.add("act", lambda h, sq=sq, c=c, t0=t0, n=n: h.activation(out=sq[:, 0:n], in_=src[:, c, t0:t0 + n], func=AF.Square),
                      reads=[bsrc], writes=[bsq])
                P.add("pe", lambda h, sq=sq, c=c, pst=pst, n=n: h.matmul(pst[:, 0:n], self.ones_f[:], sq[:, 0:n],
                                                                     start=(c == 0), stop=(c == nfeat_chunks - 1)),
                      reads=[bsq, self.b_const], writes=[bpst])
            rs, brs = self.next_tmpf()
            P.add("act", lambda h, rs=rs, pst=pst, n=n: h.activation(out=rs[:, 0:n], in_=pst[:, 0:n], func=AF.Sqrt,
                                                                 bias=self.eps_col[:, 0:1], scale=1.0 / dfeat),
                  reads=[bpst, self.b_const], writes=[brs])
            P.add("dve", lambda h, rs=rs, n=n: h.reciprocal(out=rs[:, 0:n], in_=rs[:, 0:n]), reads=[brs], writes=[brs])
            for c in range(nfeat_chunks):
                P.add("dve", lambda h, rs=rs, c=c, t0=t0, n=n: h.scalar_tensor_tensor(out=dst[:, c, t0:t0 + n], in0=src[:, c, t0:t0 + n],
                                                                          scalar=gain[:, c:c + 1], in1=rs[:, 0:n],
                                                                          op0=ALU.mult, op1=ALU.mult),
                      reads=[bsrc, brs, bgain], writes=[bdst])

    def dense_fm(self, W, k0, kc, c0, ncols, rhs_fn, brhs, tgs, evac, nw=None):
        nw = nw or self.NW
        P = self.P
        for w0 in range(0, ncols, nw):
            wn = min(nw, ncols - w0)
            wt, bwt = self.wload(W, k0, kc, c0 + w0, wn)
            for oc in range(wn // 128):
                for tg in tgs:
                    ps, bps = self.psum("dense", [0, 1, 2, 3])
                    r0 = rhs_fn(0, tg)
                    n = 1
                    for s_ in r0.shape[1:]:
                        n *= s_
                    for k in range(kc):
                        P.add("pe", lambda h, wt=wt, k=k, oc=oc, tg=tg, ps=ps, n=n: h.matmul(
                            ps[:, 0:n], wt[:, k, oc * 128:(oc + 1) * 128], rhs_fn(k, tg), start=(k == 0), stop=(k == kc - 1)),
                            reads=[bwt, brhs], writes=[bps])
                    evac((w0 // 128) + oc, tg, ps, bps)

    def dense_tm(self, W, k0, kc, c0, ncols, lhs_fn, blhs, ntt, evac):
        P = self.P
        nw = self.NW
        for w0 in range(0, ncols, nw):
            wt, bwt = self.wload(W, k0, kc, c0 + w0, nw)
            for tt in range(ntt):
                ps, bps = self.psum("dense", [0, 1, 2, 3])
                nparts = len(lhs_fn(0, tt))
                for part in range(nparts):
                    for k in range(kc):
                        def mm(h, wt=wt, k=k, tt=tt, ps=ps, part=part):
                            ap, r0, nr = lhs_fn(k, tt)[part]
                            return h.matmul(ps[r0:r0 + nr, 0:nw], ap, wt[:, k, :], start=(k == 0), stop=(k == kc - 1))
                        P.add("pe", mm, reads=[bwt, blhs], writes=[bps])
                evac(w0, tt, ps, bps)

    def finish(self, out_events):
        self.P.wait_events("sp", out_events)
        self.P.emit(self.nc, self.es)
        if getattr(self, "_es5", None) is not None:
            self._es5.close()
        self.es.close()
        return self.nc


def class_view(ap2d, dil):
    if dil == 1:
        return ap2d.rearrange("p (r pos) -> p r pos", r=1)
    return ap2d.rearrange("p (pos r) -> p r pos", r=dil)


def phase1(B, xT, bxT, hT, bhT, g_attn, bg, w_in, w_gate, bgate_sb, bbg, outs, part="all"):
    P = B.P
    if part != "gates":
        B.rmsnorm(xT, bxT, hT, bhT, g_attn, bg, T)
    out_evs = []

    def nat_rhs(k, tg):
        return hT[:, k, tg * 512:(tg + 1) * 512]

    def cls_rhs(dil):
        def f(k, tg):
            v = class_view(hT[:, k, :], dil)
            nr = (512 * dil) // T
            if dil == 1:
                return hT[:, k, tg * 512:(tg + 1) * 512]
            return v[:, tg * nr:(tg + 1) * nr, :]
        return f

    cur = {}

    def make_evac(dst, scale=None, sigmoid=False, chunk_off=0, eng_alt=("dve", "act"), dil=1, rev=False):
        cnt = [0]

        def evac(ci, tg, ps, bps):
            c = ci + chunk_off
            if tg == 0:
                cur["s"] = B.next_stg()
            stg, bstg, skey = cur["s"]
            o = stg[:, tg * 512:(tg + 1) * 512]
            psv = ps[:]
            if rev:
                psv = bass.AP(tensor=ps[:].tensor, offset=ps[:, 511:512].offset, ap=[list(ps[:].ap[0]), [-1, 512]])
            if dil > 1:
                npos = 512 // dil
                o = stg[:].rearrange("p (r pos) -> p r pos", r=dil)[:, :, tg * npos:(tg + 1) * npos]
                psv = class_view(ps[:], dil)
            if sigmoid:
                P.add("act", lambda h, o=o, ps=ps, c=c: h.activation(out=o, in_=ps[:], func=AF.Sigmoid,
                                                                    bias=bgate_sb[:, c:c + 1], scale=1.0),
                      reads=[bps, bbg], writes=[bstg])
            else:
                e = eng_alt[cnt[0] % 2]
                cnt[0] += 1
                if e == "act":
                    P.add("act", lambda h, o=o, psv=psv: h.activation(out=o, in_=psv, func=AF.Copy,
                                                                      scale=(scale if scale is not None else 1.0)),
                          reads=[bps], writes=[bstg])
                elif scale is not None:
                    P.add("dve", lambda h, o=o, psv=psv: h.tensor_scalar_mul(out=o, in0=psv, scalar1=scale),
                          reads=[bps], writes=[bstg])
                else:
                    P.add("dve", lambda h, o=o, psv=psv: h.tensor_copy(out=o, in_=psv), reads=[bps], writes=[bstg])
            if tg == 1:
                ev = P.add("sp", lambda h, stg=stg, c=c: h.dma_start(out=dst[c], in_=stg[:]), reads=[bstg],
                           writes=[B.P.buf()], dma=skey)
                out_evs.append(ev)
        return evac

    sB = 128 ** -0.5
    if part == "gates":
        B.dense_fm(w_gate, 0, KC, 0, 3 * D, nat_rhs, bhT, [0, 1], make_evac(outs["gate"], sigmoid=True))
        return out_evs
    B.dense_fm(w_in, 0, KC, 0, 1024, nat_rhs, bhT, [0, 1], make_evac(outs["qa"], scale=0.125, rev=True))
    B.dense_fm(w_in, 0, KC, 1024, 1024, nat_rhs, bhT, [0, 1], make_evac(outs["ka"]))
    for g, (_, dil) in enumerate(GROUPS):
        B.dense_fm(w_in, 0, KC, 3072 + g * 512, 512, nat_rhs, bhT, [0, 1],
                   make_evac(outs["qb"], scale=sB, chunk_off=g * 4, dil=dil))
        B.dense_fm(w_in, 0, KC, 4608 + g * 512, 512, nat_rhs, bhT, [0, 1],
                   make_evac(outs["kb"], chunk_off=g * 4, dil=dil))
    B.dense_fm(w_in, 0, KC, 7680, 512, nat_rhs, bhT, [0, 1], make_evac(outs["qc"], scale=sB))

    def make_evac_tm(dst2d):
        cnt = [0]

        def evac(w0, tt, ps, bps):
            stg, bstg, skey = B.next_stg()
            e = ("dve", "act")[cnt[0] % 2]
            cnt[0] += 1
            if e == "dve":
                P.add("dve", lambda h, stg=stg, ps=ps: h.tensor_copy(out=stg[:, 0:256], in_=ps[:, 0:256]), reads=[bps], writes=[bstg])
            else:
                P.add("act", lambda h, stg=stg, ps=ps: h.activation(out=stg[:, 0:256], in_=ps[:, 0:256], func=AF.Copy),
                      reads=[bps], writes=[bstg])
            ev = P.add("sp", lambda h, stg=stg, tt=tt, w0=w0: h.dma_start(out=dst2d[tt * 128:(tt + 1) * 128, w0:w0 + 256],
                                                                      in_=stg[:, 0:256]),
                       reads=[bstg], writes=[B.P.buf()], dma=skey)
            out_evs.append(ev)
        return evac

    def nat_lhs(k, tt):
        return [(hT[:, k, tt * 128:(tt + 1) * 128], 0, 128)]

    def cls_lhs(dil):
        def f(k, tt):
            if dil == 1:
                return [(hT[:, k, tt * 128:(tt + 1) * 128], 0, 128)]
            v = class_view(hT[:, k, :], dil)
            Lh = T // dil
            if Lh >= 128:
                r = (tt * 128) // Lh
                p0 = (tt * 128) % Lh
                return [(v[:, r, p0:p0 + 128], 0, 128)]
            nr = 128 // Lh
            return [(v[:, tt * nr + i, :], i * Lh, Lh) for i in range(nr)]
        return f

    B.dense_tm(w_in, 0, KC, 2048, 1024, nat_lhs, bhT, 8, make_evac_tm(outs["va"]))
    for g, (_, dil) in enumerate(GROUPS):
        B.dense_tm(w_in, 0, KC, 6144 + g * 512, 512, cls_lhs(dil), bhT, 8, make_evac_tm(outs["vb"][g]))
    if part == "all":
        B.dense_fm(w_gate, 0, KC, 0, 3 * D, nat_rhs, bhT, [0, 1], make_evac(outs["gate"], sigmoid=True))
    return out_evs


def load_small(B, name, dram_ap, shape, dt=F32):
    t = B.sb(name, shape, dt)
    b = B.P.buf(name)
    B.P.add("sp", lambda h: h.dma_start(out=t[:], in_=dram_ap), writes=[b], dma="small")
    B.small_bufs = getattr(B, "small_bufs", []) + [b]
    return t, b


def finalize_small(B):
    tot = B.P.dma_count.get("small", 0)
    for b in getattr(B, "small_bufs", []):
        b.w = ("dma", "small", tot)


def mem_kv(B, I):
    P = B.P
    nc = B.nc
    esm = ExitStack()
    memn = esm.enter_context(nc.sbuf_tensor(B.uname("memn"), [128, KC, 256], BF16))
    bmemn = P.buf("memn")
    memf = esm.enter_context(nc.sbuf_tensor(B.uname("memf"), [128, KC, 256], F32))
    bmemf = P.buf("memf")
    msrc = I["memT"].rearrange("(c p) t -> p c t", p=128)
    P.add("sp", lambda h: h.dma_start(out=memf[:], in_=msrc), writes=[bmemf], dma="memf")
    B.rmsnorm(memf, bmemf, memn, bmemn, I["gmem_sb"][0], I["gmem_sb"][1], 256)
    kc_sb, bkc, vc_sb, bvc = B.kc_sb, B.bkc, B.vc_sb, B.bvc

    def evac_kc(ci, tg, ps, bps):
        P.add("dve", lambda h, ci=ci, ps=ps: h.tensor_copy(out=kc_sb[:, ci, :], in_=ps[:, 0:256]), reads=[bps], writes=[bkc])

    B.dense_fm(I["w_mem_kv"], 0, KC, 0, 512, lambda k, tg: memn[:, k, :], bmemn, [0], evac_kc)

    def evac_vc(w0, tt, ps, bps):
        P.add("dve", lambda h, w0=w0, tt=tt, ps=ps: h.tensor_copy(out=vc_sb[:, tt, w0:w0 + 256], in_=ps[:, 0:256]), reads=[bps], writes=[bvc])

    B.dense_tm(I["w_mem_kv"], 0, KC, 512, 512, lambda k, tt: [(memn[:, k, tt * 128:(tt + 1) * 128], 0, 128)], bmemn, 2, evac_vc)
    return esm


W_G = 2944


def phase2(B, I, lam_init):
    P = B.P
    nc = B.nc
    es2 = ExitStack()

    def sb(name, shape, dt):
        return es2.enter_context(nc.sbuf_tensor(B.uname("s2_" + name), shape, dt))

    kc_sb, bkc, vc_sb, bvc = B.kc_sb, B.bkc, B.vc_sb, B.bvc
    oT, boT = B.A32, B.bA32
    kbuf = [sb(f"kbuf{i}", [128, 3072], BF16) for i in range(2)]
    bkbuf = P.bufs(2, "kbuf")
    vbuf = [sb("vbuf0", [128, 32, 128], BF16)]
    bvbuf = P.bufs(2, "vbuf")
    gbuf = [sb(f"gbuf{i}", [128, W_G], BF16) for i in range(2)]
    bgbuf = P.bufs(2, "gbuf")
    tbuf = [sb(f"tbuf{i}", [128, 512], F32) for i in range(2)]
    btbuf = P.bufs(2, "tbuf")
    cnt_t = [0]
    NPT = 4
    pt = [sb(f"pt{i}", [128, 512], BF16) for i in range(NPT)]
    bpt = P.bufs(NPT, "pt")
    t01 = [sb(f"t01_{i}", [128, T], F32) for i in range(2)]
    bt01 = P.bufs(2, "t01")
    obuf = sb("obuf", [128, T], F32)
    bobuf = P.buf("obuf")
    cnt = {"k": 0, "q": 0, "g": 0, "p": 0}
    qpad = [[sb(f"qpad{i}_{m}", [128, T], BF16) for m in range(2)] for i in range(2)]
    bqpad = [P.buf(f"qpad{i}") for i in range(2)]
    for i in range(2):
        P.add("dve", lambda h, i=i: h.memset(qpad[i][0][64:128, :], 0.0), writes=[bqpad[i]])
        P.add("dve", lambda h, i=i: h.memset(qpad[i][1][0:64, :], 0.0), writes=[bqpad[i]])
    qbuf = [qpad[0][0], qpad[1][0]]
    bqbuf = bqpad
    dacc = [sb(f"dacc{i}", [128, 512], F32) for i in range(2)]
    bdacc = P.bufs(2, "dacc")
    deferred = []

    def run_deferred(flush=False):
        for d in list(deferred):
            d[0] -= 1
            if flush or d[0] <= 0:
                deferred.remove(d)
                d[1]()
    cnt_d = [0]

    def nxt(name, n):
        i = cnt[name] % n
        cnt[name] += 1
        return i

    dl = sb("dl", [128, 256], F32)
    bdl = P.buf("dl")
    dl_src = bass.AP(tensor=I["dl"].tensor, offset=I["dl"].offset, ap=[[0, 128], [1, 256]])
    P.add("sp", lambda h: h.dma_start(out=dl[:], in_=dl_src), writes=[bdl], dma="dl")
    lam = sb("lam", [128, 8], F32)
    blam = P.buf("lam")
    prod = sb("prod", [128, 128], F32)
    bprod = P.buf("prod")
    dv = dl[:].rearrange("p (a d) -> p a d", d=64)
    P.add("dve", lambda h: h.tensor_tensor(out=prod[:].rearrange("p (a d) -> p a d", d=64), in0=dv[:, 0:4:2, :], in1=dv[:, 1:4:2, :],
                                           op=ALU.mult), reads=[bdl], writes=[bprod])
    P.add("dve", lambda h: h.reduce_sum(out=lam[:, 0:2], in_=prod[:].rearrange("p (a d) -> p a d", d=64), axis=mybir.AxisListType.X),
          reads=[bprod], writes=[blam])
    P.add("act", lambda h: h.activation(out=lam[:, 0:2], in_=lam[:, 0:2], func=AF.Exp), reads=[blam], writes=[blam])
    P.add("dve", lambda h: h.tensor_tensor(out=lam[:, 2:3], in0=lam[:, 1:2], in1=lam[:, 0:1], op=ALU.subtract), reads=[blam], writes=[blam])
    if lam_init is None:
        lc, blc = I["lconst_sb"]
        P.add("dve", lambda h: h.tensor_tensor(out=lam[:, 2:3], in0=lam[:, 2:3], in1=lc[:, 0:1], op=ALU.subtract),
              reads=[blam, blc], writes=[blam])
        P.add("dve", lambda h: h.tensor_tensor(out=lam[:, 3:4], in0=I["subln_sb"][0][:, 0:1], in1=lc[:, 1:2], op=ALU.mult),
              reads=[blam, blc, I["subln_sb"][1]], writes=[blam])
    else:
        P.add("dve", lambda h: h.tensor_scalar_add(out=lam[:, 2:3], in0=lam[:, 2:3], scalar1=-lam_init), reads=[blam], writes=[blam])
        P.add("dve", lambda h: h.tensor_scalar_mul(out=lam[:, 3:4], in0=I["subln_sb"][0][:, 0:1], scalar1=1.0 - lam_init),
              reads=[blam, I["subln_sb"][1]], writes=[blam])

    def pipeline(items, LA=3):
        n = len(items)
        for st_ in range(n + LA):
            if st_ < n:
                items[st_][0]()
            if st_ - LA >= 0:
                items[st_ - LA][1]()

    ST_BANKS = [4, 5, 6, 7]

    for hh in range(8):
        ki = nxt("k", 2)
        qi = nxt("q", 2)
        kt, bkt = kbuf[ki], bkbuf[ki]
        qp, bqt = qpad[qi], bqpad[qi]
        hf = hh % 2
        vt, bvt = vbuf[0][:, 16 * hf:16 * hf + 16, :], bvbuf[hf]
        for r in range(2):
            P.add("sp", lambda h, kt=kt, r=r, hh=hh: h.dma_start(out=kt[:, r * T:(r + 1) * T], in_=I["ka_all"][r, hh]),
                  writes=[bkt], dma=f"kb{ki}")
            vsrc = I["va_all"][r, :, hh * 128:(hh + 1) * 128].rearrange("(t p) e -> p t e", p=128)
            P.add("sp", lambda h, vt=vt, r=r, vsrc=vsrc: h.dma_start(out=vt[:, r * 8:(r + 1) * 8, :], in_=vsrc),
                  writes=[bvt], dma=f"vb{hf}")
        for m in range(2):
            P.add("sp", lambda h, qp=qp, hh=hh, m=m: h.dma_start(out=qp[m][m * 64:(m + 1) * 64, :], in_=I["qa"][hh][m * 64:(m + 1) * 64, :]),
                  writes=[bqt], dma=f"qp{qi}")
        items = []
        for m in range(2):
            gi = nxt("g", 2)
            gt, bgt = gbuf[gi], bgbuf[gi]
            hm = m * 8 + hh
            gsrc = bass.AP(tensor=I["mrow"].tensor, offset=I["mrow"][hm, 0:1].offset, ap=[[1, 128], [1, W_G]])
            P.add("pool", lambda h, gt=gt, gsrc=gsrc: h.dma_start(out=gt[:], in_=gsrc), writes=[bgt], dma=f"gb{gi}")
            for qg in range(2):
                grp = {}
                for k2 in range(0, 16, 2):
                    it = {}

                    def sA(m=m, qg=qg, k2=k2, it=it, gt=gt, bgt=bgt, kt=kt, qp=qp, bkt=bkt, bqt=bqt):
                        sts = [B.psum("st", ST_BANKS) for _ in range(2)]
                        for u in range(2):
                            kk = k2 + u
                            st, bst = sts[u]
                            P.add("pe", lambda h, st=st, kk=kk: h.matmul(st[:], kt[:, kk * 128:(kk + 1) * 128], qp[m][:, qg * 512:(qg + 1) * 512],
                                                                         start=True, stop=False), reads=[bkt, bqt], writes=[bst])
                        for u in range(2):
                            kk = k2 + u
                            st, bst = sts[u]
                            c0 = kk * 128 - 512 * qg + 512
                            P.add("pe", lambda h, st=st, c0=c0: h.matmul(st[:], B.ident_b[:], gt[:, c0:c0 + 512], start=False, stop=True),
                                  reads=[bgt, B.b_const], writes=[bst])
                        it["pi"] = []
                        for u in range(2):
                            st, bst = sts[u]
                            pi = nxt("p", NPT)
                            it["pi"].append(pi)
                            P.add("act", lambda h, st=st, pi=pi: h.activation(out=pt[pi][:], in_=st[:], func=AF.Exp), reads=[bst], writes=[bpt[pi]])

                    def sB(m=m, qg=qg, k2=k2, it=it, grp=grp, vt=vt, bvt=bvt):
                        if k2 == 0:
                            grp["num"] = B.psum("num", [0, 1])
                            grp["den"] = B.psum("den", [2, 3])
                            di = cnt_d[0] % 2
                            cnt_d[0] += 1
                            grp["da"] = (dacc[di], bdacc[di])
                        num, bnum = grp["num"]
                        den, bden = grp["den"]
                        da, bda = grp["da"]
                        for u in range(2):
                            kk = k2 + u
                            pi = it["pi"][u]
                            P.add("pe", lambda h, num=num, pi=pi, kk=kk: h.matmul(num[:], vt[:, kk, :], pt[pi][:], start=(kk == 0), stop=(kk == 15)),
                                  reads=[bvt, bpt[pi]], writes=[bnum])
                        pi1 = it["pi"][1]
                        P.add("pe", lambda h, den=den, pi1=pi1: h.matmul(den[:], B.ones_b[:], pt[pi1][:], start=(k2 == 0), stop=False),
                              reads=[B.b_const, bpt[pi1]], writes=[bden])
                        pi0 = it["pi"][0]
                        if k2 == 0:
                            P.add("dve", lambda h, da=da, pi0=pi0: h.tensor_copy(out=da[:], in_=pt[pi0][:]), reads=[bpt[pi0]], writes=[bda])
                        else:
                            P.add("dve", lambda h, da=da, pi0=pi0: h.tensor_tensor(out=da[:], in0=pt[pi0][:], in1=da[:], op=ALU.add),
                                  reads=[bpt[pi0], bda], writes=[bda])
                        run_deferred()
                        if k2 == 14:
                            def epi(num=num, bnum=bnum, den=den, bden=bden, da=da, bda=bda, m=m, qg=qg):
                                P.add("pe", lambda h, den=den, da=da: h.matmul(den[:], B.ones_f[:], da[:], start=False, stop=True),
                                      reads=[B.b_const, bda], writes=[bden])
                                rc, brc = B.next_tmpf()
                                P.add("dve", lambda h, rc=rc, den=den: h.reciprocal(out=rc[:], in_=den[:]), reads=[bden], writes=[brc])
                                numr = bass.AP(tensor=num[:].tensor, offset=num[:, 511:512].offset, ap=[list(num[:].ap[0]), [-1, 512]])
                                rcr = bass.AP(tensor=rc[:].tensor, offset=rc[:, 511:512].offset, ap=[list(rc[:].ap[0]), [-1, 512]])
                                P.add("dve", lambda h, rcr=rcr, numr=numr: h.tensor_tensor(out=t01[m][:, qg * 512:(qg + 1) * 512], in0=numr, in1=rcr,
                                                                                         op=ALU.mult), reads=[bnum, brc], writes=[bt01[m]])
                            deferred.append([3, epi])
                    items.append((sA, sB))
        pipeline(items, LA=1)

        def head_epi(hh=hh):
            P.add("dve", lambda h: h.scalar_tensor_tensor(out=obuf[:], in0=t01[1][:], scalar=lam[:, 2:3], in1=t01[0][:], op0=ALU.mult, op1=ALU.add),
                  reads=[bt01[0], bt01[1], blam], writes=[bobuf])
            for qg in range(2):
                sq, bsq = B.next_tmpf()
                P.add("act", lambda h, sq=sq, qg=qg: h.activation(out=sq[:], in_=obuf[:, qg * 512:(qg + 1) * 512], func=AF.Square),
                      reads=[bobuf], writes=[bsq])
                pss, bpss = B.psum("st", ST_BANKS)
                P.add("pe", lambda h, pss=pss, sq=sq: h.matmul(pss[:], B.ones_f[:], sq[:], start=True, stop=True), reads=[bsq, B.b_const], writes=[bpss])
                rs, brs = B.next_tmpf()
                P.add("act", lambda h, rs=rs, pss=pss: h.activation(out=rs[:], in_=pss[:], func=AF.Ln, bias=B.eps_col[:, 0:1], scale=1.0 / 128),
                      reads=[bpss, B.b_const], writes=[brs])
                P.add("act", lambda h, rs=rs: h.activation(out=rs[:], in_=rs[:], func=AF.Exp, scale=-0.5), reads=[brs], writes=[brs])
                P.add("dve", lambda h, rs=rs, qg=qg, hh=hh: h.scalar_tensor_tensor(out=oT[:, hh, qg * 512:(qg + 1) * 512], in0=obuf[:, qg * 512:(qg + 1) * 512],
                                                                             scalar=lam[:, 3:4], in1=rs[:], op0=ALU.mult, op1=ALU.mult),
                      reads=[bobuf, brs, blam], writes=[boT])
        deferred.append([5, head_epi])
    run_deferred(flush=True)

    accN, baccN = t01[0], bt01[0]
    accD, baccD = t01[1], bt01[1]
    for j in range(4):
        for g, (_, dil) in enumerate(GROUPS):
            Lh = T // dil
            QT = min(128, Lh)
            ntq = Lh // QT
            LP = Lh + 128
            ntile = (LP + 127) // 128
            ki = nxt("k", 2)
            qi = nxt("q", 2)
            gi = cnt_t[0] % 2
            cnt_t[0] += 1
            kt, bkt = kbuf[ki], bkbuf[ki]
            qt, bqt = qbuf[qi], bqbuf[qi]
            vt = vbuf[0]
            tb, btb = tbuf[gi], btbuf[gi]
            P.add("sp", lambda h, kt=kt, g=g, j=j, dil=dil, LP=LP: h.dma_start(out=kt[:, 0:dil * LP], in_=I[f"kbp{g}"][j]), writes=[bkt], dma=f"kb{ki}")
            vsrc = I[f"vbp{g}"][:, :, j * 128:(j + 1) * 128].rearrange("r (t p) e -> p (r t) e", p=128)
            nvt = dil * ntile
            if g == 2:
                vbase = 0
                for hf in range(2):
                    P.add("sp", lambda h, vt=vt, vsrc=vsrc, hf=hf: h.dma_start(out=vt[:, 16 * hf:16 * hf + 16, :], in_=vsrc[:, 16 * hf:16 * hf + 16, :]),
                          writes=[bvbuf[hf]], dma=f"vb{hf}")
            else:
                vbase = 16 * g
                P.add("sp", lambda h, vt=vt, vsrc=vsrc, vbase=vbase, nvt=nvt: h.dma_start(out=vt[:, vbase:vbase + nvt, :], in_=vsrc),
                      writes=[bvbuf[g]], dma=f"vb{g}")
            P.add("sp", lambda h, qt=qt, g=g, j=j: h.dma_start(out=qt[:], in_=I["qb"][g * 4 + j]), writes=[bqt], dma=f"qb{qi}")
            P.add("sp", lambda h, tb=tb, g=g, j=j: h.dma_start(out=tb[:, 0:512], in_=I["btab"][g * 4 + j]), writes=[btb], dma=f"tb{gi}")
            tbv = tb[:, 0:512].rearrange("p (a q) -> p a q", a=4)
            nat = class_view(accN[:], dil)
            natD = class_view(accD[:], dil)
            items = []
            for r in range(dil):
                for i in range(ntq):
                    q_ap = qt[:, r * Lh + i * QT: r * Lh + i * QT + QT]
                    k0 = r * LP + i * QT
                    ta0 = 1 if i == 0 else 0
                    ta1 = 3 if i == ntq - 1 else 2
                    vti = vbase + r * ntile + i * (QT // 128 if QT >= 128 else 0)
                    bvh = bvbuf[vti // 16]
                    it = {}

                    def sA(it=it, k0=k0, ta0=ta0, ta1=ta1, q_ap=q_ap, kt=kt, bkt=bkt, bqt=bqt, tbv=tbv, btb=btb, QT=QT):
                        st, bst = B.psum("st", ST_BANKS)
                        P.add("pe", lambda h, st=st: h.matmul(st[0:128, 0:QT], kt[:, k0:k0 + 128], q_ap, start=True, stop=True),
                              reads=[bkt, bqt], writes=[bst])
                        P.add("pe", lambda h, st=st: h.matmul(st[0:QT, QT:2 * QT], kt[:, k0 + 128:k0 + 128 + QT], q_ap, start=True, stop=True),
                              reads=[bkt, bqt], writes=[bst])
                        lg, blg = B.next_tmpf()
                        P.add("dve", lambda h, lg=lg, st=st: h.tensor_tensor(out=lg[:, 0:2 * QT].rearrange("p (a q) -> p a q", a=2),
                                                                          in0=st[:, 0:2 * QT].rearrange("p (a q) -> p a q", a=2),
                                                                          in1=tbv[:, ta0:ta1 + 1:ta1 - ta0, 0:QT], op=ALU.add),
                              reads=[bst, btb], writes=[blg])
                        pi = nxt("p", NPT)
                        it["pi"] = pi
                        P.add("act", lambda h, lg=lg, pi=pi: h.activation(out=pt[pi][:, 0:2 * QT], in_=lg[:, 0:2 * QT], func=AF.Exp),
                              reads=[blg], writes=[bpt[pi]])

                    def sB(it=it, vti=vti, vt=vt, bvh=bvh, QT=QT, r=r, i=i, g=g, nat=nat, natD=natD):
                        num, bnum = B.psum("num", [0, 1])
                        den, bden = B.psum("den", [2, 3])
                        pi = it["pi"]
                        P.add("pe", lambda h, num=num, pi=pi: h.matmul(num[:, 0:QT], vt[0:128, vti, :], pt[pi][0:128, 0:QT], start=True, stop=False),
                              reads=[bvh, bpt[pi]], writes=[bnum])
                        P.add("pe", lambda h, num=num, pi=pi: h.matmul(num[:, 0:QT], vt[0:QT, vti + 1, :], pt[pi][0:QT, QT:2 * QT], start=False, stop=True),
                              reads=[bvh, bpt[pi]], writes=[bnum])
                        P.add("pe", lambda h, den=den, pi=pi: h.matmul(den[:, 0:QT], B.ones_b[0:128, :], pt[pi][0:128, 0:QT], start=True, stop=False),
                              reads=[B.b_const, bpt[pi]], writes=[bden])
                        P.add("pe", lambda h, den=den, pi=pi: h.matmul(den[:, 0:QT], B.ones_b[0:QT, :], pt[pi][0:QT, QT:2 * QT], start=False, stop=True),
                              reads=[B.b_const, bpt[pi]], writes=[bden])
                        dN = nat[:, r, i * QT:(i + 1) * QT]
                        dD = natD[:, r, i * QT:(i + 1) * QT]
                        if g == 0:
                            P.add("dve", lambda h, num=num: h.tensor_copy(out=dN, in_=num[:, 0:QT]), reads=[bnum], writes=[baccN])
                            P.add("act", lambda h, den=den: h.activation(out=dD, in_=den[:, 0:QT], func=AF.Copy), reads=[bden], writes=[baccD])
                        else:
                            P.add("dve", lambda h, num=num: h.tensor_tensor(out=dN, in0=num[:, 0:QT], in1=dN, op=ALU.add), reads=[bnum, baccN], writes=[baccN])
                            P.add("dve", lambda h, den=den: h.tensor_tensor(out=dD, in0=den[:, 0:QT], in1=dD, op=ALU.add), reads=[bden, baccD], writes=[baccD])
                    items.append((sA, sB))
            pipeline(items)
        P.add("dve", lambda h: h.reciprocal(out=accD[:], in_=accD[:]), reads=[baccD], writes=[baccD])
        P.add("dve", lambda h, j=j: h.tensor_tensor(out=oT[:, 8 + j, :], in0=accN[:], in1=accD[:], op=ALU.mult), reads=[baccN, baccD], writes=[boT])

    for j in range(4):
        qi = nxt("q", 2)
        qt, bqt = qbuf[qi], bqbuf[qi]
        P.add("sp", lambda h, qt=qt, j=j: h.dma_start(out=qt[:], in_=I["qc"][j]), writes=[bqt], dma=f"qb{qi}")
        items = []
        for qg in range(2):
            grp = {}
            for mt in range(2):
                it = {}

                def sA(it=it, qg=qg, mt=mt, qt=qt, bqt=bqt, j=j):
                    st, bst = B.psum("st", ST_BANKS)
                    P.add("pe", lambda h, st=st: h.matmul(st[:], kc_sb[:, j, mt * 128:(mt + 1) * 128], qt[:, qg * 512:(qg + 1) * 512],
                                                          start=True, stop=True), reads=[bkc, bqt], writes=[bst])
                    pi = nxt("p", NPT)
                    it["pi"] = pi
                    P.add("act", lambda h, st=st, pi=pi: h.activation(out=pt[pi][:], in_=st[:], func=AF.Exp), reads=[bst], writes=[bpt[pi]])

                def sB(it=it, qg=qg, mt=mt, grp=grp, j=j):
                    if mt == 0:
                        grp["num"] = B.psum("num", [0, 1])
                        grp["den"] = B.psum("den", [2, 3])
                    num, bnum = grp["num"]
                    den, bden = grp["den"]
                    pi = it["pi"]
                    P.add("pe", lambda h, num=num, pi=pi: h.matmul(num[:], vc_sb[:, mt, j * 128:(j + 1) * 128], pt[pi][:], start=(mt == 0), stop=(mt == 1)),
                          reads=[bvc, bpt[pi]], writes=[bnum])
                    P.add("pe", lambda h, den=den, pi=pi: h.matmul(den[:], B.ones_b[:], pt[pi][:], start=(mt == 0), stop=(mt == 1)),
                          reads=[B.b_const, bpt[pi]], writes=[bden])
                    if mt == 1:
                        rc, brc = B.next_tmpf()
                        P.add("dve", lambda h, rc=rc, den=den: h.reciprocal(out=rc[:], in_=den[:]), reads=[bden], writes=[brc])
                        P.add("dve", lambda h, rc=rc, num=num: h.tensor_tensor(out=oT[:, 12 + j, qg * 512:(qg + 1) * 512], in0=num[:], in1=rc[:], op=ALU.mult),
                              reads=[bnum, brc], writes=[boT])
                items.append((sA, sB))
        pipeline(items)
    P.barrier()
    es2.close()


def phase3(B, I):
    P = B.P
    nc = B.nc
    es3 = ExitStack()
    oT, boT = B.A32, B.bA32
    merged = es3.enter_context(nc.sbuf_tensor(B.uname("merged"), [128, KC, T], BF16))
    bmerged = P.buf("merged")
    gts = [es3.enter_context(nc.sbuf_tensor(B.uname(f"gts{i}"), [128, 3, T], BF16)) for i in range(2)]
    bgts = P.bufs(2, "gts")
    acc = [es3.enter_context(nc.sbuf_tensor(B.uname(f"macc{i}"), [128, T], F32)) for i in range(2)]
    bacc = P.bufs(2, "macc")
    gview = I["gate"].rearrange("(b c) p t -> c p b t", b=3)
    projs = ((I["w_proj_a"], 8, 0), (I["w_proj_b"], 4, 8), (I["w_proj_c"], 4, 12))
    for op2 in range(0, KC, 2):
        for bi, (W, kcb, c0) in enumerate(projs):
            def evac(ci, tg, ps, bps, bi=bi, op2=op2):
                oc = op2 + ci
                gi = oc % 2
                if bi == 0 and tg == 0:
                    P.add("sp", lambda h, gi=gi, oc=oc: h.dma_start(out=gts[gi][:], in_=gview[oc]), writes=[bgts[gi]], dma=f"gts{gi}")
                sl = slice(tg * 512, (tg + 1) * 512)
                if bi == 0:
                    P.add("dve", lambda h, ps=ps, gi=gi, sl=sl: h.tensor_tensor(out=acc[gi][:, sl], in0=ps[:], in1=gts[gi][:, 0, sl], op=ALU.mult),
                          reads=[bps, bgts[gi]], writes=[bacc[gi]])
                else:
                    tmp, btmp = B.next_tmpf()
                    P.add("dve", lambda h, ps=ps, gi=gi, sl=sl, tmp=tmp, bi=bi: h.tensor_tensor(out=tmp[:], in0=ps[:], in1=gts[gi][:, bi, sl], op=ALU.mult),
                          reads=[bps, bgts[gi]], writes=[btmp])
                    if bi == 1:
                        P.add("dve", lambda h, gi=gi, sl=sl, tmp=tmp: h.tensor_tensor(out=acc[gi][:, sl], in0=acc[gi][:, sl], in1=tmp[:], op=ALU.add),
                              reads=[btmp, bacc[gi]], writes=[bacc[gi]])
                    else:
                        P.add("dve", lambda h, gi=gi, sl=sl, tmp=tmp, oc=oc: h.tensor_tensor(out=merged[:, oc, sl], in0=acc[gi][:, sl], in1=tmp[:], op=ALU.add),
                              reads=[btmp, bacc[gi]], writes=[bmerged])
            B.dense_fm(W, 0, kcb, op2 * 128, 256, lambda k, tg, c0=c0: oT[:, c0 + k, tg * 512:(tg + 1) * 512], boT, [0, 1], evac)

    def evac_out(ci, tg, ps, bps):
        sl = slice(tg * 512, (tg + 1) * 512)
        P.add("dve", lambda h, ci=ci, sl=sl, ps=ps: h.tensor_tensor(out=B.xT[:, ci, sl], in0=ps[:], in1=B.xT[:, ci, sl], op=ALU.add),
              reads=[bps, B.bxT], writes=[B.bxT])

    B.dense_fm(I["w_out"], 0, KC, 0, D, lambda k, tg: merged[:, k, tg * 512:(tg + 1) * 512], bmerged, [0, 1], evac_out)
    P.barrier()
    es3.close()


def phase4(B, I):
    P = B.P
    nc = B.nc
    es4 = ExitStack()
    h2, bh2 = B.A32, B.bA32
    B.rmsnorm(B.xT, B.bxT, h2, bh2, I["gffn_sb"][0], I["gffn_sb"][1], T)
    NH = 22
    act = es4.enter_context(nc.sbuf_tensor(B.uname("actT"), [128, NH, T], BF16))
    bact = P.buf("act")
    sg = [es4.enter_context(nc.sbuf_tensor(B.uname(f"sg{i}"), [128, 2, T], BF16)) for i in range(2)]
    bsg = P.bufs(2, "sg")

    def rhs(k, tg):
        return h2[:, k, tg * 512:(tg + 1) * 512]

    for half in range(2):
        for w0 in range(0, NH * 128, 256):
            f0 = half * NH * 128 + w0
            si = (w0 // 256) % 2

            def evac_g(ci, tg, ps, bps, si=si):
                P.add("act", lambda h, ps=ps, ci=ci, tg=tg, si=si: h.activation(out=sg[si][:, ci, tg * 512:(tg + 1) * 512], in_=ps[:], func=AF.Silu),
                      reads=[bps], writes=[bsg[si]])

            def evac_u(ci, tg, ps, bps, si=si, w0=w0):
                fc = w0 // 128 + ci
                P.add("dve", lambda h, ps=ps, ci=ci, tg=tg, si=si, fc=fc: h.tensor_tensor(out=act[:, fc, tg * 512:(tg + 1) * 512], in0=ps[:],
                                                                                 in1=sg[si][:, ci, tg * 512:(tg + 1) * 512], op=ALU.mult),
                      reads=[bps, bsg[si]], writes=[bact])
            ncol = min(256, NH * 128 - w0)
            B.dense_fm(I["w_ffn_gate"], 0, KC, f0, ncol, rhs, bh2, [0, 1], evac_g)
            B.dense_fm(I["w_ffn_up"], 0, KC, f0, ncol, rhs, bh2, [0, 1], evac_u)

        def evac_d(ci, tg, ps, bps):
            sl = slice(tg * 512, (tg + 1) * 512)
            P.add("dve", lambda h, ci=ci, sl=sl, ps=ps: h.tensor_tensor(out=B.xT[:, ci, sl], in0=ps[:], in1=B.xT[:, ci, sl], op=ALU.add),
                  reads=[bps, B.bxT], writes=[B.bxT])
        B.dense_fm(I["w_ffn_down"], half * NH * 128, NH, 0, D, lambda k, tg: act[:, k, tg * 512:(tg + 1) * 512], bact, [0, 1], evac_d, nw=128)
    P.barrier()
    es4.close()


def store_x(B, out_d, final_gain=None):
    P = B.P
    evs = []
    dst = out_d.rearrange("(c p) t -> p c t", p=128)
    if final_gain is None:
        for c in range(0, KC, 4):
            evs.append(P.add("sp", lambda h, c=c: h.dma_start(out=dst[:, c:c + 4, :], in_=B.xT[:, c:c + 4, :]), reads=[B.bxT], writes=[P.buf()], dma="xo"))
        return evs
    es5 = ExitStack()
    fo = es5.enter_context(B.nc.sbuf_tensor(B.uname("final_o"), [128, KC, T], F32))
    bfo = P.buf("fo")
    B.rmsnorm(B.xT, B.bxT, fo, bfo, final_gain[0], final_gain[1], T)
    for c in range(0, KC, 4):
        evs.append(P.add("sp", lambda h, c=c: h.dma_start(out=dst[:, c:c + 4, :], in_=fo[:, c:c + 4, :]), reads=[bfo], writes=[P.buf()], dma="xo"))
    B._es5 = es5
    return evs


def load_x(B, src2d):
    src = src2d.rearrange("(c p) t -> p c t", p=128)
    for c in range(0, KC, 4):
        B.P.add("sp", lambda h, c=c: h.dma_start(out=B.xT[:, c:c + 4, :], in_=src[:, c:c + 4, :]), writes=[B.bxT], dma="x")


KV_PARTS = (("ka", 1024), ("va", 1024), ("kb1", 1024), ("kb2", 512), ("vb1", 1024), ("vb2", 512))


class _Idx:
    def __init__(self, fn):
        self.fn = fn

    def __getitem__(self, idx):
        if isinstance(idx, tuple):
            v = self.fn(idx[0])
            rest = idx[1:]
            return v[rest] if len(rest) > 1 else v[rest[0]]
        return self.fn(idx)


def kv_views(parts):
    vb1 = parts["vb1"].rearrange("(g t2) (two e) -> g (t2 two) e", g=2, two=2)
    vb2 = parts["vb2"].rearrange("t2 (two e) -> (t2 two) e", two=2)
    kb1 = parts["kb1"].rearrange("(h p) t -> h p t", p=128)
    kb2 = parts["kb2"].rearrange("(h p) t -> h p t", p=128)
    return {
        "ka": parts["ka"].rearrange("(h p) t -> h p t", p=128),
        "va": parts["va"],
        "kb": _Idx(lambda c: kb1[c] if c < 8 else kb2[c - 8]),
        "vb": _Idx(lambda g: vb1[g] if g < 2 else vb2),
    }


def build_fused8():
    B = Builder()
    P = B.P
    I0 = {}
    for n, shp in (("xT_in", [D, T]), ("memT", [D, 256]), ("w_in", [DEPTH, D, 8192]), ("w_gate", [DEPTH, D, 3 * D]),
                   ("w_mem_kv", [DEPTH, D, 1024]), ("w_proj_a", [DEPTH, 1024, D]), ("w_proj_b", [DEPTH, 512, D]),
                   ("w_proj_c", [DEPTH, 512, D]), ("w_out", [DEPTH, D, D]), ("w_ffn_gate", [DEPTH, D, DFF]),
                   ("w_ffn_up", [DEPTH, D, DFF]), ("w_ffn_down", [DEPTH, DFF, D]), ("mrow", [16, 3072]),
                   ("btab", [12, 128, 512]), ("dl", [DEPTH, 1, 256])):
        I0[n] = B.dram_in(n, shp)
    small = {}
    for n, shp in (("g_attn", [128, DEPTH * KC]), ("b_gate", [128, DEPTH * 48]), ("gffn", [128, DEPTH * KC]),
                   ("subln", [128, DEPTH]), ("gmem", [128, KC]), ("gfinal", [128, KC])):
        small[n] = B.dram_in(n, shp)
    ident_d = B.dram_in("ident", [128, 128])
    out_d = B.dram_out("out", [D, T])
    kv_own = [{n: B.nc.dram_tensor(f"kvo_{n}{i}", [r, 1024], BF16) for n, r in KV_PARTS} for i in range(2)]
    kv_all = [{n: B.nc.dram_tensor(f"kva_{n}{i}", [2 * r, 1024], BF16) for n, r in KV_PARTS} for i in range(2)]
    Sq = {"qa": B.dram_tmp("s_qa", [8, 128, T], BF16), "qb": B.dram_tmp("s_qb", [12, 128, T], BF16),
          "qc": B.dram_tmp("s_qc", [4, 128, T], BF16), "gate": B.dram_tmp("s_gate", [48, 128, T], BF16)}
    kbp, vbp = [], []
    for g, (_, dil) in enumerate(GROUPS):
        Lh = T // dil
        nrow = ((Lh + 128 + 127) // 128) * 128
        kbp.append(B.dram_tmp(f"s_kbp{g}", [4, 128, dil * (Lh + 128)], BF16))
        vbp.append(B.dram_tmp(f"s_vbp{g}", [dil, nrow, 512], BF16))

    B.xT = B.sb("xT_sb", [128, KC, T], F32)
    B.bxT = P.buf("xT")
    B.A32 = B.sb("A32", [128, KC, T], BF16)
    B.bA32 = P.buf("A32")
    B.eps_col = B.sb("eps_col", [128, 1], F32)
    P.add("dve", lambda h: h.memset(B.eps_col[:], EPS), writes=[B.b_const])
    B.kc_sb = B.sb("kc_sb", [128, 4, 256], BF16)
    B.bkc = P.buf("kc")
    B.vc_sb = B.sb("vc_sb", [128, 2, 512], BF16)
    B.bvc = P.buf("vc")
    sm = {n: load_small(B, n + "_sb", small[n], list(small[n].shape)) for n in small}
    identf = load_small(B, "ident_f", ident_d, [128, 128])
    finalize_small(B)
    P.add("dve", lambda h: h.tensor_copy(out=B.ident_b[:], in_=identf[0][:]), reads=[identf[1]], writes=[B.b_const])
    load_x(B, I0["xT_in"])

    for l in range(DEPTH):
        lam_init = 0.8 - 0.6 * math.exp(-0.3 * l)
        own = kv_views({n: kv_own[l % 2][n].ap() for n, _ in KV_PARTS})
        allv = [kv_views({n: kv_all[l % 2][n].ap()[r * rows:(r + 1) * rows, :] for n, rows in KV_PARTS}) for r in range(2)]
        outs = dict(Sq)
        outs.update(own)
        p1args = (B, B.xT, B.bxT, B.A32, B.bA32, sm["g_attn"][0][:, l * KC:(l + 1) * KC], sm["g_attn"][1],
                  I0["w_in"][l], I0["w_gate"][l], sm["b_gate"][0][:, l * 48:(l + 1) * 48], sm["b_gate"][1], outs)
        phase1(*p1args, part="main")
        P.wait_events("pool", [("dma", k, c) for k, c in P.dma_count.items()])
        for n, _ in KV_PARTS:
            P.add("pool", lambda h, l=l, n=n: h.collective_compute("AllGather", ALU.bypass, replica_groups=[[0, 1], [2, 3], [4, 5], [6, 7]],
                                                                   ins=[kv_own[l % 2][n].ap().opt()], outs=[kv_all[l % 2][n].ap().opt()]),
                  dma="cc", inc=1)
        phase1(*p1args, part="gates")
        esm = mem_kv(B, {"memT": I0["memT"], "w_mem_kv": I0["w_mem_kv"][l], "gmem_sb": sm["gmem"]})
        P.barrier()
        esm.close()
        for g, (_, dil) in enumerate(GROUPS):
            Lh = T // dil
            for j in range(4):
                dst = kbp[g][j].rearrange("p (r q) -> p r q", r=dil)

                def src(vw, g=g, j=j, dil=dil):
                    return vw["kb"][g * 4 + j].rearrange("p (r q) -> p r q", r=dil)
                P.add("sp", lambda h, dst=dst, a=src(own), Lh=Lh: h.dma_start(out=dst[:, :, 64:64 + Lh], in_=a), dma="pad")
                P.add("sp", lambda h, dst=dst, a=src(allv[0]), Lh=Lh: h.dma_start(out=dst[:, :, 0:64], in_=a[:, :, Lh - 64:Lh]), dma="pad")
                P.add("sp", lambda h, dst=dst, a=src(allv[1]), Lh=Lh: h.dma_start(out=dst[:, :, 64 + Lh:128 + Lh], in_=a[:, :, 0:64]), dma="pad")
            dstv = vbp[g]

            def srcv(vw, g=g, dil=dil):
                return vw["vb"][g].rearrange("(r q) e -> r q e", r=dil)
            P.add("sp", lambda h, dstv=dstv, a=srcv(own), Lh=Lh: h.dma_start(out=dstv[:, 64:64 + Lh, :], in_=a), dma="pad")
            P.add("sp", lambda h, dstv=dstv, a=srcv(allv[0]), Lh=Lh: h.dma_start(out=dstv[:, 0:64, :], in_=a[:, Lh - 64:Lh, :]), dma="pad")
            P.add("sp", lambda h, dstv=dstv, a=srcv(allv[1]), Lh=Lh: h.dma_start(out=dstv[:, 64 + Lh:128 + Lh, :], in_=a[:, 0:64, :]), dma="pad")
        P.barrier()

        I = {"qa": Sq["qa"], "qb": Sq["qb"], "qc": Sq["qc"], "gate": Sq["gate"], "ka_all": _Idx(lambda r, allv=allv: allv[r]["ka"]), "va_all": _Idx(lambda r, allv=allv: allv[r]["va"]),
             "memT": I0["memT"], "w_mem_kv": I0["w_mem_kv"][l], "mrow": I0["mrow"], "btab": I0["btab"], "dl": I0["dl"][l],
             "w_proj_a": I0["w_proj_a"][l], "w_proj_b": I0["w_proj_b"][l], "w_proj_c": I0["w_proj_c"][l], "w_out": I0["w_out"][l],
             "w_ffn_gate": I0["w_ffn_gate"][l], "w_ffn_up": I0["w_ffn_up"][l], "w_ffn_down": I0["w_ffn_down"][l],
             "gmem_sb": sm["gmem"], "subln_sb": (sm["subln"][0][:, l:l + 1], sm["subln"][1]),
             "gffn_sb": (sm["gffn"][0][:, l * KC:(l + 1) * KC], sm["gffn"][1])}
        for g in range(3):
            I[f"kbp{g}"] = kbp[g]
            I[f"vbp{g}"] = vbp[g]
        phase2(B, I, lam_init)
        phase3(B, I)
        phase4(B, I)
    evs = store_x(B, out_d, sm["gfinal"])
    return B.finish(evs)


def _rel_bucket_np(rel):
    rel = np.asarray(rel, dtype=np.int64)
    half_b, max_exact = 16, 8
    n = np.abs(rel)
    nf = np.maximum(n, 1).astype(np.float32)
    large = max_exact + (np.log(nf / np.float32(max_exact)) / np.float32(math.log(1024 / max_exact))
                         * np.float32(half_b - max_exact)).astype(np.int32)
    large = np.minimum(large, half_b - 1)
    return np.where(rel > 0, half_b, 0) + np.where(n < max_exact, n, large)


def _bias_layouts(rel_bias):
    rel_bias = np.asarray(rel_bias, dtype=np.float32)
    bk = _rel_bucket_np(np.arange(0, 4097) - 2048)
    Mf = rel_bias[bk][:, :16].T
    out = []
    p = np.arange(128)[:, None]
    jj = np.arange(128)[None, :]
    for s in range(2):
        B0 = 1025 - 1024 * s
        mrow = np.ascontiguousarray(Mf[:, B0:B0 + 3072])
        btab = np.zeros((12, 128, 4, 128), np.float32)
        for g, (_, dil) in enumerate(GROUPS):
            QT = min(128, (T // dil))
            for j in range(4):
                col = 16 + g * 4 + j
                sa = p - 64 - jj
                sb_ = p + 64 - jj
                va = np.abs(sa) <= 64
                vb = np.abs(sb_) <= 64
                ta = rel_bias[_rel_bucket_np(sa * dil), col]
                tb = rel_bias[_rel_bucket_np(sb_ * dil), col]
                vaf = va & ((p >= 64) if s == 0 else True)
                vbl = vb & ((p < QT - 64) if s == 1 else True)
                for a, (tv, vv) in enumerate(((ta, va), (ta, vaf), (tb, vb), (tb, vbl))):
                    btab[g * 4 + j, :, a, :] = np.where(vv, tv, np.float32(-30000.0))
        out.append((mrow, np.ascontiguousarray(btab.reshape(12, 128, 512))))
    return out


def _fm_vec(v, nchunk):
    return np.ascontiguousarray(np.asarray(v, np.float32).reshape(nchunk, 128).T)


_NC_CACHE = {}


def _get_nc(name):
    if name not in _NC_CACHE:
        _NC_CACHE[name] = {"fused8": build_fused8}[name]()
    return _NC_CACHE[name]


def kernel(x, mem, rel_bias, mem_norm, attn_norm, w_in, diff_lambda, diff_subln, w_mem_kv, w_gate, b_gate,
           w_proj_a, w_proj_b, w_proj_c, w_out, ffn_norm, w_ffn_gate, w_ffn_up, w_ffn_down, final_norm):
    f32 = np.float32
    x = np.asarray(x, f32)
    mem = np.asarray(mem, f32)
    cores = list(range(8))
    lay = _bias_layouts(rel_bias)
    A = lambda a: np.ascontiguousarray(np.asarray(a, f32))

    def fm_all(v, nchunk):
        return np.ascontiguousarray(np.concatenate([_fm_vec(v[l], nchunk) for l in range(DEPTH)], axis=1))

    common = {
        "w_in": A(w_in), "w_gate": A(w_gate), "w_mem_kv": A(w_mem_kv), "w_proj_a": A(w_proj_a), "w_proj_b": A(w_proj_b),
        "w_proj_c": A(w_proj_c), "w_out": A(w_out), "w_ffn_gate": A(w_ffn_gate), "w_ffn_up": A(w_ffn_up), "w_ffn_down": A(w_ffn_down),
        "dl": A(diff_lambda).reshape(DEPTH, 1, 256), "g_attn": fm_all(attn_norm, KC), "b_gate": fm_all(b_gate, 48),
        "gffn": fm_all(ffn_norm, KC), "subln": np.ascontiguousarray(A(diff_subln).T), "gmem": _fm_vec(mem_norm, KC),
        "gfinal": _fm_vec(final_norm, KC), "ident": np.eye(128, dtype=f32),
    }
    in_maps = []
    for c in cores:
        b, s_ = c // 2, c % 2
        d = dict(common)
        d["xT_in"] = np.ascontiguousarray(x[b, s_ * T:(s_ + 1) * T].T)
        d["memT"] = np.ascontiguousarray(mem[b].T)
        d["mrow"] = lay[s_][0]
        d["btab"] = lay[s_][1]
        in_maps.append(d)
    res = run_bass_kernel_spmd(_get_nc("fused8"), in_maps, core_ids=cores).results
    out = np.empty((4, SEQ, D), f32)
    for c in cores:
        b, s_ = c // 2, c % 2
        out[b, s_ * T:(s_ + 1) * T] = np.asarray(res[c]["out"], f32).T
    return out
```

```python
import math
from contextlib import ExitStack
import numpy as np
import ml_dtypes
import concourse.bass as bass
import concourse.mybir as mybir
from concourse.bass_utils import run_bass_kernel_spmd

F32 = mybir.dt.float32
BF16 = mybir.dt.bfloat16
AF = mybir.ActivationFunctionType
ALU = mybir.AluOpType
NPBF = ml_dtypes.bfloat16

D = 2048
T = 1024
SEQ = 2048
DEPTH = 4
DFF = 5632
EPS = 1e-5
KC = 16
GROUPS = ((128, 1), (512, 4), (2048, 16))
NEG = -1e30

ENGS = ("pe", "act", "dve", "pool", "sp")


class Buf:
    __slots__ = ("name", "w", "r")

    def __init__(self, name=""):
        self.name = name
        self.w = None
        self.r = []


class Plan:
    def __init__(self):
        self.ops = {e: [] for e in ENGS}
        self.seen = {e: {} for e in ENGS}
        self.dma_count = {}

    def buf(self, name=""):
        return Buf(name)

    def bufs(self, n, name=""):
        return [Buf(f"{name}{i}") for i in range(n)]

    def _need(self, eng, ev, waits):
        kind, k, v = ev
        if kind == "op":
            if k == eng and eng == "pe":
                return
            key = ("op", k)
        else:
            key = ("dma", k)
        s = self.seen[eng]
        if s.get(key, -1) >= v:
            return
        s[key] = v
        if kind == "op":
            self.ops[k][v]["inc"] = True
        waits.append(ev)

    def add(self, eng, fn, reads=(), writes=(), dma=None, inc=16):
        waits = []
        for b in reads:
            if b.w is not None:
                self._need(eng, b.w, waits)
        for b in writes:
            if b.w is not None:
                self._need(eng, b.w, waits)
            for ev in b.r:
                self._need(eng, ev, waits)
        idx = len(self.ops[eng])
        op = {"fn": fn, "waits": waits, "inc": False, "dma": None}
        if dma is not None:
            c = self.dma_count.get(dma, 0) + inc
            self.dma_count[dma] = c
            op["dma"] = dma
            op["dinc"] = inc
            ev = ("dma", dma, c)
        else:
            ev = ("op", eng, idx)
        self.ops[eng].append(op)
        for b in reads:
            b.r = [e for e in b.r if not (e[0] == ev[0] and e[1] == ev[1])]
            b.r.append(ev)
        for b in writes:
            b.w = ev
            b.r = []
        return ev

    def wait_events(self, eng, evs):
        waits = []
        for ev in evs:
            self._need(eng, ev, waits)
        self.ops[eng].append({"fn": None, "waits": waits, "inc": False, "dma": None})

    def barrier(self):
        evs = []
        for e in ENGS:
            if e == "sp":
                continue
            if self.ops[e]:
                for i in range(len(self.ops[e]) - 1, -1, -1):
                    if self.ops[e][i]["fn"] is not None and self.ops[e][i]["dma"] is None:
                        evs.append(("op", e, i))
                        break
        for k, c in self.dma_count.items():
            evs.append(("dma", k, c))
        for e in ENGS:
            if e != "pool":
                self.wait_events(e, evs)

    def emit(self, nc, es):
        esem = {e: es.enter_context(nc.semaphore(f"s_{e}")) for e in ENGS}
        dsem = {k: es.enter_context(nc.semaphore(f"d_{i}")) for i, k in enumerate(self.dma_count)}
        val = {}
        for e in ENGS:
            c = 0
            v = []
            for op in self.ops[e]:
                if op["inc"]:
                    c += 1
                v.append(c)
            val[e] = v
        self.n_sems = len(esem) + len(dsem)

        def body(e):
            def run(h):
                for op in self.ops[e]:
                    for (kind, k, v) in op["waits"]:
                        if kind == "op":
                            h.wait_ge(esem[k], val[k][v])
                        else:
                            h.wait_ge(dsem[k], v)
                    if op["fn"] is None:
                        continue
                    ins = op["fn"](h)
                    if op["dma"] is not None:
                        if op["dinc"] == 1:
                            ins.then_inc(dsem[op["dma"]])
                        else:
                            ins.then_inc(dsem[op["dma"]], op["dinc"])
                    elif op["inc"]:
                        ins.then_inc(esem[e], 1)
            return run

        with nc.Block() as block:
            block.tensor(body("pe"))
            block.scalar(body("act"))
            block.vector(body("dve"))
            block.gpsimd(body("pool"))
            block.sync(body("sp"))


class Builder:
    def __init__(self):
        self.nc = bass.Bass("TRN2", target_bir_lowering=False)
        self.P = Plan()
        self.es = ExitStack()
        self.n_dma = 0
        nc = self.nc
        self.ps = [self.es.enter_context(nc.psum_tensor(f"ps{i}", [128, 512], F32)) for i in range(8)]
        self.bps = self.P.bufs(8, "ps")
        self.ps_rr = {}
        self.NW = 256
        self.wslots = [self.sb(f"wslot{i}", [128, 16 * 256], BF16) for i in range(3)]
        self.bw = self.P.bufs(3, "w")
        self.w_i = 0
        self.ones_f = self.sb("ones_f", [128, 128], F32)
        self.ones_b = self.sb("ones_b", [128, 128], BF16)
        self.b_const = self.P.buf("const")
        self.P.add("dve", lambda h: h.memset(self.ones_f[:], 1.0), writes=[self.b_const])
        self.P.add("dve", lambda h: h.memset(self.ones_b[:], 1.0), writes=[self.b_const])
        self.ident_b = self.sb("ident_b", [128, 128], BF16)
        self.stg_i = 0
        self.stg = [self.sb(f"stg{i}", [128, T], BF16) for i in range(3)]
        self.bstg = self.P.bufs(3, "stg")
        self.tmpf = [self.sb(f"tmpf{i}", [128, 512], F32) for i in range(4)]
        self.btmpf = self.P.bufs(4, "tmpf")
        self.tmpf_i = 0

    def sb(self, name, shape, dt):
        return self.es.enter_context(self.nc.sbuf_tensor(name, shape, dt))

    def uname(self, name):
        self._uid = getattr(self, "_uid", 0) + 1
        return f"{name}_u{self._uid}"

    def dram_in(self, name, shape, dt=F32):
        return self.nc.dram_tensor(name, list(shape), dt, kind="ExternalInput").ap()

    def dram_out(self, name, shape, dt=F32):
        return self.nc.dram_tensor(name, list(shape), dt, kind="ExternalOutput").ap()

    def dram_tmp(self, name, shape, dt=F32):
        return self.nc.dram_tensor(name, list(shape), dt, kind="Internal").ap()

    def dkey(self, name):
        self.n_dma += 1
        return f"{name}_{self.n_dma}"

    def psum(self, pool, banks):
        i = self.ps_rr.get(pool, 0)
        self.ps_rr[pool] = i + 1
        b = banks[i % len(banks)]
        return self.ps[b], self.bps[b]

    def next_tmpf(self):
        i = self.tmpf_i % 4
        self.tmpf_i += 1
        return self.tmpf[i], self.btmpf[i]

    def next_stg(self):
        i = self.stg_i % 3
        self.stg_i += 1
        return self.stg[i], self.bstg[i], f"stg{i}"

    def wload(self, W, k0, kc, c0, ncol):
        i = self.w_i % 3
        self.w_i += 1
        slot = self.wslots[i]
        view = slot[:, 0:kc * ncol].rearrange("p (c n) -> p c n", n=ncol)
        src = W[k0:k0 + kc * 128, c0:c0 + ncol].rearrange("(c p) n -> p c n", p=128)
        self.P.add("pool", lambda h: h.dma_start(out=view, in_=src), writes=[self.bw[i]], dma=f"w{i}")
        return view, self.bw[i]

    def rmsnorm(self, src, bsrc, dst, bdst, gain, bgain, ntok, nfeat_chunks=KC, dfeat=D):
        P = self.P
        for t0 in range(0, ntok, 512):
            n = min(512, ntok - t0)
            pst, bpst = self.psum("misc", [7])
            for c in range(nfeat_chunks):
                sq, bsq = self.next_tmpf()
                P.add("act", lambda h, sq=sq, c=c, t0=t0, n=n: h.activation(out=sq[:, 0:n], in_=src[:, c, t0:t0 + n], func=AF.Square),
                      reads=[bsrc], writes=[bsq])
                P.add("pe", lambda h, sq=sq, c=c, pst=pst, n=n: h.matmul(pst[:, 0:n], self.ones_f[:], sq[:, 0:n],
                                                                     start=(c == 0), stop=(c == nfeat_chunks - 1)),
                      reads=[bsq, self.b_const], writes=[bpst])
            rs, brs = self.next_tmpf()
            P.add("act", lambda h, rs=rs, pst=pst, n=n: h.activation(out=rs[:, 0:n], in_=pst[:, 0:n], func=AF.Sqrt,
                                                                 bias=self.eps_col[:, 0:1], scale=1.0 / dfeat),
                  reads=[bpst, self.b_const], writes=[brs])
            P.add("dve", lambda h, rs=rs, n=n: h.reciprocal(out=rs[:, 0:n], in_=rs[:, 0:n]), reads=[brs], writes=[brs])
            for c in range(nfeat_chunks):
                P.add("dve", lambda h, rs=rs, c=c, t0=t0, n=n: h.scalar_tensor_tensor(out=dst[:, c, t0:t0 + n], in0=src[:, c, t0:t0 + n],
                                                                          scalar=gain[:, c:c + 1], in1=rs[:, 0:n],
                                                                          op0=ALU.mult, op1=ALU.mult),
                      reads=[bsrc, brs, bgain], writes=[bdst])

    def dense_fm(self, W, k0, kc, c0, ncols, rhs_fn, brhs, tgs, evac, nw=None):
        nw = nw or self.NW
        P = self.P
        for w0 in range(0, ncols, nw):
            wn = min(nw, ncols - w0)
            wt, bwt = self.wload(W, k0, kc, c0 + w0, wn)
            for oc in range(wn // 128):
                for tg in tgs:
                    ps, bps = self.psum("dense", [0, 1, 2, 3])
                    r0 = rhs_fn(0, tg)
                    n = 1
                    for s_ in r0.shape[1:]:
                        n *= s_
                    for k in range(kc):
                        P.add("pe", lambda h, wt=wt, k=k, oc=oc, tg=tg, ps=ps, n=n: h.matmul(
                            ps[:, 0:n], wt[:, k, oc * 128:(oc + 1) * 128], rhs_fn(k, tg), start=(k == 0), stop=(k == kc - 1)),
                            reads=[bwt, brhs], writes=[bps])
                    evac((w0 // 128) + oc, tg, ps, bps)

    def dense_tm(self, W, k0, kc, c0, ncols, lhs_fn, blhs, ntt, evac):
        P = self.P
        nw = self.NW
        for w0 in range(0, ncols, nw):
            wt, bwt = self.wload(W, k0, kc, c0 + w0, nw)
            for tt in range(ntt):
                ps, bps = self.psum("dense", [0, 1, 2, 3])
                nparts = len(lhs_fn(0, tt))
                for part in range(nparts):
                    for k in range(kc):
                        def mm(h, wt=wt, k=k, tt=tt, ps=ps, part=part):
                            ap, r0, nr = lhs_fn(k, tt)[part]
                            return h.matmul(ps[r0:r0 + nr, 0:nw], ap, wt[:, k, :], start=(k == 0), stop=(k == kc - 1))
                        P.add("pe", mm, reads=[bwt, blhs], writes=[bps])
                evac(w0, tt, ps, bps)

    def finish(self, out_events):
        self.P.wait_events("sp", out_events)
        self.P.emit(self.nc, self.es)
        if getattr(self, "_es5", None) is not None:
            self._es5.close()
        self.es.close()
        return self.nc


def class_view(ap2d, dil):
    if dil == 1:
        return ap2d.rearrange("p (r pos) -> p r pos", r=1)
    return ap2d.rearrange("p (pos r) -> p r pos", r=dil)


def phase1(B, xT, bxT, hT, bhT, g_attn, bg, w_in, w_gate, bgate_sb, bbg, outs, part="all"):
    P = B.P
    if part != "gates":
        B.rmsnorm(xT, bxT, hT, bhT, g_attn, bg, T)
    out_evs = []

    def nat_rhs(k, tg):
        return hT[:, k, tg * 512:(tg + 1) * 512]

    def cls_rhs(dil):
        def f(k, tg):
            v = class_view(hT[:, k, :], dil)
            nr = (512 * dil) // T
            if dil == 1:
                return hT[:, k, tg * 512:(tg + 1) * 512]
            return v[:, tg * nr:(tg + 1) * nr, :]
        return f

    cur = {}

    def make_evac(dst, scale=None, sigmoid=False, chunk_off=0, eng_alt=("dve", "act"), dil=1, rev=False):
        cnt = [0]

        def evac(ci, tg, ps, bps):
            c = ci + chunk_off
            if tg == 0:
                cur["s"] = B.next_stg()
            stg, bstg, skey = cur["s"]
            o = stg[:, tg * 512:(tg + 1) * 512]
            psv = ps[:]
            if rev:
                psv = bass.AP(tensor=ps[:].tensor, offset=ps[:, 511:512].offset, ap=[list(ps[:].ap[0]), [-1, 512]])
            if dil > 1:
                npos = 512 // dil
                o = stg[:].rearrange("p (r pos) -> p r pos", r=dil)[:, :, tg * npos:(tg + 1) * npos]
                psv = class_view(ps[:], dil)
            if sigmoid:
                P.add("act", lambda h, o=o, ps=ps, c=c: h.activation(out=o, in_=ps[:], func=AF.Sigmoid,
                                                                    bias=bgate_sb[:, c:c + 1], scale=1.0),
                      reads=[bps, bbg], writes=[bstg])
            else:
                e = eng_alt[cnt[0] % 2]
                cnt[0] += 1
                if e == "act":
                    P.add("act", lambda h, o=o, psv=psv: h.activation(out=o, in_=psv, func=AF.Copy,
                                                                      scale=(scale if scale is not None else 1.0)),
                          reads=[bps], writes=[bstg])
                elif scale is not None:
                    P.add("dve", lambda h, o=o, psv=psv: h.tensor_scalar_mul(out=o, in0=psv, scalar1=scale),
                          reads=[bps], writes=[bstg])
                else:
                    P.add("dve", lambda h, o=o, psv=psv: h.tensor_copy(out=o, in_=psv), reads=[bps], writes=[bstg])
            if tg == 1:
                ev = P.add("sp", lambda h, stg=stg, c=c: h.dma_start(out=dst[c], in_=stg[:]), reads=[bstg],
                           writes=[B.P.buf()], dma=skey)
                out_evs.append(ev)
        return evac

    sB = 128 ** -0.5
    if part == "gates":
        B.dense_fm(w_gate, 0, KC, 0, 3 * D, nat_rhs, bhT, [0, 1], make_evac(outs["gate"], sigmoid=True))
        return out_evs
    B.dense_fm(w_in, 0, KC, 0, 1024, nat_rhs, bhT, [0, 1], make_evac(outs["qa"], scale=0.125, rev=True))
    B.dense_fm(w_in, 0, KC, 1024, 1024, nat_rhs, bhT, [0, 1], make_evac(outs["ka"]))
    for g, (_, dil) in enumerate(GROUPS):
        B.dense_fm(w_in, 0, KC, 3072 + g * 512, 512, nat_rhs, bhT, [0, 1],
                   make_evac(outs["qb"], scale=sB, chunk_off=g * 4, dil=dil))
        B.dense_fm(w_in, 0, KC, 4608 + g * 512, 512, nat_rhs, bhT, [0, 1],
                   make_evac(outs["kb"], chunk_off=g * 4, dil=dil))
    B.dense_fm(w_in, 0, KC, 7680, 512, nat_rhs, bhT, [0, 1], make_evac(outs["qc"], scale=sB))

    def make_evac_tm(dst2d):
        cnt = [0]

        def evac(w0, tt, ps, bps):
            stg, bstg, skey = B.next_stg()
            e = ("dve", "act")[cnt[0] % 2]
            cnt[0] += 1
            if e == "dve":
                P.add("dve", lambda h, stg=stg, ps=ps: h.tensor_copy(out=stg[:, 0:256], in_=ps[:, 0:256]), reads=[bps], writes=[bstg])
            else:
                P.add("act", lambda h, stg=stg, ps=ps: h.activation(out=stg[:, 0:256], in_=ps[:, 0:256], func=AF.Copy),
                      reads=[bps], writes=[bstg])
            ev = P.add("sp", lambda h, stg=stg, tt=tt, w0=w0: h.dma_start(out=dst2d[tt * 128:(tt + 1) * 128, w0:w0 + 256],
                                                                      in_=stg[:, 0:256]),
                       reads=[bstg], writes=[B.P.buf()], dma=skey)
            out_evs.append(ev)
        return evac

    def nat_lhs(k, tt):
        return [(hT[:, k, tt * 128:(tt + 1) * 128], 0, 128)]

    def cls_lhs(dil):
        def f(k, tt):
            if dil == 1:
                return [(hT[:, k, tt * 128:(tt + 1) * 128], 0, 128)]
            v = class_view(hT[:, k, :], dil)
            Lh = T // dil
            if Lh >= 128:
                r = (tt * 128) // Lh
                p0 = (tt * 128) % Lh
                return [(v[:, r, p0:p0 + 128], 0, 128)]
            nr = 128 // Lh
            return [(v[:, tt * nr + i, :], i * Lh, Lh) for i in range(nr)]
        return f

    B.dense_tm(w_in, 0, KC, 2048, 1024, nat_lhs, bhT, 8, make_evac_tm(outs["va"]))
    for g, (_, dil) in enumerate(GROUPS):
        B.dense_tm(w_in, 0, KC, 6144 + g * 512, 512, cls_lhs(dil), bhT, 8, make_evac_tm(outs["vb"][g]))
    if part == "all":
        B.dense_fm(w_gate, 0, KC, 0, 3 * D, nat_rhs, bhT, [0, 1], make_evac(outs["gate"], sigmoid=True))
    return out_evs


def load_small(B, name, dram_ap, shape, dt=F32):
    t = B.sb(name, shape, dt)
    b = B.P.buf(name)
    B.P.add("sp", lambda h: h.dma_start(out=t[:], in_=dram_ap), writes=[b], dma="small")
    B.small_bufs = getattr(B, "small_bufs", []) + [b]
    return t, b


def finalize_small(B):
    tot = B.P.dma_count.get("small", 0)
    for b in getattr(B, "small_bufs", []):
        b.w = ("dma", "small", tot)


def mem_kv(B, I):
    P = B.P
    nc = B.nc
    esm = ExitStack()
    memn = esm.enter_context(nc.sbuf_tensor(B.uname("memn"), [128, KC, 256], BF16))
    bmemn = P.buf("memn")
    memf = esm.enter_context(nc.sbuf_tensor(B.uname("memf"), [128, KC, 256], F32))
    bmemf = P.buf("memf")
    msrc = I["memT"].rearrange("(c p) t -> p c t", p=128)
    P.add("sp", lambda h: h.dma_start(out=memf[:], in_=msrc), writes=[bmemf], dma="memf")
    B.rmsnorm(memf, bmemf, memn, bmemn, I["gmem_sb"][0], I["gmem_sb"][1], 256)
    kc_sb, bkc, vc_sb, bvc = B.kc_sb, B.bkc, B.vc_sb, B.bvc

    def evac_kc(ci, tg, ps, bps):
        P.add("dve", lambda h, ci=ci, ps=ps: h.tensor_copy(out=kc_sb[:, ci, :], in_=ps[:, 0:256]), reads=[bps], writes=[bkc])

    B.dense_fm(I["w_mem_kv"], 0, KC, 0, 512, lambda k, tg: memn[:, k, :], bmemn, [0], evac_kc)

    def evac_vc(w0, tt, ps, bps):
        P.add("dve", lambda h, w0=w0, tt=tt, ps=ps: h.tensor_copy(out=vc_sb[:, tt, w0:w0 + 256], in_=ps[:, 0:256]), reads=[bps], writes=[bvc])

    B.dense_tm(I["w_mem_kv"], 0, KC, 512, 512, lambda k, tt: [(memn[:, k, tt * 128:(tt + 1) * 128], 0, 128)], bmemn, 2, evac_vc)
    return esm


W_G = 2944


def phase2(B, I, lam_init):
    P = B.P
    nc = B.nc
    es2 = ExitStack()

    def sb(name, shape, dt):
        return es2.enter_context(nc.sbuf_tensor(B.uname("s2_" + name), shape, dt))

    kc_sb, bkc, vc_sb, bvc = B.kc_sb, B.bkc, B.vc_sb, B.bvc
    oT, boT = B.A32, B.bA32
    kbuf = [sb(f"kbuf{i}", [128, 3072], BF16) for i in range(2)]
    bkbuf = P.bufs(2, "kbuf")
    vbuf = [sb("vbuf0", [128, 32, 128], BF16)]
    bvbuf = P.bufs(2, "vbuf")
    gbuf = [sb(f"gbuf{i}", [128, W_G], BF16) for i in range(2)]
    bgbuf = P.bufs(2, "gbuf")
    tbuf = [sb(f"tbuf{i}", [128, 512], F32) for i in range(2)]
    btbuf = P.bufs(2, "tbuf")
    cnt_t = [0]
    NPT = 4
    pt = [sb(f"pt{i}", [128, 512], BF16) for i in range(NPT)]
    bpt = P.bufs(NPT, "pt")
    t01x = sb("t01x", [128, 2, T], F32)
    t01 = [t01x[:, 0, :], t01x[:, 1, :]]
    bt01 = P.bufs(2, "t01")
    obuf = sb("obuf", [128, T], F32)
    bobuf = P.buf("obuf")
    cnt = {"k": 0, "q": 0, "g": 0, "p": 0}
    qpad = [[sb(f"qpad{i}_{m}", [128, T], BF16) for m in range(2)] for i in range(2)]
    bqpad = [P.buf(f"qpad{i}") for i in range(2)]
    for i in range(2):
        P.add("dve", lambda h, i=i: h.memset(qpad[i][0][64:128, :], 0.0), writes=[bqpad[i]])
        P.add("dve", lambda h, i=i: h.memset(qpad[i][1][0:64, :], 0.0), writes=[bqpad[i]])
    qbuf = [qpad[0][0], qpad[1][0]]
    bqbuf = bqpad
    dacc = [sb(f"dacc{i}", [128, 512], F32) for i in range(2)]
    bdacc = P.bufs(2, "dacc")
    deferred = []

    def run_deferred(flush=False):
        for d in list(deferred):
            d[0] -= 1
            if flush or d[0] <= 0:
                deferred.remove(d)
                d[1]()
    cnt_d = [0]

    def nxt(name, n):
        i = cnt[name] % n
        cnt[name] += 1
        return i

    dl = sb("dl", [128, 256], F32)
    bdl = P.buf("dl")
    dl_src = bass.AP(tensor=I["dl"].tensor, offset=I["dl"].offset, ap=[[0, 128], [1, 256]])
    P.add("sp", lambda h: h.dma_start(out=dl[:], in_=dl_src), writes=[bdl], dma="dl")
    lam = sb("lam", [128, 8], F32)
    blam = P.buf("lam")
    prod = sb("prod", [128, 128], F32)
    bprod = P.buf("prod")
    dv = dl[:].rearrange("p (a d) -> p a d", d=64)
    P.add("dve", lambda h: h.tensor_tensor(out=prod[:].rearrange("p (a d) -> p a d", d=64), in0=dv[:, 0:4:2, :], in1=dv[:, 1:4:2, :],
                                           op=ALU.mult), reads=[bdl], writes=[bprod])
    P.add("dve", lambda h: h.reduce_sum(out=lam[:, 0:2], in_=prod[:].rearrange("p (a d) -> p a d", d=64), axis=mybir.AxisListType.X),
          reads=[bprod], writes=[blam])
    P.add("act", lambda h: h.activation(out=lam[:, 0:2], in_=lam[:, 0:2], func=AF.Exp), reads=[blam], writes=[blam])
    P.add("dve", lambda h: h.tensor_tensor(out=lam[:, 2:3], in0=lam[:, 1:2], in1=lam[:, 0:1], op=ALU.subtract), reads=[blam], writes=[blam])
    if lam_init is None:
        lc, blc = I["lconst_sb"]
        P.add("dve", lambda h: h.tensor_tensor(out=lam[:, 2:3], in0=lam[:, 2:3], in1=lc[:, 0:1], op=ALU.subtract),
              reads=[blam, blc], writes=[blam])
        P.add("dve", lambda h: h.tensor_tensor(out=lam[:, 3:4], in0=I["subln_sb"][0][:, 0:1], in1=lc[:, 1:2], op=ALU.mult),
              reads=[blam, blc, I["subln_sb"][1]], writes=[blam])
    else:
        P.add("dve", lambda h: h.tensor_scalar_add(out=lam[:, 2:3], in0=lam[:, 2:3], scalar1=-lam_init), reads=[blam], writes=[blam])
        P.add("dve", lambda h: h.tensor_scalar_mul(out=lam[:, 3:4], in0=I["subln_sb"][0][:, 0:1], scalar1=1.0 - lam_init),
              reads=[blam, I["subln_sb"][1]], writes=[blam])

    def pipeline(items, LA=3):
        n = len(items)
        for st_ in range(n + LA):
            if st_ < n:
                items[st_][0]()
            if st_ - LA >= 0:
                items[st_ - LA][1]()

    ST_BANKS = [4, 5, 6, 7]

    for hh in range(8):
        ki = nxt("k", 2)
        qi = nxt("q", 2)
        kt, bkt = kbuf[ki], bkbuf[ki]
        qp, bqt = qpad[qi], bqpad[qi]
        hf = hh % 2
        vt, bvt = vbuf[0][:, 16 * hf:16 * hf + 16, :], bvbuf[hf]
        for r in range(2):
            P.add("sp", lambda h, kt=kt, r=r, hh=hh: h.dma_start(out=kt[:, r * T:(r + 1) * T], in_=I["ka_all"][r, hh]),
                  writes=[bkt], dma=f"kb{ki}")
            vsrc = I["va_all"][r, :, hh * 128:(hh + 1) * 128].rearrange("(t p) e -> p t e", p=128)
            P.add("sp", lambda h, vt=vt, r=r, vsrc=vsrc: h.dma_start(out=vt[:, r * 8:(r + 1) * 8, :], in_=vsrc),
                  writes=[bvt], dma=f"vb{hf}")
        for m in range(2):
            P.add("sp", lambda h, qp=qp, hh=hh, m=m: h.dma_start(out=qp[m][m * 64:(m + 1) * 64, :], in_=I["qa"][hh][m * 64:(m + 1) * 64, :]),
                  writes=[bqt], dma=f"qp{qi}")
        items = []
        for m in range(2):
            gi = nxt("g", 2)
            gt, bgt = gbuf[gi], bgbuf[gi]
            hm = m * 8 + hh
            gsrc = bass.AP(tensor=I["mrow"].tensor, offset=I["mrow"][hm, 0:1].offset, ap=[[1, 128], [1, W_G]])
            P.add("pool", lambda h, gt=gt, gsrc=gsrc: h.dma_start(out=gt[:], in_=gsrc), writes=[bgt], dma=f"gb{gi}")
            for qg in range(2):
                grp = {}
                for k2 in range(0, 16, 2):
                    it = {}

                    def sA(m=m, qg=qg, k2=k2, it=it, gt=gt, bgt=bgt, kt=kt, qp=qp, bkt=bkt, bqt=bqt):
                        sts = [B.psum("st", ST_BANKS) for _ in range(2)]
                        for u in range(2):
                            kk = k2 + u
                            st, bst = sts[u]
                            P.add("pe", lambda h, st=st, kk=kk: h.matmul(st[:], kt[:, kk * 128:(kk + 1) * 128], qp[m][:, qg * 512:(qg + 1) * 512],
                                                                         start=True, stop=False), reads=[bkt, bqt], writes=[bst])
                        for u in range(2):
                            kk = k2 + u
                            st, bst = sts[u]
                            c0 = kk * 128 - 512 * qg + 512
                            P.add("pe", lambda h, st=st, c0=c0: h.matmul(st[:], B.ident_b[:], gt[:, c0:c0 + 512], start=False, stop=True),
                                  reads=[bgt, B.b_const], writes=[bst])
                        it["pi"] = []
                        for u in range(2):
                            st, bst = sts[u]
                            pi = nxt("p", NPT)
                            it["pi"].append(pi)
                            P.add("act", lambda h, st=st, pi=pi: h.activation(out=pt[pi][:], in_=st[:], func=AF.Exp), reads=[bst], writes=[bpt[pi]])

                    def sB(m=m, qg=qg, k2=k2, it=it, grp=grp, vt=vt, bvt=bvt):
                        if k2 == 0:
                            grp["num"] = B.psum("num", [0, 1])
                            grp["den"] = B.psum("den", [2, 3])
                            di = cnt_d[0] % 2
                            cnt_d[0] += 1
                            grp["da"] = (dacc[di], bdacc[di])
                        num, bnum = grp["num"]
                        den, bden = grp["den"]
                        da, bda = grp["da"]
                        for u in range(2):
                            kk = k2 + u
                            pi = it["pi"][u]
                            P.add("pe", lambda h, num=num, pi=pi, kk=kk: h.matmul(num[:], vt[:, kk, :], pt[pi][:], start=(kk == 0), stop=(kk == 15)),
                                  reads=[bvt, bpt[pi]], writes=[bnum])
                        pi1 = it["pi"][1]
                        P.add("pe", lambda h, den=den, pi1=pi1: h.matmul(den[:], B.ones_b[:], pt[pi1][:], start=(k2 == 0), stop=False),
                              reads=[B.b_const, bpt[pi1]], writes=[bden])
                        pi0 = it["pi"][0]
                        if k2 == 0:
                            P.add("dve", lambda h, da=da, pi0=pi0: h.tensor_copy(out=da[:], in_=pt[pi0][:]), reads=[bpt[pi0]], writes=[bda])
                        else:
                            P.add("dve", lambda h, da=da, pi0=pi0: h.tensor_tensor(out=da[:], in0=pt[pi0][:], in1=da[:], op=ALU.add),
                                  reads=[bpt[pi0], bda], writes=[bda])
                        run_deferred()
                        if k2 == 14:
                            def epi(num=num, bnum=bnum, den=den, bden=bden, da=da, bda=bda, m=m, qg=qg):
                                P.add("pe", lambda h, den=den, da=da: h.matmul(den[:], B.ones_f[:], da[:], start=False, stop=True),
                                      reads=[B.b_const, bda], writes=[bden])
                                rc, brc = B.next_tmpf()
                                P.add("dve", lambda h, rc=rc, den=den: h.reciprocal(out=rc[:], in_=den[:]), reads=[bden], writes=[brc])
                                numr = bass.AP(tensor=num[:].tensor, offset=num[:, 511:512].offset, ap=[list(num[:].ap[0]), [-1, 512]])
                                rcr = bass.AP(tensor=rc[:].tensor, offset=rc[:, 511:512].offset, ap=[list(rc[:].ap[0]), [-1, 512]])
                                P.add("dve", lambda h, rcr=rcr, numr=numr: h.tensor_tensor(out=t01[m][:, qg * 512:(qg + 1) * 512], in0=numr, in1=rcr,
                                                                                         op=ALU.mult), reads=[bnum, brc], writes=[bt01[m]])
                            deferred.append([3, epi])
                    items.append((sA, sB))
        pipeline(items, LA=1)

        def head_epi(hh=hh):
            P.add("dve", lambda h: h.scalar_tensor_tensor(out=obuf[:], in0=t01[1][:], scalar=lam[:, 2:3], in1=t01[0][:], op0=ALU.mult, op1=ALU.add),
                  reads=[bt01[0], bt01[1], blam], writes=[bobuf])
            for qg in range(2):
                sq, bsq = B.next_tmpf()
                P.add("act", lambda h, sq=sq, qg=qg: h.activation(out=sq[:], in_=obuf[:, qg * 512:(qg + 1) * 512], func=AF.Square),
                      reads=[bobuf], writes=[bsq])
                pss, bpss = B.psum("st", ST_BANKS)
                P.add("pe", lambda h, pss=pss, sq=sq: h.matmul(pss[:], B.ones_f[:], sq[:], start=True, stop=True), reads=[bsq, B.b_const], writes=[bpss])
                rs, brs = B.next_tmpf()
                P.add("act", lambda h, rs=rs, pss=pss: h.activation(out=rs[:], in_=pss[:], func=AF.Ln, bias=B.eps_col[:, 0:1], scale=1.0 / 128),
                      reads=[bpss, B.b_const], writes=[brs])
                P.add("act", lambda h, rs=rs: h.activation(out=rs[:], in_=rs[:], func=AF.Exp, scale=-0.5), reads=[brs], writes=[brs])
                P.add("dve", lambda h, rs=rs, qg=qg, hh=hh: h.scalar_tensor_tensor(out=oT[:, hh, qg * 512:(qg + 1) * 512], in0=obuf[:, qg * 512:(qg + 1) * 512],
                                                                             scalar=lam[:, 3:4], in1=rs[:], op0=ALU.mult, op1=ALU.mult),
                      reads=[bobuf, brs, blam], writes=[boT])
        deferred.append([5, head_epi])
    run_deferred(flush=True)

    accN, baccN = t01[0], bt01[0]
    accD, baccD = t01[1], bt01[1]
    for j in range(4):
        for g, (_, dil) in enumerate(GROUPS):
            Lh = T // dil
            QT = min(128, Lh)
            ntq = Lh // QT
            LP = Lh + 128
            ntile = (LP + 127) // 128
            ki = nxt("k", 2)
            qi = nxt("q", 2)
            gi = cnt_t[0] % 2
            cnt_t[0] += 1
            kt, bkt = kbuf[ki], bkbuf[ki]
            qt, bqt = qbuf[qi], bqbuf[qi]
            vt = vbuf[0]
            tb, btb = tbuf[gi], btbuf[gi]
            P.add("sp", lambda h, kt=kt, g=g, j=j, dil=dil, LP=LP: h.dma_start(out=kt[:, 0:dil * LP], in_=I[f"kbp{g}"][j]), writes=[bkt], dma=f"kb{ki}")
            vsrc = I[f"vbp{g}"][:, :, j * 128:(j + 1) * 128].rearrange("r (t p) e -> p (r t) e", p=128)
            nvt = dil * ntile
            if g == 2:
                vbase = 0
                for hf in range(2):
                    P.add("sp", lambda h, vt=vt, vsrc=vsrc, hf=hf: h.dma_start(out=vt[:, 16 * hf:16 * hf + 16, :], in_=vsrc[:, 16 * hf:16 * hf + 16, :]),
                          writes=[bvbuf[hf]], dma=f"vb{hf}")
            else:
                vbase = 16 * g
                P.add("sp", lambda h, vt=vt, vsrc=vsrc, vbase=vbase, nvt=nvt: h.dma_start(out=vt[:, vbase:vbase + nvt, :], in_=vsrc),
                      writes=[bvbuf[g]], dma=f"vb{g}")
            P.add("sp", lambda h, qt=qt, g=g, j=j: h.dma_start(out=qt[:], in_=I["qb"][g * 4 + j]), writes=[bqt], dma=f"qb{qi}")
            P.add("sp", lambda h, tb=tb, g=g, j=j: h.dma_start(out=tb[:, 0:512], in_=I["btab"][g * 4 + j]), writes=[btb], dma=f"tb{gi}")
            tbv = tb[:, 0:512].rearrange("p (a q) -> p a q", a=4)
            if dil == 1:
                nat = t01x[:].rearrange("p a (r pos) -> p a r pos", r=1)
            else:
                nat = t01x[:].rearrange("p a (pos r) -> p a r pos", r=dil)
            natD = None
            items = []
            for r in range(dil):
                for i in range(ntq):
                    q_ap = qt[:, r * Lh + i * QT: r * Lh + i * QT + QT]
                    k0 = r * LP + i * QT
                    ta0 = 1 if i == 0 else 0
                    ta1 = 3 if i == ntq - 1 else 2
                    vti = vbase + r * ntile + i * (QT // 128 if QT >= 128 else 0)
                    bvh = bvbuf[vti // 16]
                    it = {}

                    def sA(it=it, k0=k0, ta0=ta0, ta1=ta1, q_ap=q_ap, kt=kt, bkt=bkt, bqt=bqt, tbv=tbv, btb=btb, QT=QT):
                        st, bst = B.psum("st", ST_BANKS)
                        P.add("pe", lambda h, st=st: h.matmul(st[0:128, 0:QT], kt[:, k0:k0 + 128], q_ap, start=True, stop=True),
                              reads=[bkt, bqt], writes=[bst])
                        P.add("pe", lambda h, st=st: h.matmul(st[0:QT, QT:2 * QT], kt[:, k0 + 128:k0 + 128 + QT], q_ap, start=True, stop=True),
                              reads=[bkt, bqt], writes=[bst])
                        lg, blg = B.next_tmpf()
                        P.add("dve", lambda h, lg=lg, st=st: h.tensor_tensor(out=lg[:, 0:2 * QT].rearrange("p (a q) -> p a q", a=2),
                                                                          in0=st[:, 0:2 * QT].rearrange("p (a q) -> p a q", a=2),
                                                                          in1=tbv[:, ta0:ta1 + 1:ta1 - ta0, 0:QT], op=ALU.add),
                              reads=[bst, btb], writes=[blg])
                        pi = nxt("p", NPT)
                        it["pi"] = pi
                        P.add("act", lambda h, lg=lg, pi=pi: h.activation(out=pt[pi][:, 0:2 * QT], in_=lg[:, 0:2 * QT], func=AF.Exp),
                              reads=[blg], writes=[bpt[pi]])

                    def sB(it=it, vti=vti, vt=vt, bvh=bvh, QT=QT, r=r, i=i, g=g, nat=nat, natD=natD):
                        nd, bnd = B.psum("num", [0, 1, 2, 3])
                        pi = it["pi"]
                        P.add("pe", lambda h, nd=nd, pi=pi: h.matmul(nd[:, 0:QT], vt[0:128, vti, :], pt[pi][0:128, 0:QT], start=True, stop=False),
                              reads=[bvh, bpt[pi]], writes=[bnd])
                        P.add("pe", lambda h, nd=nd, pi=pi: h.matmul(nd[:, 0:QT], vt[0:QT, vti + 1, :], pt[pi][0:QT, QT:2 * QT], start=False, stop=True),
                              reads=[bvh, bpt[pi]], writes=[bnd])
                        P.add("pe", lambda h, nd=nd, pi=pi: h.matmul(nd[:, QT:2 * QT], B.ones_b[0:128, :], pt[pi][0:128, 0:QT], start=True, stop=False),
                              reads=[B.b_const, bpt[pi]], writes=[bnd])
                        P.add("pe", lambda h, nd=nd, pi=pi: h.matmul(nd[:, QT:2 * QT], B.ones_b[0:QT, :], pt[pi][0:QT, QT:2 * QT], start=False, stop=True),
                              reads=[B.b_const, bpt[pi]], writes=[bnd])
                        dND = nat[:, :, r, i * QT:(i + 1) * QT]
                        src = nd[:, 0:2 * QT].rearrange("p (a q) -> p a q", a=2)
                        if g == 0:
                            P.add("dve", lambda h, src=src: h.tensor_copy(out=dND, in_=src), reads=[bnd], writes=[baccN, baccD])
                        else:
                            P.add("dve", lambda h, src=src: h.tensor_tensor(out=dND, in0=src, in1=dND, op=ALU.add), reads=[bnd, baccN, baccD],
                                  writes=[baccN, baccD])
                    items.append((sA, sB))
            pipeline(items)
        P.add("dve", lambda h: h.reciprocal(out=accD[:], in_=accD[:]), reads=[baccD], writes=[baccD])
        P.add("dve", lambda h, j=j: h.tensor_tensor(out=oT[:, 8 + j, :], in0=accN[:], in1=accD[:], op=ALU.mult), reads=[baccN, baccD], writes=[boT])

    for j in range(4):
        qi = nxt("q", 2)
        qt, bqt = qbuf[qi], bqbuf[qi]
        P.add("sp", lambda h, qt=qt, j=j: h.dma_start(out=qt[:], in_=I["qc"][j]), writes=[bqt], dma=f"qb{qi}")
        items = []
        for qg in range(2):
            grp = {}
            for mt in range(2):
                it = {}

                def sA(it=it, qg=qg, mt=mt, qt=qt, bqt=bqt, j=j):
                    st, bst = B.psum("st", ST_BANKS)
                    P.add("pe", lambda h, st=st: h.matmul(st[:], kc_sb[:, j, mt * 128:(mt + 1) * 128], qt[:, qg * 512:(qg + 1) * 512],
                                                          start=True, stop=True), reads=[bkc, bqt], writes=[bst])
                    pi = nxt("p", NPT)
                    it["pi"] = pi
                    P.add("act", lambda h, st=st, pi=pi: h.activation(out=pt[pi][:], in_=st[:], func=AF.Exp), reads=[bst], writes=[bpt[pi]])

                def sB(it=it, qg=qg, mt=mt, grp=grp, j=j):
                    if mt == 0:
                        grp["num"] = B.psum("num", [0, 1])
                        grp["den"] = B.psum("den", [2, 3])
                    num, bnum = grp["num"]
                    den, bden = grp["den"]
                    pi = it["pi"]
                    P.add("pe", lambda h, num=num, pi=pi: h.matmul(num[:], vc_sb[:, mt, j * 128:(j + 1) * 128], pt[pi][:], start=(mt == 0), stop=(mt == 1)),
                          reads=[bvc, bpt[pi]], writes=[bnum])
                    P.add("pe", lambda h, den=den, pi=pi: h.matmul(den[:], B.ones_b[:], pt[pi][:], start=(mt == 0), stop=(mt == 1)),
                          reads=[B.b_const, bpt[pi]], writes=[bden])
                    if mt == 1:
                        rc, brc = B.next_tmpf()
                        P.add("dve", lambda h, rc=rc, den=den: h.reciprocal(out=rc[:], in_=den[:]), reads=[bden], writes=[brc])
                        P.add("dve", lambda h, rc=rc, num=num: h.tensor_tensor(out=oT[:, 12 + j, qg * 512:(qg + 1) * 512], in0=num[:], in1=rc[:], op=ALU.mult),
                              reads=[bnum, brc], writes=[boT])
                items.append((sA, sB))
        pipeline(items)
    P.barrier()
    es2.close()


def phase3(B, I):
    P = B.P
    nc = B.nc
    es3 = ExitStack()
    oT, boT = B.A32, B.bA32
    merged = es3.enter_context(nc.sbuf_tensor(B.uname("merged"), [128, KC, T], BF16))
    bmerged = P.buf("merged")
    gts = [es3.enter_context(nc.sbuf_tensor(B.uname(f"gts{i}"), [128, 3, T], BF16)) for i in range(2)]
    bgts = P.bufs(2, "gts")
    acc = [es3.enter_context(nc.sbuf_tensor(B.uname(f"macc{i}"), [128, T], F32)) for i in range(2)]
    bacc = P.bufs(2, "macc")
    gview = I["gate"].rearrange("(b c) p t -> c p b t", b=3)
    projs = ((I["w_proj_a"], 8, 0), (I["w_proj_b"], 4, 8), (I["w_proj_c"], 4, 12))
    for op2 in range(0, KC, 2):
        for bi, (W, kcb, c0) in enumerate(projs):
            def evac(ci, tg, ps, bps, bi=bi, op2=op2):
                oc = op2 + ci
                gi = oc % 2
                if bi == 0 and tg == 0:
                    P.add("sp", lambda h, gi=gi, oc=oc: h.dma_start(out=gts[gi][:], in_=gview[oc]), writes=[bgts[gi]], dma=f"gts{gi}")
                sl = slice(tg * 512, (tg + 1) * 512)
                if bi == 0:
                    P.add("dve", lambda h, ps=ps, gi=gi, sl=sl: h.tensor_tensor(out=acc[gi][:, sl], in0=ps[:], in1=gts[gi][:, 0, sl], op=ALU.mult),
                          reads=[bps, bgts[gi]], writes=[bacc[gi]])
                else:
                    tmp, btmp = B.next_tmpf()
                    P.add("dve", lambda h, ps=ps, gi=gi, sl=sl, tmp=tmp, bi=bi: h.tensor_tensor(out=tmp[:], in0=ps[:], in1=gts[gi][:, bi, sl], op=ALU.mult),
                          reads=[bps, bgts[gi]], writes=[btmp])
                    if bi == 1:
                        P.add("dve", lambda h, gi=gi, sl=sl, tmp=tmp: h.tensor_tensor(out=acc[gi][:, sl], in0=acc[gi][:, sl], in1=tmp[:], op=ALU.add),
                              reads=[btmp, bacc[gi]], writes=[bacc[gi]])
                    else:
                        P.add("dve", lambda h, gi=gi, sl=sl, tmp=tmp, oc=oc: h.tensor_tensor(out=merged[:, oc, sl], in0=acc[gi][:, sl], in1=tmp[:], op=ALU.add),
                              reads=[btmp, bacc[gi]], writes=[bmerged])
            B.dense_fm(W, 0, kcb, op2 * 128, 256, lambda k, tg, c0=c0: oT[:, c0 + k, tg * 512:(tg + 1) * 512], boT, [0, 1], evac)

    def evac_out(ci, tg, ps, bps):
        sl = slice(tg * 512, (tg + 1) * 512)
        P.add("dve", lambda h, ci=ci, sl=sl, ps=ps: h.tensor_tensor(out=B.xT[:, ci, sl], in0=ps[:], in1=B.xT[:, ci, sl], op=ALU.add),
              reads=[bps, B.bxT], writes=[B.bxT])

    B.dense_fm(I["w_out"], 0, KC, 0, D, lambda k, tg: merged[:, k, tg * 512:(tg + 1) * 512], bmerged, [0, 1], evac_out)
    P.barrier()
    es3.close()


def phase4(B, I):
    P = B.P
    nc = B.nc
    es4 = ExitStack()
    h2, bh2 = B.A32, B.bA32
    B.rmsnorm(B.xT, B.bxT, h2, bh2, I["gffn_sb"][0], I["gffn_sb"][1], T)
    NH = 22
    act = es4.enter_context(nc.sbuf_tensor(B.uname("actT"), [128, NH, T], BF16))
    bact = P.buf("act")
    sg = [es4.enter_context(nc.sbuf_tensor(B.uname(f"sg{i}"), [128, 2, T], BF16)) for i in range(2)]
    bsg = P.bufs(2, "sg")

    def rhs(k, tg):
        return h2[:, k, tg * 512:(tg + 1) * 512]

    for half in range(2):
        for w0 in range(0, NH * 128, 256):
            f0 = half * NH * 128 + w0
            si = (w0 // 256) % 2

            def evac_g(ci, tg, ps, bps, si=si):
                P.add("act", lambda h, ps=ps, ci=ci, tg=tg, si=si: h.activation(out=sg[si][:, ci, tg * 512:(tg + 1) * 512], in_=ps[:], func=AF.Silu),
                      reads=[bps], writes=[bsg[si]])

            def evac_u(ci, tg, ps, bps, si=si, w0=w0):
                fc = w0 // 128 + ci
                P.add("dve", lambda h, ps=ps, ci=ci, tg=tg, si=si, fc=fc: h.tensor_tensor(out=act[:, fc, tg * 512:(tg + 1) * 512], in0=ps[:],
                                                                                 in1=sg[si][:, ci, tg * 512:(tg + 1) * 512], op=ALU.mult),
                      reads=[bps, bsg[si]], writes=[bact])
            ncol = min(256, NH * 128 - w0)
            B.dense_fm(I["w_ffn_gate"], 0, KC, f0, ncol, rhs, bh2, [0, 1], evac_g)
            B.dense_fm(I["w_ffn_up"], 0, KC, f0, ncol, rhs, bh2, [0, 1], evac_u)

        def evac_d(ci, tg, ps, bps):
            sl = slice(tg * 512, (tg + 1) * 512)
            P.add("dve", lambda h, ci=ci, sl=sl, ps=ps: h.tensor_tensor(out=B.xT[:, ci, sl], in0=ps[:], in1=B.xT[:, ci, sl], op=ALU.add),
                  reads=[bps, B.bxT], writes=[B.bxT])
        B.dense_fm(I["w_ffn_down"], half * NH * 128, NH, 0, D, lambda k, tg: act[:, k, tg * 512:(tg + 1) * 512], bact, [0, 1], evac_d, nw=128)
    P.barrier()
    es4.close()


def store_x(B, out_d, final_gain=None):
    P = B.P
    evs = []
    dst = out_d.rearrange("(c p) t -> p c t", p=128)
    if final_gain is None:
        for c in range(0, KC, 4):
            evs.append(P.add("sp", lambda h, c=c: h.dma_start(out=dst[:, c:c + 4, :], in_=B.xT[:, c:c + 4, :]), reads=[B.bxT], writes=[P.buf()], dma="xo"))
        return evs
    es5 = ExitStack()
    fo = es5.enter_context(B.nc.sbuf_tensor(B.uname("final_o"), [128, KC, T], F32))
    bfo = P.buf("fo")
    B.rmsnorm(B.xT, B.bxT, fo, bfo, final_gain[0], final_gain[1], T)
    for c in range(0, KC, 4):
        evs.append(P.add("sp", lambda h, c=c: h.dma_start(out=dst[:, c:c + 4, :], in_=fo[:, c:c + 4, :]), reads=[bfo], writes=[P.buf()], dma="xo"))
    B._es5 = es5
    return evs


def load_x(B, src2d):
    src = src2d.rearrange("(c p) t -> p c t", p=128)
    for c in range(0, KC, 4):
        B.P.add("sp", lambda h, c=c: h.dma_start(out=B.xT[:, c:c + 4, :], in_=src[:, c:c + 4, :]), writes=[B.bxT], dma="x")


KV_PARTS = (("ka", 1024), ("va", 1024), ("kb1", 1024), ("kb2", 512), ("vb1", 1024), ("vb2", 512))


class _Idx:
    def __init__(self, fn):
        self.fn = fn

    def __getitem__(self, idx):
        if isinstance(idx, tuple):
            v = self.fn(idx[0])
            rest = idx[1:]
            return v[rest] if len(rest) > 1 else v[rest[0]]
        return self.fn(idx)


def kv_views(parts):
    vb1 = parts["vb1"].rearrange("(g t2) (two e) -> g (t2 two) e", g=2, two=2)
    vb2 = parts["vb2"].rearrange("t2 (two e) -> (t2 two) e", two=2)
    kb1 = parts["kb1"].rearrange("(h p) t -> h p t", p=128)
    kb2 = parts["kb2"].rearrange("(h p) t -> h p t", p=128)
    return {
        "ka": parts["ka"].rearrange("(h p) t -> h p t", p=128),
        "va": parts["va"],
        "kb": _Idx(lambda c: kb1[c] if c < 8 else kb2[c - 8]),
        "vb": _Idx(lambda g: vb1[g] if g < 2 else vb2),
    }


def build_fused8():
    B = Builder()
    P = B.P
    I0 = {}
    for n, shp in (("xT_in", [D, T]), ("memT", [D, 256]), ("w_in", [DEPTH, D, 8192]), ("w_gate", [DEPTH, D, 3 * D]),
                   ("w_mem_kv", [DEPTH, D, 1024]), ("w_proj_a", [DEPTH, 1024, D]), ("w_proj_b", [DEPTH, 512, D]),
                   ("w_proj_c", [DEPTH, 512, D]), ("w_out", [DEPTH, D, D]), ("w_ffn_gate", [DEPTH, D, DFF]),
                   ("w_ffn_up", [DEPTH, D, DFF]), ("w_ffn_down", [DEPTH, DFF, D]), ("mrow", [16, 3072]),
                   ("btab", [12, 128, 512]), ("dl", [DEPTH, 1, 256])):
        I0[n] = B.dram_in(n, shp)
    small = {}
    for n, shp in (("g_attn", [128, DEPTH * KC]), ("b_gate", [128, DEPTH * 48]), ("gffn", [128, DEPTH * KC]),
                   ("subln", [128, DEPTH]), ("gmem", [128, KC]), ("gfinal", [128, KC])):
        small[n] = B.dram_in(n, shp)
    ident_d = B.dram_in("ident", [128, 128])
    out_d = B.dram_out("out", [D, T])
    kv_own = [{n: B.nc.dram_tensor(f"kvo_{n}{i}", [r, 1024], BF16) for n, r in KV_PARTS} for i in range(2)]
    kv_all = [{n: B.nc.dram_tensor(f"kva_{n}{i}", [2 * r, 1024], BF16) for n, r in KV_PARTS} for i in range(2)]
    Sq = {"qa": B.dram_tmp("s_qa", [8, 128, T], BF16), "qb": B.dram_tmp("s_qb", [12, 128, T], BF16),
          "qc": B.dram_tmp("s_qc", [4, 128, T], BF16), "gate": B.dram_tmp("s_gate", [48, 128, T], BF16)}
    kbp, vbp = [], []
    for g, (_, dil) in enumerate(GROUPS):
        Lh = T // dil
        nrow = ((Lh + 128 + 127) // 128) * 128
        kbp.append(B.dram_tmp(f"s_kbp{g}", [4, 128, dil * (Lh + 128)], BF16))
        vbp.append(B.dram_tmp(f"s_vbp{g}", [dil, nrow, 512], BF16))

    B.xT = B.sb("xT_sb", [128, KC, T], F32)
    B.bxT = P.buf("xT")
    B.A32 = B.sb("A32", [128, KC, T], BF16)
    B.bA32 = P.buf("A32")
    B.eps_col = B.sb("eps_col", [128, 1], F32)
    P.add("dve", lambda h: h.memset(B.eps_col[:], EPS), writes=[B.b_const])
    B.kc_sb = B.sb("kc_sb", [128, 4, 256], BF16)
    B.bkc = P.buf("kc")
    B.vc_sb = B.sb("vc_sb", [128, 2, 512], BF16)
    B.bvc = P.buf("vc")
    sm = {n: load_small(B, n + "_sb", small[n], list(small[n].shape)) for n in small}
    identf = load_small(B, "ident_f", ident_d, [128, 128])
    finalize_small(B)
    P.add("dve", lambda h: h.tensor_copy(out=B.ident_b[:], in_=identf[0][:]), reads=[identf[1]], writes=[B.b_const])
    load_x(B, I0["xT_in"])

    for l in range(DEPTH):
        lam_init = 0.8 - 0.6 * math.exp(-0.3 * l)
        own = kv_views({n: kv_own[l % 2][n].ap() for n, _ in KV_PARTS})
        allv = [kv_views({n: kv_all[l % 2][n].ap()[r * rows:(r + 1) * rows, :] for n, rows in KV_PARTS}) for r in range(2)]
        outs = dict(Sq)
        outs.update(own)
        p1args = (B, B.xT, B.bxT, B.A32, B.bA32, sm["g_attn"][0][:, l * KC:(l + 1) * KC], sm["g_attn"][1],
                  I0["w_in"][l], I0["w_gate"][l], sm["b_gate"][0][:, l * 48:(l + 1) * 48], sm["b_gate"][1], outs)
        phase1(*p1args, part="main")
        P.wait_events("pool", [("dma", k, c) for k, c in P.dma_count.items()])
        for n, _ in KV_PARTS:
            P.add("pool", lambda h, l=l, n=n: h.collective_compute("AllGather", ALU.bypass, replica_groups=[[0, 1], [2, 3], [4, 5], [6, 7]],
                                                                   ins=[kv_own[l % 2][n].ap().opt()], outs=[kv_all[l % 2][n].ap().opt()]),
                  dma="cc", inc=1)
        phase1(*p1args, part="gates")
        esm = mem_kv(B, {"memT": I0["memT"], "w_mem_kv": I0["w_mem_kv"][l], "gmem_sb": sm["gmem"]})
        P.barrier()
        esm.close()
        for g, (_, dil) in enumerate(GROUPS):
            Lh = T // dil
            for j in range(4):
                dst = kbp[g][j].rearrange("p (r q) -> p r q", r=dil)

                def src(vw, g=g, j=j, dil=dil):
                    return vw["kb"][g * 4 + j].rearrange("p (r q) -> p r q", r=dil)
                P.add("sp", lambda h, dst=dst, a=src(own), Lh=Lh: h.dma_start(out=dst[:, :, 64:64 + Lh], in_=a), dma="pad")
                P.add("sp", lambda h, dst=dst, a=src(allv[0]), Lh=Lh: h.dma_start(out=dst[:, :, 0:64], in_=a[:, :, Lh - 64:Lh]), dma="pad")
                P.add("sp", lambda h, dst=dst, a=src(allv[1]), Lh=Lh: h.dma_start(out=dst[:, :, 64 + Lh:128 + Lh], in_=a[:, :, 0:64]), dma="pad")
            dstv = vbp[g]

            def srcv(vw, g=g, dil=dil):
                return vw["vb"][g].rearrange("(r q) e -> r q e", r=dil)
            P.add("sp", lambda h, dstv=dstv, a=srcv(own), Lh=Lh: h.dma_start(out=dstv[:, 64:64 + Lh, :], in_=a), dma="pad")
            P.add("sp", lambda h, dstv=dstv, a=srcv(allv[0]), Lh=Lh: h.dma_start(out=dstv[:, 0:64, :], in_=a[:, Lh - 64:Lh, :]), dma="pad")
            P.add("sp", lambda h, dstv=dstv, a=srcv(allv[1]), Lh=Lh: h.dma_start(out=dstv[:, 64 + Lh:128 + Lh, :], in_=a[:, 0:64, :]), dma="pad")
        P.barrier()

        I = {"qa": Sq["qa"], "qb": Sq["qb"], "qc": Sq["qc"], "gate": Sq["gate"], "ka_all": _Idx(lambda r, allv=allv: allv[r]["ka"]), "va_all": _Idx(lambda r, allv=allv: allv[r]["va"]),
             "memT": I0["memT"], "w_mem_kv": I0["w_mem_kv"][l], "mrow": I0["mrow"], "btab": I0["btab"], "dl": I0["dl"][l],
             "w_proj_a": I0["w_proj_a"][l], "w_proj_b": I0["w_proj_b"][l], "w_proj_c": I0["w_proj_c"][l], "w_out": I0["w_out"][l],
             "w_ffn_gate": I0["w_ffn_gate"][l], "w_ffn_up": I0["w_ffn_up"][l], "w_ffn_down": I0["w_ffn_down"][l],
             "gmem_sb": sm["gmem"], "subln_sb": (sm["subln"][0][:, l:l + 1], sm["subln"][1]),
             "gffn_sb": (sm["gffn"][0][:, l * KC:(l + 1) * KC], sm["gffn"][1])}
        for g in range(3):
            I[f"kbp{g}"] = kbp[g]
            I[f"vbp{g}"] = vbp[g]
        phase2(B, I, lam_init)
        phase3(B, I)
        phase4(B, I)
    evs = store_x(B, out_d, sm["gfinal"])
    return B.finish(evs)


def _rel_bucket_np(rel):
    rel = np.asarray(rel, dtype=np.int64)
    half_b, max_exact = 16, 8
    n = np.abs(rel)
    nf = np.maximum(n, 1).astype(np.float32)
    large = max_exact + (np.log(nf / np.float32(max_exact)) / np.float32(math.log(1024 / max_exact))
                         * np.float32(half_b - max_exact)).astype(np.int32)
    large = np.minimum(large, half_b - 1)
    return np.where(rel > 0, half_b, 0) + np.where(n < max_exact, n, large)


def _bias_layouts(rel_bias):
    rel_bias = np.asarray(rel_bias, dtype=np.float32)
    bk = _rel_bucket_np(np.arange(0, 4097) - 2048)
    Mf = rel_bias[bk][:, :16].T
    out = []
    p = np.arange(128)[:, None]
    jj = np.arange(128)[None, :]
    for s in range(2):
        B0 = 1025 - 1024 * s
        mrow = np.ascontiguousarray(Mf[:, B0:B0 + 3072])
        btab = np.zeros((12, 128, 4, 128), np.float32)
        for g, (_, dil) in enumerate(GROUPS):
            QT = min(128, (T // dil))
            for j in range(4):
                col = 16 + g * 4 + j
                sa = p - 64 - jj
                sb_ = p + 64 - jj
                va = np.abs(sa) <= 64
                vb = np.abs(sb_) <= 64
                ta = rel_bias[_rel_bucket_np(sa * dil), col]
                tb = rel_bias[_rel_bucket_np(sb_ * dil), col]
                vaf = va & ((p >= 64) if s == 0 else True)
                vbl = vb & ((p < QT - 64) if s == 1 else True)
                for a, (tv, vv) in enumerate(((ta, va), (ta, vaf), (tb, vb), (tb, vbl))):
                    btab[g * 4 + j, :, a, :] = np.where(vv, tv, np.float32(-30000.0))
        out.append((mrow, np.ascontiguousarray(btab.reshape(12, 128, 512))))
    return out


def _fm_vec(v, nchunk):
    return np.ascontiguousarray(np.asarray(v, np.float32).reshape(nchunk, 128).T)


_NC_CACHE = {}


def _get_nc(name):
    if name not in _NC_CACHE:
        _NC_CACHE[name] = {"fused8": build_fused8}[name]()
    return _NC_CACHE[name]


def kernel(x, mem, rel_bias, mem_norm, attn_norm, w_in, diff_lambda, diff_subln, w_mem_kv, w_gate, b_gate,
           w_proj_a, w_proj_b, w_proj_c, w_out, ffn_norm, w_ffn_gate, w_ffn_up, w_ffn_down, final_norm):
    f32 = np.float32
    x = np.asarray(x, f32)
    mem = np.asarray(mem, f32)
    cores = list(range(8))
    lay = _bias_layouts(rel_bias)
    A = lambda a: np.ascontiguousarray(np.asarray(a, f32))

    def fm_all(v, nchunk):
        return np.ascontiguousarray(np.concatenate([_fm_vec(v[l], nchunk) for l in range(DEPTH)], axis=1))

    common = {
        "w_in": A(w_in), "w_gate": A(w_gate), "w_mem_kv": A(w_mem_kv), "w_proj_a": A(w_proj_a), "w_proj_b": A(w_proj_b),
        "w_proj_c": A(w_proj_c), "w_out": A(w_out), "w_ffn_gate": A(w_ffn_gate), "w_ffn_up": A(w_ffn_up), "w_ffn_down": A(w_ffn_down),
        "dl": A(diff_lambda).reshape(DEPTH, 1, 256), "g_attn": fm_all(attn_norm, KC), "b_gate": fm_all(b_gate, 48),
        "gffn": fm_all(ffn_norm, KC), "subln": np.ascontiguousarray(A(diff_subln).T), "gmem": _fm_vec(mem_norm, KC),
        "gfinal": _fm_vec(final_norm, KC), "ident": np.eye(128, dtype=f32),
    }
    in_maps = []
    for c in cores:
        b, s_ = c // 2, c % 2
        d = dict(common)
        d["xT_in"] = np.ascontiguousarray(x[b, s_ * T:(s_ + 1) * T].T)
        d["memT"] = np.ascontiguousarray(mem[b].T)
        d["mrow"] = lay[s_][0]
        d["btab"] = lay[s_][1]
        in_maps.append(d)
    res = run_bass_kernel_spmd(_get_nc("fused8"), in_maps, core_ids=cores).results
    out = np.empty((4, SEQ, D), f32)
    for c in cores:
        b, s_ = c // 2, c % 2
        out[b, s_ * T:(s_ + 1) * T] = np.asarray(res[c]["out"], f32).T
    return out
```
